# Optimizing a Trainium2 kernel written in Bass

```python
import math
import jax, jax.numpy as jnp
from jax import lax
import numpy as np

D_MODEL = 2048
BATCH = 8
SEQ = 2048
DEPTH = 1

N_MEM = 256
HEAD_DIM = 128
ROPE_THETA = 500000.0
ROT_DIM = HEAD_DIM // 4
EPS = 1e-6
NEG = -1e30
DSWA_GROUPS = ((128, 1), (512, 4), (2048, 16))
DSWA_HEADS_PER_GROUP = 2
DSWA_HEADS = DSWA_HEADS_PER_GROUP * len(DSWA_GROUPS)
DSWA_WIDTH = DSWA_HEADS * HEAD_DIM
DSWA_OUT = DSWA_HEADS_PER_GROUP * HEAD_DIM
HY_WIDTH = 3 * D_MODEL // 8
HY_ORDER = 2
HY_SHORT = 3
HY_EMB = 33
HY_FILTER_HIDDEN = 64
HY_FAST_DECAY = 0.3
HY_SLOW_DECAY = 1.5
HY_TARGET = 1e-2
MEM_HEADS = 4
MEM_WIDTH = MEM_HEADS * HEAD_DIM
N_BRANCH = 3
D_FF = 256 * ((8 * D_MODEL // 3 + 255) // 256)
IN_WIDTH = 3 * DSWA_WIDTH + (HY_ORDER + 1) * HY_WIDTH + MEM_WIDTH + N_BRANCH * D_MODEL
IN_SPLITS = (DSWA_WIDTH, 2 * DSWA_WIDTH, 3 * DSWA_WIDTH,
             3 * DSWA_WIDTH + (HY_ORDER + 1) * HY_WIDTH,
             3 * DSWA_WIDTH + (HY_ORDER + 1) * HY_WIDTH + MEM_WIDTH)

kernel_name = 'hybrid_gated_dilated_hyena_memory_encoder_layer'

F32 = jnp.float32


def rmsnorm(x, g):
    xf = x.astype(F32)
    y = xf * lax.rsqrt(jnp.mean(xf * xf, axis=-1, keepdims=True) + EPS)
    return (y * g.astype(F32)).astype(x.dtype)


def swiglu(h, w_in, w_down):
    a, b = jnp.split(h @ w_in, 2, axis=-1)
    return (jax.nn.silu(a) * b) @ w_down


def rope_tables(seq):
    inv = jnp.power(ROPE_THETA, -jnp.arange(0, ROT_DIM, 2, dtype=F32) / ROT_DIM)
    ang = jnp.arange(seq, dtype=F32)[:, None] * inv[None, :]
    return jnp.cos(ang), jnp.sin(ang)


def partial_rope(t, cos, sin):
    tf = t.astype(F32)
    half = ROT_DIM // 2
    t1, t2 = tf[..., :half], tf[..., half:ROT_DIM]
    c, s = cos[None, :, None, :], sin[None, :, None, :]
    out = jnp.concatenate([t1 * c - t2 * s, t2 * c + t1 * s, tf[..., ROT_DIM:]], axis=-1)
    return out.astype(t.dtype)


def dilated_window_attn(q, k, v, window, dilation):
    B, S, H, Dh = q.shape
    half = window // (2 * dilation)
    L = S // dilation
    nb = -(-L // half)
    Lp = nb * half

    def sub(t):
        return t.astype(F32).reshape(B, L, dilation, H, Dh).transpose(0, 2, 1, 3, 4)

    qs = jnp.pad(sub(q), ((0, 0), (0, 0), (0, Lp - L), (0, 0), (0, 0)))
    qs = qs.reshape(B, dilation, nb, half, H, Dh)

    def windows(t):
        tp = jnp.pad(sub(t), ((0, 0), (0, 0), (half, Lp - L + half), (0, 0), (0, 0)))
        tp = tp.reshape(B, dilation, nb + 2, half, H, Dh)
        return jnp.concatenate([tp[:, :, :-2], tp[:, :, 1:-1], tp[:, :, 2:]], axis=3)

    kw, vw = windows(k), windows(v)
    s = jnp.einsum('bdnqhe,bdnkhe->bdnhqk', qs, kw) / math.sqrt(Dh)
    qi = jnp.arange(half)[:, None]
    kj = jnp.arange(3 * half)[None, :]
    rel = kj - half - qi
    kpos = jnp.arange(nb)[:, None, None] * half - half + kj[None]
    valid = (jnp.abs(rel) <= half)[None] & (kpos >= 0) & (kpos < L)
    s = jnp.where(valid[None, None, :, None], s, NEG)
    m = jnp.max(s, axis=-1, keepdims=True)
    p = jnp.exp(s - m)
    den = jnp.sum(p, axis=-1, keepdims=True)
    o = jnp.einsum('bdnhqk,bdnkhe->bdnhqe', p, vw) / den
    lse = (m + jnp.log(den))[..., 0]
    o = o.transpose(0, 1, 2, 4, 3, 5).reshape(B, dilation, Lp, H, Dh)[:, :, :L]
    o = o.transpose(0, 2, 1, 3, 4).reshape(B, S, H, Dh)
    lse = lse.transpose(0, 1, 2, 4, 3).reshape(B, dilation, Lp, H)[:, :, :L]
    lse = lse.transpose(0, 2, 1, 3).reshape(B, S, H)
    return o, lse


def hyena_positional_features(L):
    bands = (HY_EMB - 1) // 2
    t = jnp.linspace(0.0, 1.0, L, dtype=F32)[:, None]
    w = 2.0 * math.pi * jnp.arange(L, dtype=F32)[:, None] / L
    f = jnp.linspace(1e-4, bands - 1, bands, dtype=F32)[None, :]
    return jnp.concatenate([t, jnp.cos(f * w), -jnp.sin(f * w)], axis=-1)


def hyena_filters(z, w1, b1, w2, b2, w3, b3, w4, freq):
    L = z.shape[0]
    fr = freq.astype(F32)
    act = lambda u: jnp.sin(fr * u)
    hh = act(z @ w1.astype(F32) + b1.astype(F32))
    hh = act(hh @ w2.astype(F32) + b2.astype(F32))
    hh = act(hh @ w3.astype(F32) + b3.astype(F32))
    h = (hh @ w4.astype(F32)).reshape(L, HY_ORDER, 2, HY_WIDTH)
    max_decay = math.log(HY_TARGET) / HY_FAST_DECAY
    min_decay = math.log(HY_TARGET) / HY_SLOW_DECAY
    deltas = jnp.linspace(min_decay, max_decay, HY_WIDTH, dtype=F32)
    t = jnp.linspace(0.0, 1.0, L, dtype=F32)[:, None]
    decay = jnp.exp(-t * jnp.abs(deltas)[None, :])
    h = h * decay[:, None, None, :]
    return h.transpose(1, 2, 0, 3)


def bidir_fftconv(u, h, bias):
    L = u.shape[1]
    n = 2 * L
    uf = u.astype(F32)
    ud = jnp.stack([uf, uf[:, ::-1]], axis=0)
    U = jnp.fft.rfft(ud, n=n, axis=2)
    Hf = jnp.fft.rfft(h, n=n, axis=1)[:, None]
    y = jnp.fft.irfft(U * Hf, n=n, axis=2)[:, :, :L]
    return y[0] + y[1][:, ::-1] + uf * bias.astype(F32)


def hyena_mixer(u, conv_w, conv_b, filters, bias):
    S = u.shape[1]
    pad = HY_SHORT // 2
    up = jnp.pad(u, ((0, 0), (pad, pad), (0, 0)))
    uc = sum(up[:, j:j + S] * conv_w[j] for j in range(HY_SHORT)) + conv_b
    parts = jnp.split(uc, HY_ORDER + 1, axis=-1)
    z = parts[0]
    for o in range(HY_ORDER):
        z = parts[o + 1].astype(F32) * bidir_fftconv(z, filters[o], bias[o])
    return z.astype(u.dtype)


def head_rmsnorm(t, g):
    tf = t.astype(F32)
    y = tf * lax.rsqrt(jnp.mean(tf * tf, axis=-1, keepdims=True) + EPS)
    return (y * g.astype(F32)).astype(t.dtype)


def mem_attention(q, mem_n, w_kv, gq, gk):
    B, S, _ = q.shape
    M = mem_n.shape[1]
    k, v = jnp.split(mem_n @ w_kv, 2, axis=-1)
    q = head_rmsnorm(q.reshape(B, S, MEM_HEADS, HEAD_DIM), gq).astype(F32)
    k = head_rmsnorm(k.reshape(B, M, MEM_HEADS, HEAD_DIM), gk).astype(F32)
    v = v.reshape(B, M, MEM_HEADS, HEAD_DIM).astype(F32)
    s = jnp.einsum('bshe,bmhe->bhsm', q, k) / math.sqrt(HEAD_DIM)
    p = jax.nn.softmax(s, axis=-1)
    o = jnp.einsum('bhsm,bmhe->bshe', p, v)
    return o.reshape(B, S, MEM_WIDTH)


def setup_inputs(seed: int = 0) -> dict:
    key = jax.random.key(seed)
    ks = iter(jax.random.split(key, 48))

    def nrm(shape, scale):
        return scale * jax.random.normal(next(ks), shape, jnp.float32)

    def gain(shape):
        return 1.0 + 0.02 * jax.random.normal(next(ks), shape, jnp.float32)

    Lr = DEPTH
    Hf = HY_FILTER_HIDDEN
    return {
        'x': nrm((BATCH, SEQ, D_MODEL), 1.0),
        'mem': nrm((BATCH, N_MEM, D_MODEL), 1.0),
        'g_ff1': gain((Lr, D_MODEL)),
        'w_ff1_in': nrm((Lr, D_MODEL, 2 * D_FF), D_MODEL ** -0.5),
        'w_ff1_out': nrm((Lr, D_FF, D_MODEL), D_FF ** -0.5),
        'g_mix': gain((Lr, D_MODEL)),
        'w_in': nrm((Lr, D_MODEL, IN_WIDTH), D_MODEL ** -0.5),
        'a_gq': gain((Lr, HEAD_DIM)),
        'a_gk': gain((Lr, HEAD_DIM)),
        'hy_conv_w': nrm((Lr, HY_SHORT, (HY_ORDER + 1) * HY_WIDTH), HY_SHORT ** -0.5),
        'hy_conv_b': nrm((Lr, (HY_ORDER + 1) * HY_WIDTH), 0.02),
        'hy_f_w1': nrm((Lr, HY_EMB, Hf), HY_EMB ** -0.5),
        'hy_f_b1': nrm((Lr, Hf), 0.02),
        'hy_f_w2': nrm((Lr, Hf, Hf), Hf ** -0.5),
        'hy_f_b2': nrm((Lr, Hf), 0.02),
        'hy_f_w3': nrm((Lr, Hf, Hf), Hf ** -0.5),
        'hy_f_b3': nrm((Lr, Hf), 0.02),
        'hy_f_w4': nrm((Lr, Hf, HY_ORDER * 2 * HY_WIDTH), 0.1 * Hf ** -0.5),
        'hy_f_freq': gain((Lr, Hf)),
        'hy_bias': nrm((Lr, HY_ORDER, HY_WIDTH), 0.1),
        'g_mem': gain((Lr, D_MODEL)),
        'w_mem_kv': nrm((Lr, D_MODEL, 2 * MEM_WIDTH), D_MODEL ** -0.5),
        'm_gq': gain((Lr, HEAD_DIM)),
        'm_gk': gain((Lr, HEAD_DIM)),
        'w_br_a': nrm((Lr, DSWA_OUT, D_MODEL), DSWA_OUT ** -0.5),
        'w_br_b': nrm((Lr, HY_WIDTH, D_MODEL), HY_WIDTH ** -0.5),
        'w_br_c': nrm((Lr, MEM_WIDTH, D_MODEL), MEM_WIDTH ** -0.5),
        'w_out': nrm((Lr, D_MODEL, D_MODEL), D_MODEL ** -0.5),
        'g_ff2': gain((Lr, D_MODEL)),
        'w_ff2_in': nrm((Lr, D_MODEL, 2 * D_FF), D_MODEL ** -0.5),
        'w_ff2_out': nrm((Lr, D_FF, D_MODEL), D_FF ** -0.5),
        'g_post': gain((Lr, D_MODEL)),
    }


def reference(x, mem, g_ff1, w_ff1_in, w_ff1_out, g_mix, w_in, a_gq, a_gk,
              hy_conv_w, hy_conv_b, hy_f_w1, hy_f_b1, hy_f_w2, hy_f_b2, hy_f_w3, hy_f_b3,
              hy_f_w4, hy_f_freq, hy_bias, g_mem, w_mem_kv, m_gq, m_gk,
              w_br_a, w_br_b, w_br_c, w_out, g_ff2, w_ff2_in, w_ff2_out, g_post):
    B, S, _ = x.shape
    cos, sin = rope_tables(S)
    hy_z = hyena_positional_features(S)
    for l in range(DEPTH):
        x = x + 0.5 * swiglu(rmsnorm(x, g_ff1[l]), w_ff1_in[l], w_ff1_out[l])

        h = rmsnorm(x, g_mix[l])
        proj = h @ w_in[l]
        a_q, a_k, a_v, hy_u, m_q, gate_logits = jnp.split(proj, IN_SPLITS, axis=-1)

        a_q = partial_rope(head_rmsnorm(a_q.reshape(B, S, DSWA_HEADS, HEAD_DIM), a_gq[l]), cos, sin)
        a_k = partial_rope(head_rmsnorm(a_k.reshape(B, S, DSWA_HEADS, HEAD_DIM), a_gk[l]), cos, sin)
        a_v = a_v.reshape(B, S, DSWA_HEADS, HEAD_DIM)
        outs, lses = [], []
        for g, (win, dil) in enumerate(DSWA_GROUPS):
            sl = slice(g * DSWA_HEADS_PER_GROUP, (g + 1) * DSWA_HEADS_PER_GROUP)
            o, lse = dilated_window_attn(a_q[:, :, sl], a_k[:, :, sl], a_v[:, :, sl], win, dil)
            outs.append(o)
            lses.append(lse)
        alpha = jax.nn.softmax(jnp.stack(lses, axis=0), axis=0)[..., None]
        y_a = jnp.sum(alpha * jnp.stack(outs, axis=0), axis=0).reshape(B, S, DSWA_OUT).astype(x.dtype)

        filt = hyena_filters(hy_z, hy_f_w1[l], hy_f_b1[l], hy_f_w2[l], hy_f_b2[l],
                             hy_f_w3[l], hy_f_b3[l], hy_f_w4[l], hy_f_freq[l])
        y_b = hyena_mixer(hy_u, hy_conv_w[l], hy_conv_b[l], filt, hy_bias[l])

        y_c = mem_attention(m_q, rmsnorm(mem, g_mem[l]), w_mem_kv[l], m_gq[l], m_gk[l]).astype(x.dtype)

        ga, gb, gc = jnp.split(jax.nn.sigmoid(gate_logits.astype(F32)), N_BRANCH, axis=-1)
        merged = ga * (y_a @ w_br_a[l]) + gb * (y_b @ w_br_b[l]) + gc * (y_c @ w_br_c[l])
        x = x + merged.astype(x.dtype) @ w_out[l]

        x = x + 0.5 * swiglu(rmsnorm(x, g_ff2[l]), w_ff2_in[l], w_ff2_out[l])
        x = rmsnorm(x, g_post[l])
    return x
```

```python
import contextlib
import math
import numpy as np
import ml_dtypes
import concourse.bass as bass
import concourse.mybir as mybir
from concourse.bass_utils import run_bass_kernel_spmd

F32 = mybir.dt.float32
BF16 = mybir.dt.bfloat16
AF = mybir.ActivationFunctionType
ALU = mybir.AluOpType

D = 2048
S = 2048
DFF = 5632
NMEM = 256
EPS = 1e-6
INW = 11264
HYW = 768
NFFT = 4096
ENGS = ("pe", "act", "dve", "pool", "sp")
SAME_ENGINE_SYNC = True


class Res:
    __slots__ = ("name", "lw", "rd", "rdd")

    def __init__(self, name=""):
        self.name = name
        self.lw = None
        self.rd = {}
        self.rdd = []


class Op:
    __slots__ = ("eng", "fn", "deps", "dma", "need_inc", "sem", "val", "pre", "n")

    def __init__(self, eng, fn, dma, n):
        self.eng = eng
        self.fn = fn
        self.dma = dma
        self.deps = []
        self.need_inc = False
        self.sem = None
        self.val = 0
        self.pre = None
        self.n = n


class Prog:
    def __init__(self, nc, stack, ndma=8):
        self.nc = nc
        self.S = ndma
        self.sem = {e: stack.enter_context(nc.semaphore("c_" + e)) for e in ENGS}
        self.cnt = {e: 0 for e in ENGS}
        self.dsem = {
            q: [stack.enter_context(nc.semaphore("d_%s%d" % (q, i))) for i in range(ndma)]
            for q in ("sp", "act", "pool")
        }
        self.dcnt = {q: 0 for q in ("sp", "act", "pool")}
        self.waited = {e: {} for e in ENGS}
        self.ops = []
        self.res = []
        self.nphase = 0
        self.total_ops = 0

    def R(self, name=""):
        r = Res(name)
        self.res.append(r)
        return r

    def Rs(self, n, name=""):
        return [self.R("%s%d" % (name, i)) for i in range(n)]

    def add(self, eng, fn, r=(), w=(), dma=False):
        op = Op(eng, fn, dma, len(self.ops))
        cd = {}
        dd = {}

        def dep(p):
            if p is None or p is op:
                return
            if p.dma:
                dd[id(p)] = p
            else:
                q = cd.get(p.eng)
                if q is None or q.n < p.n:
                    cd[p.eng] = p

        for x in r:
            dep(x.lw)
        for x in w:
            dep(x.lw)
            for y in x.rd.values():
                dep(y)
            for y in x.rdd:
                dep(y)
        for x in r:
            if dma:
                x.rdd.append(op)
            else:
                x.rd[eng] = op
        for x in w:
            x.lw = op
            x.rd = {}
            x.rdd = []
        op.deps = list(cd.values()) + list(dd.values())
        self.ops.append(op)
        return op

    def _needs_signal(self, p, x):
        if p.dma:
            return True
        if p.eng == x.eng and not x.dma:
            if p.eng == "pe":
                return False
            return SAME_ENGINE_SYNC
        return True

    def end(self):
        nc = self.nc
        ops = self.ops
        for x in ops:
            x.deps = [p for p in x.deps if self._needs_signal(p, x)]
            for p in x.deps:
                p.need_inc = True
        for x in ops:
            if x.dma:
                i = self.dcnt[x.eng]
                self.dcnt[x.eng] += 1
                x.sem = self.dsem[x.eng][i % self.S]
                x.val = 16 * (i // self.S + 1)
                x.pre = (x.sem, 16 * (i // self.S)) if i >= self.S else None
            elif x.need_inc:
                self.cnt[x.eng] += 1
                x.sem = self.sem[x.eng]
                x.val = self.cnt[x.eng]
        final_dma = []
        for q in ("sp", "act", "pool"):
            n = self.dcnt[q]
            for j in range(min(n, self.S)):
                i = n - 1 - j
                final_dma.append((self.dsem[q][i % self.S], 16 * (i // self.S + 1)))

        def emit(eng_name, e):
            wd = self.waited[eng_name]

            def wait(sem, val):
                if wd.get(id(sem), 0) >= val:
                    return
                wd[id(sem)] = val
                e.wait_ge(sem, val)

            for x in ops:
                if x.eng != eng_name:
                    continue
                for p in x.deps:
                    wait(p.sem, p.val)
                if x.pre is not None:
                    wait(*x.pre)
                ins = x.fn(e)
                if x.dma:
                    ins.then_inc(x.sem, 16)
                elif x.need_inc:
                    ins.then_inc(x.sem, 1)
            if eng_name == "sp":
                for sem, val in final_dma:
                    wait(sem, val)

        with nc.Block("ph%d" % self.nphase, no_gpsimd_drain=True) as block:
            @block.tensor
            def _(e):
                emit("pe", e)

            @block.scalar
            def _(e):
                emit("act", e)

            @block.vector
            def _(e):
                emit("dve", e)

            @block.gpsimd
            def _(e):
                emit("pool", e)

            @block.sync
            def _(e):
                emit("sp", e)

        self.nphase += 1
        self.total_ops += len(ops)
        self.ops = []
        for r in self.res:
            r.lw = None
            r.rd = {}
            r.rdd = []
        self.res = []


class Ctx:
    pass


def dma(P, q, out, in_, r=(), w=()):
    return P.add(q, lambda e: e.dma_start(out=out, in_=in_), r=r, w=w, dma=True)


_UID = [0]


def uid(name):
    _UID[0] += 1
    return "%s_u%d" % (name, _UID[0])


def sb(st, nc, name, shape, dt):
    return st.enter_context(nc.sbuf_tensor(uid(name), shape, dt))


def pst(st, nc, name, shape, dt):
    return st.enter_context(nc.psum_tensor(uid(name), shape, dt))


def chunked(ap2d):
    return ap2d.rearrange("(k p) n -> p k n", p=128)


def phase_tm_norm(C, P, src, ntiles, g_row, dstT, r_dst, xT_dram=None):
    nc = C.nc
    with contextlib.ExitStack() as st:
        gb = sb(st, nc, "n_gb", [128, D], F32)
        xs = [sb(st, nc, "n_xs%d" % i, [128, D], F32) for i in range(3)]
        junk = sb(st, nc, "n_junk", [128, D], BF16)
        ss = [sb(st, nc, "n_ss%d" % i, [128, 1], F32) for i in range(3)]
        rs = [sb(st, nc, "n_rs%d" % i, [128, 1], F32) for i in range(3)]
        xb = [sb(st, nc, "n_xb%d" % i, [128, D], BF16) for i in range(3)]
        xTs = [sb(st, nc, "n_xTs%d" % i, [128, 16, 128], F32) for i in range(3)]
        pt = [pst(st, nc, "n_pt%d" % i, [128, 8, 128], BF16) for i in range(3)]
        ptf = [pst(st, nc, "n_ptf%d" % i, [128, 4, 128], F32) for i in range(4)]
        r_gb = P.R()
        r_xs, r_ss, r_rs, r_xb, r_xTs, r_pt, r_ptf = (P.Rs(4) for _ in range(7))
        r_junk = P.R()
        r_xs = P.Rs(3)
        dma(P, "sp", gb[:], g_row, w=[r_gb])
        npt = 0
        nptf = 0
        dma(P, "sp", xs[0][:], src[0:128, :], w=[r_xs[0]])
        for i in range(ntiles):
            b = i % 3
            bx = i % 3
            if i + 1 < ntiles:
                dma(P, "sp", xs[(i + 1) % 3][:], src[(i + 1) * 128:(i + 2) * 128, :], w=[r_xs[(i + 1) % 3]])
            P.add("act", lambda e, b=b, bx=bx: e.activation(out=junk[:], in_=xs[bx][:], func=AF.Square, accum_out=ss[b][:]),
                  r=[r_xs[bx]], w=[r_junk, r_ss[b]])
            P.add("act", lambda e, b=b: e.activation(out=rs[b][:], in_=ss[b][:], func=AF.Sqrt, scale=1.0 / D, bias=C.epsc[:, 0:1]),
                  r=[r_ss[b]], w=[r_rs[b]])
            P.add("dve", lambda e, b=b: e.reciprocal(out=rs[b][:], in_=rs[b][:]), r=[], w=[r_rs[b]])
            P.add("dve", lambda e, b=b, bx=bx: e.scalar_tensor_tensor(out=xb[b][:], in0=xs[bx][:], scalar=rs[b][:, 0:1],
                                                               in1=gb[:], op0=ALU.mult, op1=ALU.mult),
                  r=[r_xs[bx], r_rs[b], r_gb], w=[r_xb[b]])
            for k0 in range(0, 16, 8):
                s_ = npt % 3
                npt += 1
                for k in range(k0, k0 + 8):
                    P.add("pe", lambda e, b=b, k=k, s_=s_: e.transpose(out=pt[s_][:, k % 8, :],
                                                                      in_=xb[b][:, k * 128:(k + 1) * 128],
                                                                      identity=C.identb[:]),
                          r=[r_xb[b]], w=[r_pt[s_]])
                P.add("act", lambda e, s_=s_, k0=k0, i=i: e.copy(out=dstT[:, k0:k0 + 8, i * 128:(i + 1) * 128],
                                                              in_=pt[s_][:]),
                      r=[], w=[r_pt[s_], r_dst])
            if xT_dram is not None:
                for k0 in range(0, 16, 4):
                    s_ = nptf % 4
                    nptf += 1
                    for k in range(k0, k0 + 4):
                        P.add("pe", lambda e, bx=bx, k=k, s_=s_: e.transpose(out=ptf[s_][:, k % 4, :],
                                                                          in_=xs[bx][:, k * 128:(k + 1) * 128],
                                                                          identity=C.identf[:]),
                              r=[r_xs[bx]], w=[r_ptf[s_]])
                    P.add("dve", lambda e, s_=s_, k0=k0, b=b: e.tensor_copy(out=xTs[b][:, k0:k0 + 4, :], in_=ptf[s_][:]),
                          r=[], w=[r_ptf[s_], r_xTs[b]])
                dma(P, "sp", chunked(xT_dram)[:, :, i * 128:(i + 1) * 128], xTs[b][:], r=[r_xTs[b]])
        P.end()


def phase_fm_norm(C, P, xT_dram, gT_col, dstT, final_out=None):
    nc = C.nc
    xTc = chunked(xT_dram)
    G = 512
    NG = S // G
    NB = G // 512
    with contextlib.ExitStack() as st:
        xr = [sb(st, nc, "f_xr%d" % i, [128, G], F32) for i in range(32)]
        sq = [sb(st, nc, "f_sq%d" % i, [128, G], BF16) for i in range(2)]
        rstd = [sb(st, nc, "f_rstd%d" % i, [128, G], F32) for i in range(2)]
        ps = pst(st, nc, "f_ps", [128, 4, 512], F32)
        r_xr = P.Rs(32)
        r_sq = P.Rs(2)
        r_ps = P.Rs(4)
        r_rstd = P.Rs(2)
        r_dst = P.R()
        if final_out is not None:
            yn = [sb(st, nc, "f_yn%d" % i, [128, 512], F32) for i in range(2)]
            ot = [sb(st, nc, "f_ot%d" % i, [128, 16, 4, 128], F32) for i in range(2)]
            ptf = [pst(st, nc, "f_ptf%d" % i, [128, 4, 128], F32) for i in range(4)]
            r_yn = P.Rs(2)
            r_ptf = P.Rs(4)
            r_ot = P.Rs(2)
        nq = 0

        def loads(g):
            for c in range(16):
                dma(P, "sp", xr[(g % 2) * 16 + c][:], xTc[:, c, g * G:(g + 1) * G], w=[r_xr[(g % 2) * 16 + c]])

        for g in range(NG):
            gs = slice(g * G, (g + 1) * G)
            pb = (g % 2) * 2 if NB == 2 else g % 4
            xo_ = (g % 2) * 16
            if g == 0:
                loads(0)
            if g + 1 < NG:
                loads(g + 1)
            for c in range(16):
                P.add("act", lambda e, c=c, xo_=xo_: e.activation(out=sq[c % 2][:], in_=xr[xo_ + c][:], func=AF.Square), r=[r_xr[xo_ + c]], w=[r_sq[c % 2]])
                for blk in range(NB):
                    P.add("pe", lambda e, c=c, blk=blk, pb=pb: e.matmul(ps[:, pb + blk, :], lhsT=C.onesb[:], rhs=sq[c % 2][:, blk * 512:(blk + 1) * 512],
                                                                      start=(c == 0), stop=(c == 15)), r=[r_sq[c % 2]], w=[r_ps[pb + blk]])
            rb = g % 2
            psv = ps[:, pb:pb + NB, :].rearrange("p b n -> p (b n)")
            P.add("act", lambda e, rb=rb, psv=psv: e.activation(out=rstd[rb][:], in_=psv, func=AF.Ln, scale=1.0 / D, bias=C.epsc[:, 0:1]),
                  r=[], w=[r_ps[pb + b_] for b_ in range(NB)] + [r_rstd[rb]])
            P.add("act", lambda e, rb=rb: e.activation(out=rstd[rb][:], in_=rstd[rb][:], func=AF.Exp, scale=-0.5), r=[], w=[r_rstd[rb]])
            if final_out is None:
                for c in range(16):
                    P.add("dve", lambda e, c=c, rb=rb, gs=gs, xo_=xo_: e.scalar_tensor_tensor(out=dstT[:, c, gs], in0=xr[xo_ + c][:], scalar=gT_col[:, c:c + 1],
                                                                                  in1=rstd[rb][:], op0=ALU.mult, op1=ALU.mult),
                          r=[r_xr[xo_ + c], r_rstd[rb]], w=[r_dst])
            else:
                ob = g % 2
                for c in range(16):
                    y = yn[c % 2]
                    P.add("dve", lambda e, c=c, y=y, rb=rb, xo_=xo_: e.scalar_tensor_tensor(out=y[:], in0=xr[xo_ + c][:], scalar=gT_col[:, c:c + 1],
                                                                                in1=rstd[rb][:], op0=ALU.mult, op1=ALU.mult),
                          r=[r_xr[xo_ + c], r_rstd[rb]], w=[r_yn[c % 2]])
                    s_ = nq % 4
                    nq += 1
                    for tt in range(4):
                        P.add("pe", lambda e, y=y, tt=tt, s_=s_: e.transpose(out=ptf[s_][:, tt, :], in_=y[:, tt * 128:(tt + 1) * 128], identity=C.identf[:]),
                              r=[r_yn[c % 2]], w=[r_ptf[s_]])
                    if c % 2 == 0:
                        P.add("act", lambda e, s_=s_, c=c, ob=ob: e.copy(out=ot[ob][:, c, :, :], in_=ptf[s_][:]), r=[], w=[r_ptf[s_], r_ot[ob]])
                    else:
                        P.add("dve", lambda e, s_=s_, c=c, ob=ob: e.tensor_copy(out=ot[ob][:, c, :, :], in_=ptf[s_][:]), r=[], w=[r_ptf[s_], r_ot[ob]])
                for tt in range(4):
                    t0 = g * 512 + tt * 128
                    dma(P, "sp", final_out[t0:t0 + 128, :].rearrange("p (c j) -> p c j", j=128), ot[ob][:, :, tt, :], r=[r_ot[ob]])
        P.end()


def phase_ffn(C, P, xnT, w_in, w_out, xT_dram):
    nc = C.nc
    xTc = chunked(xT_dram)
    PARTS = ((0, 22), (22, 22))
    JP = 22
    with contextlib.ExitStack() as st:
        gT = sb(st, nc, "m_gT", [128, JP, S], BF16)
        NWB = 2
        w1 = [sb(st, nc, "m_w1_%d" % i, [128, 16, 2, 128], BF16) for i in range(NWB)]
        w2 = [sb(st, nc, "m_w2_%d" % i, [128, JP, 128], BF16) for i in range(2)]
        sg = [sb(st, nc, "m_sg%d" % i, [128, 512], F32) for i in range(2)]
        xr = [sb(st, nc, "m_xr%d" % i, [128, S], F32) for i in range(2)]
        ps = pst(st, nc, "m_ps", [128, 8, 512], F32)
        r_ps = P.Rs(8)
        r_w1 = P.Rs(NWB)
        r_w2 = P.Rs(2)
        r_sg = P.Rs(2)
        r_xr = P.Rs(2)
        r_xn = P.R()
        r_xT = P.Rs(16)
        r_g = [[P.R() for _ in range(4)] for _ in range(JP)]
        nsg = 0
        nw1 = 0
        nw2 = 0
        nxr = 0
        unit = 0
        for (j0, njp) in PARTS:
            for jl in range(njp):
                j = j0 + jl
                wb = nw1 % NWB
                nw1 += 1
                dma(P, "pool", w1[wb][:, :, 0, :], chunked(w_in[:, j * 128:(j + 1) * 128]), w=[r_w1[wb]])
                dma(P, "pool", w1[wb][:, :, 1, :], chunked(w_in[:, DFF + j * 128:DFF + (j + 1) * 128]), w=[r_w1[wb]])
                for h in range(2):
                    bs = (unit % 2) * 4
                    unit += 1
                    for ab in range(2):
                        for blk in range(2):
                            bank = bs + ab * 2 + blk
                            tb = h * 2 + blk
                            for k in range(16):
                                P.add("pe", lambda e, bank=bank, wb=wb, ab=ab, k=k, tb=tb: e.matmul(
                                    ps[:, bank, :], lhsT=w1[wb][:, k, ab, :], rhs=xnT[:, k, tb * 512:(tb + 1) * 512],
                                    start=(k == 0), stop=(k == 15)), r=[r_w1[wb], r_xn], w=[r_ps[bank]])
                    for blk in range(2):
                        tb = h * 2 + blk
                        si = nsg % 2
                        nsg += 1
                        P.add("act", lambda e, si=si, bank=bs + blk: e.activation(out=sg[si][:], in_=ps[:, bank, :], func=AF.Silu),
                              r=[], w=[r_ps[bs + blk], r_sg[si]])
                        P.add("dve", lambda e, si=si, bank=bs + 2 + blk, jl=jl, tb=tb: e.tensor_tensor(
                            out=gT[:, jl, tb * 512:(tb + 1) * 512], in0=ps[:, bank, :], in1=sg[si][:], op=ALU.mult),
                            r=[r_sg[si]], w=[r_ps[bs + 2 + blk], r_g[jl][tb]])
            for c in range(16):
                wb = nw2 % 2
                nw2 += 1
                dma(P, "pool", w2[wb][:, 0:njp, :], chunked(w_out[j0 * 128:(j0 + njp) * 128, c * 128:(c + 1) * 128]), w=[r_w2[wb]])
                bs = (unit % 2) * 4
                unit += 1
                xb = nxr % 2
                nxr += 1
                dma(P, "sp", xr[xb][:], xTc[:, c, :], r=[r_xT[c]], w=[r_xr[xb]])
                for tb in range(4):
                    for jl in range(njp):
                        P.add("pe", lambda e, bank=bs + tb, wb=wb, jl=jl, tb=tb, njp=njp: e.matmul(
                            ps[:, bank, :], lhsT=w2[wb][:, jl, :], rhs=gT[:, jl, tb * 512:(tb + 1) * 512],
                            start=(jl == 0), stop=(jl == njp - 1)), r=[r_w2[wb], r_g[jl][tb]], w=[r_ps[bs + tb]])
                for tb in range(4):
                    P.add("dve", lambda e, bank=bs + tb, xb=xb, tb=tb: e.scalar_tensor_tensor(
                        out=xr[xb][:, tb * 512:(tb + 1) * 512], in0=ps[:, bank, :], scalar=0.5,
                        in1=xr[xb][:, tb * 512:(tb + 1) * 512], op0=ALU.mult, op1=ALU.add),
                        r=[], w=[r_ps[bs + tb], r_xr[xb]])
                dma(P, "sp", xTc[:, c, :], xr[xb][:], r=[r_xr[xb]], w=[r_xT[c]])
        P.end()


def phase_memkv(C, P, memnT, w_kv, gk_col, kmT, vm):
    nc = C.nc
    with contextlib.ExitStack() as st:
        wk = sb(st, nc, "k_w", [128, 16, 1024], BF16)
        sq = sb(st, nc, "k_sq", [128, 256], BF16)
        rstd = sb(st, nc, "k_rstd", [128, 256], F32)
        ps = pst(st, nc, "k_ps", [128, 4, 512], F32)
        r_w = P.Rs(2)
        r_ps = P.Rs(4)
        r_sq, r_rstd, r_km, r_vm, r_mn = P.Rs(5)
        for hf in range(2):
            dma(P, "pool", wk[:, :, hf * 512:(hf + 1) * 512], chunked(w_kv[:, hf * 512:(hf + 1) * 512]), w=[r_w[hf]])
        for h in range(4):
            for k in range(16):
                P.add("pe", lambda e, h=h, k=k: e.matmul(ps[:, 0, 0:256], lhsT=wk[:, k, h * 128:(h + 1) * 128], rhs=memnT[:, k, :],
                                                         start=(k == 0), stop=(k == 15)), r=[r_w[0], r_mn], w=[r_ps[0]])
            P.add("act", lambda e: e.activation(out=sq[:], in_=ps[:, 0, 0:256], func=AF.Square), r=[], w=[r_ps[0], r_sq])
            P.add("pe", lambda e: e.matmul(ps[:, 1, 0:256], lhsT=C.onesb[:], rhs=sq[:], start=True, stop=True), r=[r_sq], w=[r_ps[1]])
            P.add("act", lambda e: e.activation(out=rstd[:], in_=ps[:, 1, 0:256], func=AF.Ln, scale=1.0 / 128, bias=C.epsc[:, 0:1]),
                  r=[], w=[r_ps[1], r_rstd])
            P.add("act", lambda e: e.activation(out=rstd[:], in_=rstd[:], func=AF.Exp, scale=-0.5), r=[], w=[r_rstd])
            P.add("dve", lambda e, h=h: e.scalar_tensor_tensor(out=kmT[:, h, :], in0=ps[:, 0, 0:256], scalar=gk_col[:, 0:1], in1=rstd[:],
                                                              op0=ALU.mult, op1=ALU.mult), r=[r_rstd], w=[r_ps[0], r_km])
        for mt in range(2):
            for k in range(16):
                P.add("pe", lambda e, mt=mt, k=k: e.matmul(ps[:, 2 + mt, :], lhsT=memnT[:, k, mt * 128:(mt + 1) * 128], rhs=wk[:, k, 512:1024],
                                                           start=(k == 0), stop=(k == 15)), r=[r_w[1], r_mn], w=[r_ps[2 + mt]])
            P.add("act", lambda e, mt=mt: e.copy(out=vm[:, mt, :], in_=ps[:, 2 + mt, :]), r=[], w=[r_ps[2 + mt], r_vm])
        P.end()


def phase_win(C, P, hT, w_in, T):
    nc = C.nc
    GROUPS = ((1, 2048), (4, 512), (16, 128))
    with contextlib.ExitStack() as st:
        NWB = 3
        wb = [sb(st, nc, "w_wb%d" % i, [128, 16, 256], BF16) for i in range(NWB)]
        cosT = sb(st, nc, "w_cos", [128, S], F32)
        sinT = sb(st, nc, "w_sin", [128, S], F32)
        Rm = sb(st, nc, "w_R", [128, 128], BF16)
        sq = [sb(st, nc, "w_sq%d" % i, [128, 1024], BF16) for i in range(2)]
        qb = [sb(st, nc, "w_qb%d" % i, [128, 1024], BF16) for i in range(2)]
        rstd = [sb(st, nc, "w_rstd%d" % i, [128, 1024], F32) for i in range(2)]
        u1 = [sb(st, nc, "w_u1%d" % i, [128, 1024], F32) for i in range(2)]
        u2 = [sb(st, nc, "w_u2%d" % i, [128, 1024], F32) for i in range(2)]
        qo = [sb(st, nc, "w_qo%d" % i, [128, 1024], BF16) for i in range(2)]
        ub = [sb(st, nc, "w_ub%d" % i, [128, S + 2], F32) for i in range(2)]
        uc = [sb(st, nc, "w_uc%d" % i, [128, S], F32) for i in range(2)]
        ut = [sb(st, nc, "w_ut%d" % i, [128, 16, 128], BF16) for i in range(2)]
        gs = [sb(st, nc, "w_gs%d" % i, [128, 1024], F32) for i in range(3)]
        vs = [sb(st, nc, "w_vs%d" % i, [128, 256], BF16) for i in range(2)]
        psg = pst(st, nc, "w_psg", [128, 4, 512], F32)
        pss = pst(st, nc, "w_pss", [128, 2, 512], F32)
        psr = pst(st, nc, "w_psr", [128, 2, 512], F32)
        r_wb = P.Rs(NWB)
        r_psg = P.Rs(4)
        r_pss = P.Rs(2)
        r_psr = P.Rs(2)
        r_sq, r_qb, r_rstd, r_u1, r_u2, r_qo, r_ub, r_uc, r_ut, r_vs = (P.Rs(2) for _ in range(10))
        r_gs = P.Rs(3)
        r_c = P.Rs(3)
        r_h = P.R()
        dma(P, "sp", cosT[:], T.cosT, w=[r_c[0]])
        dma(P, "sp", sinT[:], T.sinT, w=[r_c[1]])
        dma(P, "sp", Rm[:], T.Rm, w=[r_c[2]])
        for i in range(2):
            P.add("pool", lambda e, i=i: e.memset(ub[i][:], 0.0), w=[r_ub[i]])
        cnt = {"u": 0, "qk": 0, "hy": 0, "g": 0, "v": 0, "tr": 0}

        def gemm_unit(wbi, jj, h):
            s_ = cnt["u"] % 2
            cnt["u"] += 1
            for blk in range(2):
                bank = s_ * 2 + blk
                tb = h * 2 + blk
                for k in range(16):
                    P.add("pe", lambda e, bank=bank, k=k, tb=tb: e.matmul(psg[:, bank, :], lhsT=wb[wbi][:, k, jj * 128:(jj + 1) * 128],
                                                                          rhs=hT[:, k, tb * 512:(tb + 1) * 512], start=(k == 0), stop=(k == 15)),
                          r=[r_wb[wbi], r_h], w=[r_psg[bank]])
            return s_

        def psv(s_):
            return psg[:, s_ * 2:s_ * 2 + 2, :].rearrange("p b n -> p (b n)")

        def norm_stats_b(b):
            for blk in range(2):
                P.add("pe", lambda e, blk=blk: e.matmul(pss[:, blk, :], lhsT=C.onesb[:], rhs=sq[b][:, blk * 512:(blk + 1) * 512], start=True, stop=True),
                      r=[r_sq[b]], w=[r_pss[blk]])
            P.add("act", lambda e: e.activation(out=rstd[b][:], in_=pss[:].rearrange("p b n -> p (b n)"), func=AF.Ln, scale=1.0 / 128,
                                                bias=C.epsc[:, 0:1]), r=[], w=[r_pss[0], r_pss[1], r_rstd[b]])
            P.add("act", lambda e: e.activation(out=rstd[b][:], in_=rstd[b][:], func=AF.Exp, scale=-0.5), r=[], w=[r_rstd[b]])

        pending = []

        def flush():
            for f in pending:
                f()
            del pending[:]

        dma(P, "sp", T.hT_d, hT[:], r=[r_h])
        for npi, pi in enumerate((0, 9, 1, 10, 11, 2, 12, 3, 13, 14, 4, 15, 5, 16, 17, 18, 19, 6, 7, 8)):
            wbi = npi % NWB
            dma(P, "pool", wb[wbi][:], chunked(w_in[:, pi * 256:(pi + 1) * 256]), w=[r_wb[wbi]])
            if 6 <= pi < 9:
                flush()
                g = pi - 6
                d, L = GROUPS[g]
                nt = L // 128
                for i in range(16):
                    r_, mt = i // nt, i % nt
                    start = r_ + d * mt * 128
                    bank = i % 2
                    for k in range(16):
                        P.add("pe", lambda e, k=k, start=start, d=d, bank=bank, wbi=wbi: e.matmul(
                            pss[:, bank, 0:256], lhsT=hT[:, k, start:start + d * 127 + 1:d], rhs=wb[wbi][:, k, :],
                            start=(k == 0), stop=(k == 15)), r=[r_wb[wbi], r_h], w=[r_pss[bank]])
                    vb = cnt["v"] % 2
                    cnt["v"] += 1
                    P.add("act", lambda e, vb=vb, bank=bank: e.copy(out=vs[vb][:], in_=pss[:, bank, 0:256]), r=[], w=[r_pss[bank], r_vs[vb]])
                    dma(P, "sp", T.va_d[g, i], vs[vb][:], r=[r_vs[vb]])
                continue
            for jj in range(2):
                j = pi * 2 + jj
                for h in range(2):
                    s_ = gemm_unit(wbi, jj, h)
                    flush()
                    banks = [r_psg[s_ * 2], r_psg[s_ * 2 + 1]]
                    hs = slice(h * 1024, (h + 1) * 1024)
                    if j < 12:
                        b = cnt["qk"] % 2
                        cnt["qk"] += 1
                        gcol = T.gq_col if j < 6 else T.gk_col
                        P.add("act", lambda e, s_=s_, b=b: e.activation(out=sq[b][:], in_=psv(s_), func=AF.Square), r=[], w=banks + [r_sq[b]])
                        P.add("act", lambda e, s_=s_, b=b, gcol=gcol: e.activation(out=qb[b][:], in_=psv(s_), func=AF.Copy, scale=gcol[:, 0:1]),
                              r=[], w=banks + [r_qb[b]])
                        P.add("dve", lambda e, s_=s_, b=b, gcol=gcol, hs=hs: e.scalar_tensor_tensor(out=u1[b][:], in0=psv(s_), scalar=gcol[:, 0:1], in1=cosT[:, hs],
                                                                                               op0=ALU.mult, op1=ALU.mult), r=[r_c[0]], w=banks + [r_u1[b]])

                        def stage_b(b=b, hs=hs, j=j):
                            norm_stats_b(b)
                            for blk in range(2):
                                P.add("pe", lambda e, b=b, blk=blk: e.matmul(psr[:, blk, :], lhsT=Rm[:], rhs=qb[b][:, blk * 512:(blk + 1) * 512], start=True, stop=True),
                                      r=[r_qb[b], r_c[2]], w=[r_psr[blk]])
                            P.add("dve", lambda e, b=b, hs=hs: e.tensor_tensor(out=u2[b][:], in0=psr[:].rearrange("p b n -> p (b n)"), in1=sinT[:, hs], op=ALU.mult),
                                  r=[r_c[1]], w=[r_psr[0], r_psr[1], r_u2[b]])
                            P.add("dve", lambda e, b=b: e.tensor_tensor(out=u1[b][:], in0=u1[b][:], in1=u2[b][:], op=ALU.add), r=[r_u2[b]], w=[r_u1[b]])
                            P.add("dve", lambda e, b=b: e.tensor_tensor(out=qo[b][:], in0=u1[b][:], in1=rstd[b][:], op=ALU.mult), r=[r_u1[b], r_rstd[b]], w=[r_qo[b]])
                            dma(P, "sp", T.qk_d[j][:, hs], qo[b][:], r=[r_qo[b]])
                        pending.append(stage_b)
                    elif 18 <= j < 36:
                        jh = j - 18
                        b = cnt["hy"] % 2
                        if h == 1:
                            cnt["hy"] += 1
                        P.add("act", lambda e, s_=s_, b=b, h=h: e.copy(out=ub[b][:, 1 + h * 1024:1 + (h + 1) * 1024], in_=psv(s_)), r=[], w=banks + [r_ub[b]])
                        if h == 1:
                            def stage_b(b=b, jh=jh):
                                P.add("dve", lambda e, b=b, jh=jh: e.tensor_scalar(out=uc[b][:], in0=ub[b][:, 1:S + 1], scalar1=T.cw[:, jh, 1:2], scalar2=T.cb[:, jh:jh + 1],
                                                                                  op0=ALU.mult, op1=ALU.add), r=[r_ub[b]], w=[r_uc[b]])
                                P.add("dve", lambda e, b=b, jh=jh: e.scalar_tensor_tensor(out=uc[b][:], in0=ub[b][:, 0:S], scalar=T.cw[:, jh, 0:1], in1=uc[b][:],
                                                                                         op0=ALU.mult, op1=ALU.add), r=[r_ub[b]], w=[r_uc[b]])
                                P.add("dve", lambda e, b=b, jh=jh: e.scalar_tensor_tensor(out=uc[b][:], in0=ub[b][:, 2:S + 2], scalar=T.cw[:, jh, 2:3], in1=uc[b][:],
                                                                                         op0=ALU.mult, op1=ALU.add), r=[r_ub[b]], w=[r_uc[b]])
                                dma(P, "sp", T.hyx_d[jh * 128:(jh + 1) * 128, :], uc[b][:], r=[r_uc[b]])
                                if jh < 6:
                                    for t4 in range(4):
                                        bank = cnt["tr"] % 2
                                        cnt["tr"] += 1
                                        for tt in range(4):
                                            tc_ = t4 * 4 + tt
                                            P.add("pe", lambda e, b=b, tc_=tc_, bank=bank, tt=tt: e.transpose(out=psr[:, bank, tt * 128:(tt + 1) * 128], in_=uc[b][:, tc_ * 128:(tc_ + 1) * 128],
                                                                                                             identity=C.identf[:]), r=[r_uc[b]], w=[r_psr[bank]])
                                        P.add("act", lambda e, b=b, bank=bank, t4=t4: e.copy(out=ut[b][:, t4 * 4:(t4 + 1) * 4, :], in_=psr[:, bank, :].rearrange("p (a c) -> p a c", c=128)),
                                              r=[], w=[r_psr[bank], r_ut[b]])
                                    dma(P, "sp", T.utm_d.rearrange("(a p) c -> p a c", p=128)[:, :, jh * 128:(jh + 1) * 128], ut[b][:], r=[r_ut[b]])
                            pending.append(stage_b)
                    elif 36 <= j < 40:
                        b = cnt["qk"] % 2
                        cnt["qk"] += 1
                        P.add("act", lambda e, s_=s_, b=b: e.activation(out=sq[b][:], in_=psv(s_), func=AF.Square), r=[], w=banks + [r_sq[b]])
                        P.add("act", lambda e, s_=s_, b=b: e.activation(out=u1[b][:], in_=psv(s_), func=AF.Copy, scale=T.mq_col[:, 0:1]), r=[], w=banks + [r_u1[b]])

                        def stage_b(b=b, hs=hs, j=j):
                            norm_stats_b(b)
                            P.add("dve", lambda e, b=b: e.tensor_tensor(out=qo[b][:], in0=u1[b][:], in1=rstd[b][:], op=ALU.mult), r=[r_u1[b], r_rstd[b]], w=[r_qo[b]])
                            dma(P, "sp", T.qc_d[j - 36][:, hs], qo[b][:], r=[r_qo[b]])
                        pending.append(stage_b)
        flush()
        P.end()


def phase_attn_a(C, P, T):
    nc = C.nc
    GROUPS = ((1, 2048), (4, 512), (16, 128))
    SCALE = 1.0 / math.sqrt(128.0)
    with contextlib.ExitStack() as st:
        mask = sb(st, nc, "a_mask", [128, 256], BF16)
        qT = [sb(st, nc, "a_q%d" % i, [128, S], BF16) for i in range(2)]
        kT = [sb(st, nc, "a_k%d" % i, [128, S], BF16) for i in range(2)]
        vv = [sb(st, nc, "a_v%d" % i, [128, 16, 256], BF16) for i in range(2)]
        accn = sb(st, nc, "a_accn", [128, S], F32)
        accd = sb(st, nc, "a_accd", [128, S], F32)
        E = [sb(st, nc, "a_E%d" % i, [128, 256], BF16) for i in range(3)]
        Em = [sb(st, nc, "a_Em%d" % i, [128, 256], BF16) for i in range(3)]
        yo = sb(st, nc, "a_yo", [128, S], BF16)
        ps = pst(st, nc, "a_ps", [128, 8, 512], F32)
        r_ps = P.Rs(8)
        r_q, r_k, r_v = P.Rs(2), P.Rs(2), P.Rs(2)
        r_E, r_Em = P.Rs(3), P.Rs(3)
        r_mask, r_accn, r_accd, r_yo = P.Rs(4)
        dma(P, "sp", mask[:], T.mask, w=[r_mask])
        nld = 0
        for hh in range(2):
            P.add("pool", lambda e: e.memset(accn[:], 0.0), w=[r_accn])
            P.add("pool", lambda e: e.memset(accd[:], 0.0), w=[r_accd])
            for g in range(3):
                d, L = GROUPS[g]
                nt = L // 128
                head = g * 2 + hh
                lb = nld % 2
                nld += 1
                dma(P, "sp", qT[lb][:], T.qk_d[head], w=[r_q[lb]])
                dma(P, "sp", kT[lb][:], T.qk_d[6 + head], w=[r_k[lb]])
                dma(P, "sp", vv[lb][:], T.va_d[g].rearrange("i p c -> p i c"), w=[r_v[lb]])
                its = [(r_, kt) for r_ in range(d) for kt in range(nt)]
                N = len(its)
                LAG = 2

                def geom(it):
                    r_, kt = its[it]
                    qlo = max(0, 128 * kt - 64)
                    qhi = min(L, 128 * kt + 192)
                    nq = qhi - qlo
                    mo = qlo - (128 * kt - 64)
                    ks = slice(r_ + d * 128 * kt, r_ + d * (128 * kt + 127) + 1, d)
                    qs = slice(r_ + d * qlo, r_ + d * (qhi - 1) + 1, d)
                    return r_ * nt + kt, nq, mo, ks, qs

                for it in range(N + LAG):
                    if it < N:
                        i, nq, mo, ks, qs = geom(it)
                        b3 = it % 3
                        P.add("pe", lambda e, b3=b3, nq=nq, ks=ks, qs=qs, lb=lb: e.matmul(ps[:, b3, 0:nq], lhsT=kT[lb][:, ks], rhs=qT[lb][:, qs], start=True, stop=True),
                              r=[r_q[lb], r_k[lb]], w=[r_ps[b3]])
                        P.add("act", lambda e, b3=b3, nq=nq: e.activation(out=E[b3][:, 0:nq], in_=ps[:, b3, 0:nq], func=AF.Exp, scale=SCALE), r=[], w=[r_ps[b3], r_E[b3]])
                        P.add("pool", lambda e, b3=b3, nq=nq, mo=mo: e.tensor_tensor(out=Em[b3][:, 0:nq], in0=E[b3][:, 0:nq], in1=mask[:, mo:mo + nq], op=ALU.mult),
                              r=[r_E[b3], r_mask], w=[r_Em[b3]])
                    jt = it - LAG
                    if jt >= 0:
                        i, nq, mo, ks, qs = geom(jt)
                        b3 = jt % 3
                        b2 = jt % 2
                        P.add("pe", lambda e, b3=b3, b2=b2, nq=nq, i=i, lb=lb, hh=hh: e.matmul(ps[:, 3 + b2, 0:nq], lhsT=vv[lb][:, i, hh * 128:(hh + 1) * 128], rhs=Em[b3][:, 0:nq],
                                                                                           start=True, stop=True), r=[r_v[lb], r_Em[b3]], w=[r_ps[3 + b2]])
                        P.add("pe", lambda e, b3=b3, b2=b2, nq=nq: e.matmul(ps[:, 5 + b2, 0:nq], lhsT=C.onesb[:], rhs=Em[b3][:, 0:nq], start=True, stop=True),
                              r=[r_Em[b3]], w=[r_ps[5 + b2]])
                        P.add("dve", lambda e, b2=b2, nq=nq, qs=qs: e.tensor_tensor(out=accn[:, qs], in0=accn[:, qs], in1=ps[:, 3 + b2, 0:nq], op=ALU.add), r=[], w=[r_ps[3 + b2], r_accn])
                        P.add("dve", lambda e, b2=b2, nq=nq, qs=qs: e.tensor_tensor(out=accd[:, qs], in0=accd[:, qs], in1=ps[:, 5 + b2, 0:nq], op=ALU.add), r=[], w=[r_ps[5 + b2], r_accd])
            P.add("dve", lambda e: e.reciprocal(out=accd[:], in_=accd[:]), r=[], w=[r_accd])
            P.add("dve", lambda e: e.tensor_tensor(out=yo[:], in0=accn[:], in1=accd[:], op=ALU.mult), r=[r_accn, r_accd], w=[r_yo])
            dma(P, "sp", T.ymix_d[hh], yo[:], r=[r_yo])
        P.end()


def phase_attn_c(C, P, T, kmT, vm):
    nc = C.nc
    SCALE = 1.0 / math.sqrt(128.0)
    with contextlib.ExitStack() as st:
        qT = [sb(st, nc, "c_q%d" % i, [128, S], BF16) for i in range(2)]
        E = [sb(st, nc, "c_E%d" % i, [128, 2, 512], BF16) for i in range(2)]
        rden = [sb(st, nc, "c_rd%d" % i, [128, 512], F32) for i in range(2)]
        yo = [sb(st, nc, "c_yo%d" % i, [128, S], BF16) for i in range(2)]
        ps = pst(st, nc, "c_ps", [128, 8, 512], F32)
        r_ps = P.Rs(8)
        r_q, r_E, r_rd, r_yo = P.Rs(2), P.Rs(2), P.Rs(2), P.Rs(2)
        r_km, r_vm = P.Rs(2)
        its = [(h, tb) for h in range(4) for tb in range(4)]

        def front(it):
            h, tb = its[it]
            qb = h % 2
            b = it % 2
            ts = slice(tb * 512, (tb + 1) * 512)
            if tb == 0:
                dma(P, "sp", qT[qb][:], T.qc_d[h], w=[r_q[qb]])
            for mt in range(2):
                P.add("pe", lambda e, b=b, mt=mt, h=h, qb=qb, ts=ts: e.matmul(ps[:, b * 2 + mt, :], lhsT=kmT[:, h, mt * 128:(mt + 1) * 128], rhs=qT[qb][:, ts], start=True, stop=True),
                      r=[r_km, r_q[qb]], w=[r_ps[b * 2 + mt]])
                P.add("act", lambda e, b=b, mt=mt: e.activation(out=E[b][:, mt, :], in_=ps[:, b * 2 + mt, :], func=AF.Exp, scale=SCALE), r=[], w=[r_ps[b * 2 + mt], r_E[b]])

        def back(it):
            h, tb = its[it]
            qb = h % 2
            b = it % 2
            ts = slice(tb * 512, (tb + 1) * 512)
            for mt in range(2):
                P.add("pe", lambda e, b=b, mt=mt, h=h: e.matmul(ps[:, 4 + b, :], lhsT=vm[:, mt, h * 128:(h + 1) * 128], rhs=E[b][:, mt, :], start=(mt == 0), stop=(mt == 1)),
                      r=[r_vm, r_E[b]], w=[r_ps[4 + b]])
            for mt in range(2):
                P.add("pe", lambda e, b=b, mt=mt: e.matmul(ps[:, 6 + b, :], lhsT=C.onesb[:], rhs=E[b][:, mt, :], start=(mt == 0), stop=(mt == 1)), r=[r_E[b]], w=[r_ps[6 + b]])
            P.add("act", lambda e, b=b: e.activation(out=rden[b][:], in_=ps[:, 6 + b, :], func=AF.Ln), r=[], w=[r_ps[6 + b], r_rd[b]])
            P.add("act", lambda e, b=b: e.activation(out=rden[b][:], in_=rden[b][:], func=AF.Exp, scale=-1.0), r=[], w=[r_rd[b]])
            P.add("dve", lambda e, b=b, qb=qb, ts=ts: e.tensor_tensor(out=yo[qb][:, ts], in0=ps[:, 4 + b, :], in1=rden[b][:], op=ALU.mult), r=[r_rd[b]], w=[r_ps[4 + b], r_yo[qb]])
            if tb == 3:
                dma(P, "sp", T.ymix_d[8 + h], yo[qb][:], r=[r_yo[qb]])

        front(0)
        for it in range(len(its)):
            if it + 1 < len(its):
                front(it + 1)
            back(it)
        P.end()


def phase_hy_filter_mlp(C, P, T, hh3, hh3b):
    nc = C.nc
    MAGIC = 12582912.0
    with contextlib.ExitStack() as st:
        zT = sb(st, nc, "h_zT", [64, S], F32)
        w1 = sb(st, nc, "h_w1", [64, 64], F32)
        w2 = sb(st, nc, "h_w2", [64, 64], F32)
        w3 = sb(st, nc, "h_w3", [64, 64], F32)
        hb = [sb(st, nc, "h_hb%d" % i, [64, S], F32) for i in range(2)]
        v = sb(st, nc, "h_v", [64, S], F32)
        t = sb(st, nc, "h_t", [64, S], F32)
        ps = pst(st, nc, "h_ps", [128, 4, 512], F32)
        r_ps = P.Rs(4)
        r_z, r_w1, r_w2, r_w3, r_v, r_t = P.Rs(6)
        r_hb = P.Rs(2)
        r_h3 = P.R()
        dma(P, "sp", zT[0:33, :], T.zT, w=[r_z])
        dma(P, "sp", w1[0:33, :], T.f_w1, w=[r_w1])
        dma(P, "sp", w2[:], T.f_w2, w=[r_w2])
        dma(P, "sp", w3[:], T.f_w3, w=[r_w3])
        srcs = [(zT, 33, w1, r_z, r_w1), (hb[0], 64, w2, r_hb[0], r_w2), (hb[1], 64, w3, r_hb[1], r_w3)]
        dsts = [(hb[0], r_hb[0]), (hb[1], r_hb[1]), (hh3, r_h3)]
        for li in range(3):
            src, kk, ww, r_s, r_w = srcs[li]
            dst, r_d = dsts[li]
            for tb in range(4):
                P.add("pe", lambda e, tb=tb, src=src, kk=kk, ww=ww: e.matmul(ps[0:64, tb, :], lhsT=ww[0:kk, :], rhs=src[0:kk, tb * 512:(tb + 1) * 512], start=True, stop=True),
                      r=[r_s, r_w], w=[r_ps[tb]])
            P.add("dve", lambda e, li=li: e.tensor_scalar(out=v[:], in0=ps[0:64, :, :].rearrange("p b n -> p (b n)"), scalar1=T.f_b[0:64, li:li + 1], scalar2=T.f_freq[0:64, 0:1],
                                                          op0=ALU.add, op1=ALU.mult), r=[], w=r_ps + [r_v])
            P.add("dve", lambda e: e.tensor_scalar(out=t[:], in0=v[:], scalar1=1.0 / (2 * math.pi), scalar2=MAGIC, op0=ALU.mult, op1=ALU.add), r=[r_v], w=[r_t])
            P.add("dve", lambda e: e.tensor_scalar(out=t[:], in0=t[:], scalar1=-MAGIC, scalar2=-2 * math.pi, op0=ALU.add, op1=ALU.mult), r=[], w=[r_t])
            P.add("dve", lambda e: e.tensor_tensor(out=v[:], in0=v[:], in1=t[:], op=ALU.add), r=[r_t], w=[r_v])
            P.add("act", lambda e, dst=dst: e.activation(out=dst[0:64, :], in_=v[:], func=AF.Sin), r=[r_v], w=[r_d])
        P.add("act", lambda e: e.copy(out=hh3b[0:64, :], in_=hh3[0:64, :]), r=[r_h3], w=[P.R()])
        P.end()


def phase_hy_filter(C, P, T, hh3, o, ksum, kdif):
    nc = C.nc
    with contextlib.ExitStack() as st:
        w4 = sb(st, nc, "g_w4", [64, 1536], BF16)
        dec = [sb(st, nc, "g_dec%d" % i, [128, HYW], F32) for i in range(2)]
        hbs = [sb(st, nc, "g_hbs%d" % i, [128, HYW], F32) for i in range(2)]
        sm = [sb(st, nc, "g_sm%d" % i, [128, HYW], F32) for i in range(2)]
        df = [sb(st, nc, "g_df%d" % i, [128, HYW], F32) for i in range(2)]
        ps = pst(st, nc, "g_ps", [128, 8, 512], F32)
        r_ps = P.Rs(8)
        r_w4, r_h3, r_ks, r_kd = P.Rs(4)
        r_dec, r_hbs, r_sm, r_df = P.Rs(2), P.Rs(2), P.Rs(2), P.Rs(2)
        dma(P, "pool", w4[:], T.f_w4[:, o * 1536:(o + 1) * 1536], w=[r_w4])
        for tt in range(16):
            b = tt % 2
            dma(P, "sp", dec[b][:], T.decay[tt * 128:(tt + 1) * 128, :], w=[r_dec[b]])
            for di in range(2):
                for (c0, cn, bk) in ((0, 512, 0), (512, 256, 1)):
                    bank = b * 4 + di * 2 + bk
                    P.add("pe", lambda e, tt=tt, di=di, c0=c0, cn=cn, bank=bank: e.matmul(ps[:, bank, 0:cn], lhsT=hh3[0:64, tt * 128:(tt + 1) * 128],
                                                                                        rhs=w4[:, di * HYW + c0:di * HYW + c0 + cn], start=True, stop=True),
                          r=[r_h3, r_w4], w=[r_ps[bank]])
            pf = ps[:, b * 4:b * 4 + 2, :].rearrange("p b n -> p (b n)")[:, 0:HYW]
            pb_ = ps[:, b * 4 + 2:b * 4 + 4, :].rearrange("p b n -> p (b n)")[:, 0:HYW]
            bf_ = [r_ps[b * 4], r_ps[b * 4 + 1]]
            bb_ = [r_ps[b * 4 + 2], r_ps[b * 4 + 3]]
            P.add("act", lambda e, b=b, pb_=pb_: e.copy(out=hbs[b][:], in_=pb_), r=[], w=bb_ + [r_hbs[b]])
            P.add("dve", lambda e, b=b, pf=pf: e.tensor_tensor(out=sm[b][:], in0=pf, in1=hbs[b][:], op=ALU.add), r=[r_hbs[b]], w=bf_ + [r_sm[b]])
            P.add("dve", lambda e, b=b, pf=pf: e.tensor_tensor(out=df[b][:], in0=hbs[b][:], in1=pf, op=ALU.subtract), r=[r_hbs[b]], w=bf_ + [r_df[b]])
            P.add("dve", lambda e, b=b, tt=tt: e.tensor_tensor(out=ksum[:, tt, :], in0=sm[b][:], in1=dec[b][:], op=ALU.mult), r=[r_sm[b], r_dec[b]], w=[r_ks])
            P.add("dve", lambda e, b=b, tt=tt: e.tensor_tensor(out=kdif[:, tt, :], in0=df[b][:], in1=dec[b][:], op=ALU.mult), r=[r_df[b], r_dec[b]], w=[r_kd])
        P.end()


def phase_hy_fwd(C, P, T, utm_src, ksum, kdif, Yre, Yim):
    nc = C.nc
    with contextlib.ExitStack() as st:
        utm = sb(st, nc, "d_utm", [128, 16, HYW], BF16)
        cs = [sb(st, nc, "d_cs%d" % i, [128, 2, 16, 128], BF16) for i in range(3)]
        kr = [sb(st, nc, "d_kr%d" % i, [128, HYW], F32) for i in range(2)]
        ki = [sb(st, nc, "d_ki%d" % i, [128, HYW], F32) for i in range(2)]
        t1 = [sb(st, nc, "d_t%d" % i, [128, HYW], F32) for i in range(4)]
        ps = pst(st, nc, "d_ps", [128, 8, 512], F32)
        r_ps = P.Rs(8)
        r_utm, r_ks, r_kd, r_yr, r_yi = P.Rs(5)
        r_cs = P.Rs(3)
        r_kr, r_ki = P.Rs(2), P.Rs(2)
        r_t = P.Rs(4)
        dma(P, "sp", utm[:], utm_src.rearrange("(a p) c -> p a c", p=128), w=[r_utm])

        def pv(b0):
            return ps[:, b0:b0 + 2, :].rearrange("p b n -> p (b n)")[:, 0:HYW]

        for fc in range(16):
            cb = fc % 3
            dma(P, "sp", cs[cb][:, 0], T.cts[fc].rearrange("p (a f) -> p a f", f=128), w=[r_cs[cb]])
            dma(P, "sp", cs[cb][:, 1], T.sts[fc].rearrange("p (a f) -> p a f", f=128), w=[r_cs[cb]])
            jobs = ((0, 0, ksum, r_ks), (2, 1, kdif, r_kd), (4, 0, utm, r_utm), (6, 1, utm, r_utm))
            for (b0, ci, src, r_s) in jobs:
                for (c0, cn, bk) in ((0, 512, 0), (512, 256, 1)):
                    for tc_ in range(16):
                        P.add("pe", lambda e, b0=b0, bk=bk, cn=cn, c0=c0, ci=ci, tc_=tc_, src=src, cb=cb: e.matmul(
                            ps[:, b0 + bk, 0:cn], lhsT=cs[cb][:, ci, tc_, :], rhs=src[:, tc_, c0:c0 + cn], start=(tc_ == 0), stop=(tc_ == 15)),
                            r=[r_cs[cb], r_s], w=[r_ps[b0 + bk]])
            kb = fc % 2
            P.add("act", lambda e, kb=kb: e.copy(out=kr[kb][:], in_=pv(0)), r=[], w=[r_ps[0], r_ps[1], r_kr[kb]])
            P.add("act", lambda e, kb=kb: e.copy(out=ki[kb][:], in_=pv(2)), r=[], w=[r_ps[2], r_ps[3], r_ki[kb]])
            bu = [r_ps[4], r_ps[5]]
            bs_ = [r_ps[6], r_ps[7]]
            P.add("dve", lambda e, kb=kb: e.tensor_tensor(out=t1[0][:], in0=pv(4), in1=kr[kb][:], op=ALU.mult), r=[r_kr[kb]], w=bu + [r_t[0]])
            P.add("dve", lambda e, kb=kb: e.tensor_tensor(out=t1[1][:], in0=pv(6), in1=ki[kb][:], op=ALU.mult), r=[r_ki[kb]], w=bs_ + [r_t[1]])
            P.add("dve", lambda e, kb=kb: e.tensor_tensor(out=t1[2][:], in0=pv(6), in1=kr[kb][:], op=ALU.mult), r=[r_kr[kb]], w=bs_ + [r_t[2]])
            P.add("dve", lambda e, kb=kb: e.tensor_tensor(out=t1[3][:], in0=pv(4), in1=ki[kb][:], op=ALU.mult), r=[r_ki[kb]], w=bu + [r_t[3]])
            P.add("dve", lambda e, fc=fc: e.tensor_tensor(out=Yre[:, fc, :], in0=t1[0][:], in1=t1[1][:], op=ALU.add), r=[r_t[0], r_t[1]], w=[r_yr])
            P.add("dve", lambda e, fc=fc: e.tensor_tensor(out=Yim[:, fc, :], in0=t1[2][:], in1=t1[3][:], op=ALU.subtract), r=[r_t[2], r_t[3]], w=[r_yi])
        P.end()


def phase_hy_inv(C, P, T, o, Yre, Yim):
    nc = C.nc
    with contextlib.ExitStack() as st:
        cs = [sb(st, nc, "i_cs%d" % i, [128, 2, 16, 512], BF16) for i in range(2)]
        zp = [sb(st, nc, "i_zp%d" % i, [128, 512], F32) for i in range(2)]
        xo = [sb(st, nc, "i_xo%d" % i, [128, 512], F32) for i in range(2)]
        zz = [sb(st, nc, "i_zz%d" % i, [128, 512], F32) for i in range(2)]
        zb = [sb(st, nc, "i_zb%d" % i, [128, 512], BF16) for i in range(2)]
        ut = [sb(st, nc, "i_ut%d" % i, [128, 4, 128], BF16) for i in range(2)]
        ps = pst(st, nc, "i_ps", [128, 4, 512], F32)
        r_ps = P.Rs(4)
        r_cs, r_zp, r_xo, r_zz, r_zb, r_ut = (P.Rs(2) for _ in range(6))
        r_yr, r_yi = P.Rs(2)
        zprev = T.hyx_d if o == 0 else T.z1_d
        units = [(tb, cc) for tb in range(4) for cc in range(6)]

        def loads(n):
            tb, cc = units[n]
            b = n % 2
            ts = slice(tb * 512, (tb + 1) * 512)
            dma(P, "sp", zp[b][:], zprev[cc * 128:(cc + 1) * 128, ts], w=[r_zp[b]])
            dma(P, "sp", xo[b][:], T.hyx_d[(o + 1) * HYW + cc * 128:(o + 1) * HYW + (cc + 1) * 128, ts], w=[r_xo[b]])

        pending = []
        for tb in range(2):
            cb = tb % 2
            dma(P, "sp", cs[cb][:, 0], T.cinv[tb].rearrange("p (a t) -> p a t", t=512), w=[r_cs[cb]])
            dma(P, "sp", cs[cb][:, 1], T.sinv[tb].rearrange("p (a t) -> p a t", t=512), w=[r_cs[cb]])
        loads(0)
        for n, (tb, cc) in enumerate(units):
            b = n % 2
            cb = tb % 2
            ts = slice(tb * 512, (tb + 1) * 512)
            if cc == 0 and 1 <= tb <= 2:
                dma(P, "sp", cs[1 - cb][:, 0], T.cinv[tb + 1].rearrange("p (a t) -> p a t", t=512), w=[r_cs[1 - cb]])
                dma(P, "sp", cs[1 - cb][:, 1], T.sinv[tb + 1].rearrange("p (a t) -> p a t", t=512), w=[r_cs[1 - cb]])
            for fc in range(16):
                P.add("pe", lambda e, b=b, fc=fc, cc=cc, cb=cb: e.matmul(ps[:, b, :], lhsT=Yre[:, fc, cc * 128:(cc + 1) * 128], rhs=cs[cb][:, 0, fc, :], start=(fc == 0), stop=False),
                      r=[r_yr, r_cs[cb]], w=[r_ps[b]])
                P.add("pe", lambda e, b=b, fc=fc, cc=cc, cb=cb: e.matmul(ps[:, b, :], lhsT=Yim[:, fc, cc * 128:(cc + 1) * 128], rhs=cs[cb][:, 1, fc, :], start=False, stop=(fc == 15)),
                      r=[r_yi, r_cs[cb]], w=[r_ps[b]])
            for f in pending:
                f()
            del pending[:]
            P.add("dve", lambda e, b=b, cc=cc: e.tensor_scalar(out=zp[b][:], in0=zp[b][:], scalar1=T.hbias[:, o * 6 + cc:o * 6 + cc + 1], scalar2=None, op0=ALU.mult), r=[], w=[r_zp[b]])
            P.add("dve", lambda e, b=b: e.scalar_tensor_tensor(out=zz[b][:], in0=ps[:, b, :], scalar=2.0 / NFFT, in1=zp[b][:], op0=ALU.mult, op1=ALU.add),
                  r=[r_zp[b]], w=[r_ps[b], r_zz[b]])
            if o == 0:
                P.add("dve", lambda e, b=b: e.tensor_tensor(out=zz[b][:], in0=zz[b][:], in1=xo[b][:], op=ALU.mult), r=[r_xo[b]], w=[r_zz[b]])
                if n + 1 < len(units):
                    loads(n + 1)
                dma(P, "sp", T.z1_d[cc * 128:(cc + 1) * 128, ts], zz[b][:], r=[r_zz[b]])

                def stage_b(b=b, tb=tb, cc=cc):
                    for tt in range(4):
                        P.add("pe", lambda e, b=b, tt=tt: e.transpose(out=ps[:, 2 + b, tt * 128:(tt + 1) * 128], in_=zz[b][:, tt * 128:(tt + 1) * 128], identity=C.identf[:]),
                              r=[r_zz[b]], w=[r_ps[2 + b]])
                    P.add("act", lambda e, b=b: e.copy(out=ut[b][:], in_=ps[:, 2 + b, :].rearrange("p (a c) -> p a c", c=128)), r=[], w=[r_ps[2 + b], r_ut[b]])
                    dma(P, "sp", T.utm2_d.rearrange("(a p) c -> p a c", p=128)[:, tb * 4:(tb + 1) * 4, cc * 128:(cc + 1) * 128], ut[b][:], r=[r_ut[b]])
                pending.append(stage_b)
            else:
                P.add("dve", lambda e, b=b: e.tensor_tensor(out=zb[b][:], in0=zz[b][:], in1=xo[b][:], op=ALU.mult), r=[r_xo[b], r_zz[b]], w=[r_zb[b]])
                if n + 1 < len(units):
                    loads(n + 1)
                dma(P, "sp", T.ymix_d[2 + cc][:, ts], zb[b][:], r=[r_zb[b]])
        for f in pending:
            f()
        P.end()


def phase_merge(C, P, T, w_in, w_brs, mg):
    nc = C.nc
    with contextlib.ExitStack() as st:
        hTh = sb(st, nc, "o_hT", [128, 16, 1024], BF16)
        ym = sb(st, nc, "o_ym", [128, 12, 1024], BF16)
        wg = [sb(st, nc, "o_wg%d" % i, [128, 16, 3, 128], BF16) for i in range(2)]
        wbr = [sb(st, nc, "o_wbr%d" % i, [128, 12, 128], BF16) for i in range(2)]
        sg = [sb(st, nc, "o_sg%d" % i, [128, 1024], F32) for i in range(3)]
        tq = [sb(st, nc, "o_tq%d" % i, [128, 1024], F32) for i in range(2)]
        mm = [sb(st, nc, "o_mm%d" % i, [128, 1024], F32) for i in range(2)]
        ps = pst(st, nc, "o_ps", [128, 8, 512], F32)
        r_ps = P.Rs(8)
        r_h = P.Rs(16)
        r_ym = P.Rs(12)
        r_wg, r_wbr, r_tq, r_mm = P.Rs(2), P.Rs(2), P.Rs(2), P.Rs(2)
        r_sg = P.Rs(3)
        r_mg = P.R()
        BR = ((0, 2, 0), (2, 6, 1), (8, 4, 2))
        GATE0 = 5120
        n = 0
        nu = 0
        nt = 0
        for h in range(2):
            hs = slice(h * 1024, (h + 1) * 1024)
            for k in range(16):
                dma(P, "sp", hTh[:, k, :], T.hT_d[:, k, hs], w=[r_h[k]])
            for i in range(12):
                dma(P, "sp", ym[:, i, :], T.ymix_d[i][:, hs], w=[r_ym[i]])
            for c in range(16):
                wb_ = c % 2
                for (k0, nk, bi) in BR:
                    dma(P, "pool", wg[wb_][:, :, bi, :], chunked(w_in[:, GATE0 + bi * D + c * 128:GATE0 + bi * D + (c + 1) * 128]), w=[r_wg[wb_]])
                for (k0, nk, bi) in BR:
                    dma(P, "pool", wbr[wb_][:, k0:k0 + nk, :], chunked(w_brs[bi][:, c * 128:(c + 1) * 128]), w=[r_wbr[wb_]])
                mb = nu % 2
                nu += 1
                for (k0, nk, bi) in BR:
                    gset = (n % 2) * 2
                    bset = 4 + (n % 2) * 2
                    sb_ = n % 3
                    n += 1
                    for blk in range(2):
                        for k in range(16):
                            P.add("pe", lambda e, gset=gset, blk=blk, k=k, wb_=wb_, bi=bi: e.matmul(
                                ps[:, gset + blk, :], lhsT=wg[wb_][:, k, bi, :], rhs=hTh[:, k, blk * 512:(blk + 1) * 512], start=(k == 0), stop=(k == 15)),
                                r=[r_wg[wb_], r_h[k]], w=[r_ps[gset + blk]])
                    for blk in range(2):
                        for kk in range(nk):
                            P.add("pe", lambda e, bset=bset, blk=blk, k=k0 + kk, wb_=wb_, kk=kk, nk=nk: e.matmul(
                                ps[:, bset + blk, :], lhsT=wbr[wb_][:, k, :], rhs=ym[:, k, blk * 512:(blk + 1) * 512], start=(kk == 0), stop=(kk == nk - 1)),
                                r=[r_wbr[wb_], r_ym[k0 + kk]], w=[r_ps[bset + blk]])
                    gv = ps[:, gset:gset + 2, :].rearrange("p b n -> p (b n)")
                    bv = ps[:, bset:bset + 2, :].rearrange("p b n -> p (b n)")
                    P.add("act", lambda e, gv=gv, sb_=sb_: e.activation(out=sg[sb_][:], in_=gv, func=AF.Sigmoid), r=[], w=[r_ps[gset], r_ps[gset + 1], r_sg[sb_]])
                    bbanks = [r_ps[bset], r_ps[bset + 1]]
                    if bi == 0:
                        P.add("dve", lambda e, bv=bv, sb_=sb_, mb=mb: e.tensor_tensor(out=mm[mb][:], in0=bv, in1=sg[sb_][:], op=ALU.mult), r=[r_sg[sb_]], w=bbanks + [r_mm[mb]])
                    else:
                        tb_ = nt % 2
                        nt += 1
                        P.add("dve", lambda e, bv=bv, sb_=sb_, tb_=tb_: e.tensor_tensor(out=tq[tb_][:], in0=bv, in1=sg[sb_][:], op=ALU.mult), r=[r_sg[sb_]], w=bbanks + [r_tq[tb_]])
                        if bi == 1:
                            P.add("dve", lambda e, mb=mb, tb_=tb_: e.tensor_tensor(out=mm[mb][:], in0=mm[mb][:], in1=tq[tb_][:], op=ALU.add), r=[r_tq[tb_]], w=[r_mm[mb]])
                        else:
                            P.add("dve", lambda e, mb=mb, tb_=tb_, c=c, hs=hs: e.tensor_tensor(out=mg[:, c, hs], in0=mm[mb][:], in1=tq[tb_][:], op=ALU.add),
                                  r=[r_tq[tb_], r_mm[mb]], w=[r_mg])
        P.end()


def phase_out_proj(C, P, mg, w_o, xT_dram):
    nc = C.nc
    xTc = chunked(xT_dram)
    with contextlib.ExitStack() as st:
        wo = [sb(st, nc, "p_wo%d" % i, [128, 16, 256], BF16) for i in range(2)]
        xr = [sb(st, nc, "p_xr%d" % i, [128, S], F32) for i in range(3)]
        ps = pst(st, nc, "p_ps", [128, 8, 512], F32)
        r_ps = P.Rs(8)
        r_wo = P.Rs(2)
        r_xr = P.Rs(3)
        r_mg = P.R()
        for c2 in range(8):
            wb_ = c2 % 2
            dma(P, "pool", wo[wb_][:], chunked(w_o[:, c2 * 256:(c2 + 1) * 256]), w=[r_wo[wb_]])
            for cc in range(2):
                c = c2 * 2 + cc
                bs = (c % 2) * 4
                xb = c % 3
                dma(P, "sp", xr[xb][:], xTc[:, c, :], w=[r_xr[xb]])
                for tb in range(4):
                    for k in range(16):
                        P.add("pe", lambda e, bank=bs + tb, wb_=wb_, k=k, cc=cc, tb=tb: e.matmul(
                            ps[:, bank, :], lhsT=wo[wb_][:, k, cc * 128:(cc + 1) * 128], rhs=mg[:, k, tb * 512:(tb + 1) * 512],
                            start=(k == 0), stop=(k == 15)), r=[r_wo[wb_], r_mg], w=[r_ps[bs + tb]])
                for tb in range(4):
                    P.add("dve", lambda e, bank=bs + tb, xb=xb, tb=tb: e.tensor_tensor(
                        out=xr[xb][:, tb * 512:(tb + 1) * 512], in0=ps[:, bank, :], in1=xr[xb][:, tb * 512:(tb + 1) * 512], op=ALU.add),
                        r=[], w=[r_ps[bs + tb], r_xr[xb]])
                dma(P, "sp", xTc[:, c, :], xr[xb][:], r=[r_xr[xb]])
        P.end()
NCOL = 136


def build_program(upto=99, debug=False):
    nc = bass.Bass("TRN2", target_bir_lowering=False)
    C = Ctx()
    C.nc = nc
    T = Ctx()

    def inp(name, shape, dt=F32):
        return nc.dram_tensor(name, list(shape), dt, kind="ExternalInput").ap()

    def scr(name, shape, dt=F32):
        return nc.dram_tensor(name, list(shape), dt).ap()

    x = inp("x", [S, D])
    mem = inp("mem", [NMEM, D])
    g_ff1 = inp("g_ff1", [1, D])
    g_mem = inp("g_mem", [1, D])
    w_ff1_in = inp("w_ff1_in", [D, 2 * DFF])
    w_ff1_out = inp("w_ff1_out", [DFF, D])
    w_ff2_in = inp("w_ff2_in", [D, 2 * DFF])
    w_ff2_out = inp("w_ff2_out", [DFF, D])
    w_in = inp("w_in", [D, INW])
    w_mem_kv = inp("w_mem_kv", [D, 1024])
    w_br_a = inp("w_br_a", [256, D])
    w_br_b = inp("w_br_b", [768, D])
    w_br_c = inp("w_br_c", [512, D])
    w_out = inp("w_out", [D, D])
    T.f_w1 = inp("hy_f_w1", [33, 64])
    T.f_w2 = inp("hy_f_w2", [64, 64])
    T.f_w3 = inp("hy_f_w3", [64, 64])
    T.f_w4 = inp("hy_f_w4", [64, 3072])
    cols_d = inp("c_cols", [128, NCOL])
    fcols_d = inp("c_fcols", [64, 4])
    identb_d = inp("c_identb", [128, 128], BF16)
    identf_d = inp("c_identf", [128, 128])
    T.cosT = inp("c_cosT", [128, S])
    T.sinT = inp("c_sinT", [128, S])
    T.Rm = inp("c_Rm", [128, 128], BF16)
    T.mask = inp("c_mask", [128, 256], BF16)
    T.zT = inp("c_zT", [33, S])
    T.decay = inp("c_decay", [S, HYW])
    T.cts = inp("c_cts", [16, 128, 2048], BF16)
    T.sts = inp("c_sts", [16, 128, 2048], BF16)
    T.cinv = inp("c_cinv", [4, 128, 8192], BF16)
    T.sinv = inp("c_sinv", [4, 128, 8192], BF16)
    out = nc.dram_tensor("out", [S, D], F32, kind="ExternalOutput").ap()
    xT = scr("xT_scr", [D, S])
    T.qk_d = scr("qk_scr", [12, 128, S], BF16)
    T.va_d = scr("va_scr", [3, 16, 128, 256], BF16)
    T.hyx_d = scr("hyx_scr", [3 * HYW, S])
    T.utm_d = scr("utm_scr", [S, HYW], BF16)
    T.utm2_d = scr("utm2_scr", [S, HYW], BF16)
    T.qc_d = scr("qc_scr", [4, 128, S], BF16)
    T.hT_d = scr("hT_scr", [128, 16, S], BF16)
    T.z1_d = scr("z1_scr", [HYW, S])
    T.ymix_d = scr("ymix_scr", [12, 128, S], BF16)
    dbg = {}
    if debug:
        dbg["xT"] = nc.dram_tensor("dbg_xT", [D, S], F32, kind="ExternalOutput").ap()
        dbg["ymix"] = nc.dram_tensor("dbg_ymix", [12, 128, S], BF16, kind="ExternalOutput").ap()
        dbg["qk"] = nc.dram_tensor("dbg_qk", [12, 128, S], BF16, kind="ExternalOutput").ap()
        dbg["hyx"] = nc.dram_tensor("dbg_hyx", [3 * HYW, S], F32, kind="ExternalOutput").ap()
        dbg["z1"] = nc.dram_tensor("dbg_z1", [HYW, S], F32, kind="ExternalOutput").ap()

    with contextlib.ExitStack() as st:
        P = Prog(nc, st)
        C.identb = sb(st, nc, "identb", [128, 128], BF16)
        C.identf = sb(st, nc, "identf", [128, 128], F32)
        C.onesb = sb(st, nc, "onesb", [128, 128], BF16)
        C.epsc = sb(st, nc, "epsc", [128, 1], F32)
        cols = sb(st, nc, "cols", [128, NCOL], F32)
        fcols = sb(st, nc, "fcols", [64, 4], F32)
        kmT = sb(st, nc, "kmT", [128, 4, NMEM], BF16)
        vm = sb(st, nc, "vm", [128, 2, 512], BF16)
        r0 = P.Rs(6)
        dma(P, "sp", C.identb[:], identb_d, w=[r0[0]])
        dma(P, "sp", C.identf[:], identf_d, w=[r0[1]])
        dma(P, "sp", cols[:], cols_d, w=[r0[2]])
        dma(P, "sp", fcols[:], fcols_d, w=[r0[3]])
        P.add("dve", lambda e: e.memset(C.onesb[:], 1.0), w=[r0[4]])
        P.add("dve", lambda e: e.memset(C.epsc[:], EPS), w=[r0[5]])
        P.end()
        T.gq_col = cols[:, 0:1]
        T.gk_col = cols[:, 1:2]
        T.mq_col = cols[:, 2:3]
        mk_col = cols[:, 3:4]
        gmixT = cols[:, 4:20]
        gff2T = cols[:, 20:36]
        gpostT = cols[:, 36:52]
        T.hbias = cols[:, 52:64]
        T.cw = cols[:, 64:118].rearrange("p (j t) -> p j t", t=3)
        T.cb = cols[:, 118:136]
        T.f_b = fcols[:, 0:3]
        T.f_freq = fcols[:, 3:4]

        def dump():
            if debug:
                r = P.Rs(5)
                dma(P, "sp", dbg["xT"], xT, w=[r[0]])
                dma(P, "sp", dbg["ymix"], T.ymix_d, w=[r[1]])
                dma(P, "sp", dbg["qk"], T.qk_d, w=[r[2]])
                dma(P, "sp", dbg["hyx"], T.hyx_d, w=[r[3]])
                dma(P, "sp", dbg["z1"], T.z1_d, w=[r[4]])
                P.end()

        def run():
            with contextlib.ExitStack() as st2:
                xnT = sb(st2, nc, "xnT", [128, 16, S], BF16)
                phase_tm_norm(C, P, x, 16, g_ff1[0:1, :].broadcast_to([128, D]), xnT, P.R(), xT_dram=xT)
                if upto < 1:
                    return
                phase_ffn(C, P, xnT, w_ff1_in, w_ff1_out, xT)
            if upto < 2:
                return
            with contextlib.ExitStack() as st2:
                memnT = sb(st2, nc, "memnT", [128, 16, NMEM], BF16)
                phase_tm_norm(C, P, mem, 2, g_mem[0:1, :].broadcast_to([128, D]), memnT, P.R())
                phase_memkv(C, P, memnT, w_mem_kv, mk_col, kmT, vm)
            with contextlib.ExitStack() as st2:
                hT = sb(st2, nc, "hT", [128, 16, S], BF16)
                phase_fm_norm(C, P, xT, gmixT, hT)
                phase_win(C, P, hT, w_in, T)
            if upto < 3:
                return
            phase_attn_a(C, P, T)
            phase_attn_c(C, P, T, kmT, vm)
            if upto < 4:
                return
            with contextlib.ExitStack() as st2:
                hh3 = sb(st2, nc, "hh3", [64, S], F32)
                hh3b = sb(st2, nc, "hh3b", [64, S], BF16)
                phase_hy_filter_mlp(C, P, T, hh3, hh3b)
                for o in range(2):
                    with contextlib.ExitStack() as st3:
                        Yre = sb(st3, nc, "Yre", [128, 16, HYW], BF16)
                        Yim = sb(st3, nc, "Yim", [128, 16, HYW], BF16)
                        with contextlib.ExitStack() as st4:
                            ksum = sb(st4, nc, "ksum", [128, 16, HYW], BF16)
                            kdif = sb(st4, nc, "kdif", [128, 16, HYW], BF16)
                            phase_hy_filter(C, P, T, hh3b, o, ksum, kdif)
                            phase_hy_fwd(C, P, T, T.utm_d if o == 0 else T.utm2_d, ksum, kdif, Yre, Yim)
                        phase_hy_inv(C, P, T, o, Yre, Yim)
            if upto < 5:
                return
            with contextlib.ExitStack() as st2:
                mg = sb(st2, nc, "mg", [128, 16, S], BF16)
                phase_merge(C, P, T, w_in, (w_br_a, w_br_b, w_br_c), mg)
                phase_out_proj(C, P, mg, w_out, xT)
            if upto < 6:
                return
            with contextlib.ExitStack() as st2:
                xnT = sb(st2, nc, "xnT2", [128, 16, S], BF16)
                phase_fm_norm(C, P, xT, gff2T, xnT)
                phase_ffn(C, P, xnT, w_ff2_in, w_ff2_out, xT)
            if upto < 7:
                return
            phase_fm_norm(C, P, xT, gpostT, None, final_out=out)

        run()
        dump()
    C.P = P
    return nc


_CONSTS = {}


def host_consts():
    if _CONSTS:
        return _CONSTS
    bf = ml_dtypes.bfloat16
    c = {}
    c["c_identb"] = np.eye(128, dtype=np.float32).astype(bf)
    c["c_identf"] = np.eye(128, dtype=np.float32)
    inv = np.power(np.float32(500000.0), -np.arange(0, 32, 2, dtype=np.float32) / np.float32(32)).astype(np.float32)
    ang = (np.arange(S, dtype=np.float32)[:, None] * inv[None, :]).astype(np.float32)
    cosT = np.ones((128, S), np.float32)
    sinT = np.zeros((128, S), np.float32)
    cosT[0:16] = np.cos(ang).T
    cosT[16:32] = np.cos(ang).T
    sinT[0:16] = np.sin(ang).T
    sinT[16:32] = np.sin(ang).T
    c["c_cosT"] = cosT
    c["c_sinT"] = sinT
    Rm = np.zeros((128, 128), np.float32)
    for d in range(16):
        Rm[d + 16, d] = -1.0
        Rm[d, d + 16] = 1.0
    c["c_Rm"] = Rm.astype(bf)
    a = np.arange(128)[:, None]
    b = np.arange(256)[None, :]
    c["c_mask"] = ((b >= a) & (b <= a + 128)).astype(np.float32).astype(bf)
    bands = 16
    t = np.linspace(0.0, 1.0, S, dtype=np.float32)[:, None]
    w = (2.0 * math.pi * np.arange(S, dtype=np.float32)[:, None] / S).astype(np.float32)
    f = np.linspace(1e-4, bands - 1, bands, dtype=np.float32)[None, :]
    z = np.concatenate([t, np.cos(f * w), -np.sin(f * w)], axis=-1).astype(np.float32)
    c["c_zT"] = np.ascontiguousarray(z.T)
    max_decay = math.log(1e-2) / 0.3
    min_decay = math.log(1e-2) / 1.5
    deltas = np.linspace(min_decay, max_decay, HYW, dtype=np.float32)
    c["c_decay"] = np.exp(-t * np.abs(deltas)[None, :]).astype(np.float32)
    fi = np.arange(2048, dtype=np.int64)
    ti = np.arange(2048, dtype=np.int64)
    ph = ((2 * fi[:, None] + 1) * ti[None, :]) % (2 * NFFT)
    th = ph.astype(np.float64) * (math.pi / NFFT)
    Cft = np.cos(th)
    Sft = np.sin(th)
    def fwd(M):
        A = M.T.reshape(16, 128, 16, 128)
        return np.ascontiguousarray(A.transpose(2, 1, 0, 3).reshape(16, 128, 2048).astype(np.float32).astype(bf))
    def invm(M):
        A = M.reshape(16, 128, 4, 512)
        return np.ascontiguousarray(A.transpose(2, 1, 0, 3).reshape(4, 128, 8192).astype(np.float32).astype(bf))
    c["c_cts"] = fwd(Cft)
    c["c_sts"] = fwd(Sft)
    c["c_cinv"] = invm(Cft)
    c["c_sinv"] = invm(Sft)
    _CONSTS.update(c)
    return _CONSTS


def layout_small(inputs):
    cols = np.zeros((128, NCOL), np.float32)
    cols[:, 0] = inputs["a_gq"][0]
    cols[:, 1] = inputs["a_gk"][0]
    cols[:, 2] = inputs["m_gq"][0]
    cols[:, 3] = inputs["m_gk"][0]
    cols[:, 4:20] = inputs["g_mix"][0].reshape(16, 128).T
    cols[:, 20:36] = inputs["g_ff2"][0].reshape(16, 128).T
    cols[:, 36:52] = inputs["g_post"][0].reshape(16, 128).T
    cols[:, 52:64] = inputs["hy_bias"][0].reshape(12, 128).T
    cw = inputs["hy_conv_w"][0]
    cols[:, 64:118] = cw.reshape(3, 18, 128).transpose(2, 1, 0).reshape(128, 54)
    cols[:, 118:136] = inputs["hy_conv_b"][0].reshape(18, 128).T
    fcols = np.zeros((64, 4), np.float32)
    fcols[:, 0] = inputs["hy_f_b1"][0]
    fcols[:, 1] = inputs["hy_f_b2"][0]
    fcols[:, 2] = inputs["hy_f_b3"][0]
    fcols[:, 3] = inputs["hy_f_freq"][0]
    return {"c_cols": cols, "c_fcols": fcols}


BIG = ("g_ff1", "g_mem", "w_ff1_in", "w_ff1_out", "w_ff2_in", "w_ff2_out", "w_in", "w_mem_kv", "w_br_a", "w_br_b",
       "w_br_c", "w_out", "hy_f_w1", "hy_f_w2", "hy_f_w3", "hy_f_w4")


def make_in_map(inputs, b):
    m = dict(host_consts())
    m.update(layout_small(inputs))
    m["x"] = np.ascontiguousarray(inputs["x"][b])
    m["mem"] = np.ascontiguousarray(inputs["mem"][b])
    for k in BIG:
        v = np.asarray(inputs[k])
        m[k] = np.ascontiguousarray(v[0]) if v.ndim == 3 else np.ascontiguousarray(v)
    return m


def kernel(**inputs):
    inputs = {k: np.asarray(v) for k, v in inputs.items()}
    nc = build_program()
    in_maps = [make_in_map(inputs, b) for b in range(8)]
    res = run_bass_kernel_spmd(nc, in_maps, core_ids=list(range(8)))
    return np.stack([np.asarray(r["out"], dtype=np.float32) for r in res.results], axis=0)
```

```python
import contextlib
import math
import numpy as np
import ml_dtypes
import concourse.bass as bass
import concourse.mybir as mybir
from concourse.bass_utils import run_bass_kernel_spmd

F32 = mybir.dt.float32
BF16 = mybir.dt.bfloat16
AF = mybir.ActivationFunctionType
ALU = mybir.AluOpType

D = 2048
S = 2048
DFF = 5632
NMEM = 256
EPS = 1e-6
INW = 11264
HYW = 768
NFFT = 4096
ENGS = ("pe", "act", "dve", "pool", "sp")
SAME_ENGINE_SYNC = True


class Res:
    __slots__ = ("name", "lw", "rd", "rdd")

    def __init__(self, name=""):
        self.name = name
        self.lw = None
        self.rd = {}
        self.rdd = []


class Op:
    __slots__ = ("eng", "fn", "deps", "dma", "need_inc", "sem", "val", "pre", "n")

    def __init__(self, eng, fn, dma, n):
        self.eng = eng
        self.fn = fn
        self.dma = dma
        self.deps = []
        self.need_inc = False
        self.sem = None
        self.val = 0
        self.pre = None
        self.n = n


class Prog:
    def __init__(self, nc, stack, ndma=8):
        self.nc = nc
        self.S = ndma
        self.sem = {e: stack.enter_context(nc.semaphore("c_" + e)) for e in ENGS}
        self.cnt = {e: 0 for e in ENGS}
        self.dsem = {
            q: [stack.enter_context(nc.semaphore("d_%s%d" % (q, i))) for i in range(ndma)]
            for q in ("sp", "act", "pool")
        }
        self.dcnt = {q: 0 for q in ("sp", "act", "pool")}
        self.waited = {e: {} for e in ENGS}
        self.ops = []
        self.res = []
        self.nphase = 0
        self.total_ops = 0

    def R(self, name=""):
        r = Res(name)
        self.res.append(r)
        return r

    def Rs(self, n, name=""):
        return [self.R("%s%d" % (name, i)) for i in range(n)]

    def add(self, eng, fn, r=(), w=(), dma=False):
        op = Op(eng, fn, dma, len(self.ops))
        cd = {}
        dd = {}

        def dep(p):
            if p is None or p is op:
                return
            if p.dma:
                dd[id(p)] = p
            else:
                q = cd.get(p.eng)
                if q is None or q.n < p.n:
                    cd[p.eng] = p

        for x in r:
            dep(x.lw)
        for x in w:
            dep(x.lw)
            for y in x.rd.values():
                dep(y)
            for y in x.rdd:
                dep(y)
        for x in r:
            if dma:
                x.rdd.append(op)
            else:
                x.rd[eng] = op
        for x in w:
            x.lw = op
            x.rd = {}
            x.rdd = []
        op.deps = list(cd.values()) + list(dd.values())
        self.ops.append(op)
        return op

    def _needs_signal(self, p, x):
        if p.dma:
            return True
        if p.eng == x.eng and not x.dma:
            if p.eng == "pe":
                return False
            return SAME_ENGINE_SYNC
        return True

    def end(self):
        nc = self.nc
        ops = self.ops
        for x in ops:
            x.deps = [p for p in x.deps if self._needs_signal(p, x)]
            for p in x.deps:
                p.need_inc = True
        for x in ops:
            if x.dma:
                i = self.dcnt[x.eng]
                self.dcnt[x.eng] += 1
                x.sem = self.dsem[x.eng][i % self.S]
                x.val = 16 * (i // self.S + 1)
                x.pre = (x.sem, 16 * (i // self.S)) if i >= self.S else None
            elif x.need_inc:
                self.cnt[x.eng] += 1
                x.sem = self.sem[x.eng]
                x.val = self.cnt[x.eng]
        final_dma = []
        for q in ("sp", "act", "pool"):
            n = self.dcnt[q]
            for j in range(min(n, self.S)):
                i = n - 1 - j
                final_dma.append((self.dsem[q][i % self.S], 16 * (i // self.S + 1)))

        def emit(eng_name, e):
            wd = self.waited[eng_name]

            def wait(sem, val):
                if wd.get(id(sem), 0) >= val:
                    return
                wd[id(sem)] = val
                e.wait_ge(sem, val)

            for x in ops:
                if x.eng != eng_name:
                    continue
                for p in x.deps:
                    wait(p.sem, p.val)
                if x.pre is not None:
                    wait(*x.pre)
                ins = x.fn(e)
                if x.dma:
                    ins.then_inc(x.sem, 16)
                elif x.need_inc:
                    ins.then_inc(x.sem, 1)
            if eng_name == "sp":
                for sem, val in final_dma:
                    wait(sem, val)

        with nc.Block("ph%d" % self.nphase, no_gpsimd_drain=True) as block:
            @block.tensor
            def _(e):
                emit("pe", e)

            @block.scalar
            def _(e):
                emit("act", e)

            @block.vector
            def _(e):
                emit("dve", e)

            @block.gpsimd
            def _(e):
                emit("pool", e)

            @block.sync
            def _(e):
                emit("sp", e)

        self.nphase += 1
        self.total_ops += len(ops)
        self.ops = []
        for r in self.res:
            r.lw = None
            r.rd = {}
            r.rdd = []
        self.res = []


class Ctx:
    pass


def dma(P, q, out, in_, r=(), w=()):
    return P.add(q, lambda e: e.dma_start(out=out, in_=in_), r=r, w=w, dma=True)


_UID = [0]


def uid(name):
    _UID[0] += 1
    return "%s_u%d" % (name, _UID[0])


def sb(st, nc, name, shape, dt):
    return st.enter_context(nc.sbuf_tensor(uid(name), shape, dt))


def pst(st, nc, name, shape, dt):
    return st.enter_context(nc.psum_tensor(uid(name), shape, dt))


def chunked(ap2d):
    return ap2d.rearrange("(k p) n -> p k n", p=128)


def phase_tm_norm(C, P, src, ntiles, g_row, dstT, r_dst, xT_dram=None):
    nc = C.nc
    with contextlib.ExitStack() as st:
        gb = sb(st, nc, "n_gb", [128, D], F32)
        xs = [sb(st, nc, "n_xs%d" % i, [128, D], F32) for i in range(3)]
        junk = sb(st, nc, "n_junk", [128, D], BF16)
        ss = [sb(st, nc, "n_ss%d" % i, [128, 1], F32) for i in range(3)]
        rs = [sb(st, nc, "n_rs%d" % i, [128, 1], F32) for i in range(3)]
        xb = [sb(st, nc, "n_xb%d" % i, [128, D], BF16) for i in range(3)]
        xTs = [sb(st, nc, "n_xTs%d" % i, [128, 16, 512], F32) for i in range(2)]
        pt = [pst(st, nc, "n_pt%d" % i, [128, 8, 128], BF16) for i in range(3)]
        ptf = [pst(st, nc, "n_ptf%d" % i, [128, 4, 128], F32) for i in range(4)]
        r_gb = P.R()
        r_xs, r_ss, r_rs, r_xb, r_xTs, r_pt, r_ptf = (P.Rs(4) for _ in range(7))
        r_junk = P.R()
        r_xs = P.Rs(3)
        dma(P, "sp", gb[:], g_row, w=[r_gb])
        npt = 0
        nptf = 0
        dma(P, "sp", xs[0][:], src[0:128, :], w=[r_xs[0]])
        for i in range(ntiles):
            b = i % 3
            bx = i % 3
            if i + 1 < ntiles:
                dma(P, "sp", xs[(i + 1) % 3][:], src[(i + 1) * 128:(i + 2) * 128, :], w=[r_xs[(i + 1) % 3]])
            P.add("act", lambda e, b=b, bx=bx: e.activation(out=junk[:], in_=xs[bx][:], func=AF.Square, accum_out=ss[b][:]),
                  r=[r_xs[bx]], w=[r_junk, r_ss[b]])
            P.add("act", lambda e, b=b: e.activation(out=rs[b][:], in_=ss[b][:], func=AF.Sqrt, scale=1.0 / D, bias=C.epsc[:, 0:1]),
                  r=[r_ss[b]], w=[r_rs[b]])
            P.add("dve", lambda e, b=b: e.reciprocal(out=rs[b][:], in_=rs[b][:]), r=[], w=[r_rs[b]])
            P.add("dve", lambda e, b=b, bx=bx: e.scalar_tensor_tensor(out=xb[b][:], in0=xs[bx][:], scalar=rs[b][:, 0:1],
                                                               in1=gb[:], op0=ALU.mult, op1=ALU.mult),
                  r=[r_xs[bx], r_rs[b], r_gb], w=[r_xb[b]])
            for k0 in range(0, 16, 8):
                s_ = npt % 3
                npt += 1
                for k in range(k0, k0 + 8):
                    P.add("pe", lambda e, b=b, k=k, s_=s_: e.transpose(out=pt[s_][:, k % 8, :],
                                                                      in_=xb[b][:, k * 128:(k + 1) * 128],
                                                                      identity=C.identb[:]),
                          r=[r_xb[b]], w=[r_pt[s_]])
                P.add("act", lambda e, s_=s_, k0=k0, i=i: e.copy(out=dstT[:, k0:k0 + 8, i * 128:(i + 1) * 128],
                                                              in_=pt[s_][:]),
                      r=[], w=[r_pt[s_], r_dst])
            if xT_dram is not None:
                for k0 in range(0, 16, 4):
                    s_ = nptf % 4
                    nptf += 1
                    for k in range(k0, k0 + 4):
                        P.add("pe", lambda e, bx=bx, k=k, s_=s_: e.transpose(out=ptf[s_][:, k % 4, :],
                                                                          in_=xs[bx][:, k * 128:(k + 1) * 128],
                                                                          identity=C.identf[:]),
                              r=[r_xs[bx]], w=[r_ptf[s_]])
                    P.add("dve", lambda e, s_=s_, k0=k0, q=(i // 4) % 2, o_=(i % 4) * 128: e.tensor_copy(out=xTs[q][:, k0:k0 + 4, o_:o_ + 128], in_=ptf[s_][:]),
                          r=[], w=[r_ptf[s_], r_xTs[(i // 4) % 2]])
                if i % 4 == 3:
                    i4 = i // 4
                    dma(P, "sp", chunked(xT_dram)[:, :, i4 * 512:(i4 + 1) * 512], xTs[i4 % 2][:], r=[r_xTs[i4 % 2]])
        P.end()


def phase_fm_norm(C, P, xT_dram, gT_col, dstT, final_out=None):
    nc = C.nc
    xTc = chunked(xT_dram)
    G = 512
    NG = S // G
    NB = G // 512
    with contextlib.ExitStack() as st:
        xr = [sb(st, nc, "f_xr%d" % i, [128, G], F32) for i in range(32)]
        sq = [sb(st, nc, "f_sq%d" % i, [128, G], BF16) for i in range(2)]
        rstd = [sb(st, nc, "f_rstd%d" % i, [128, G], F32) for i in range(2)]
        ps = pst(st, nc, "f_ps", [128, 4, 512], F32)
        r_xr = P.Rs(32)
        r_sq = P.Rs(2)
        r_ps = P.Rs(4)
        r_rstd = P.Rs(2)
        r_dst = P.R()
        if final_out is not None:
            yn = [sb(st, nc, "f_yn%d" % i, [128, 512], F32) for i in range(2)]
            ot = [sb(st, nc, "f_ot%d" % i, [128, 16, 4, 128], F32) for i in range(2)]
            ptf = [pst(st, nc, "f_ptf%d" % i, [128, 4, 128], F32) for i in range(4)]
            r_yn = P.Rs(2)
            r_ptf = P.Rs(4)
            r_ot = P.Rs(2)
        nq = 0

        def loads(g):
            for c in range(16):
                dma(P, "sp", xr[(g % 2) * 16 + c][:], xTc[:, c, g * G:(g + 1) * G], w=[r_xr[(g % 2) * 16 + c]])

        for g in range(NG):
            gs = slice(g * G, (g + 1) * G)
            pb = (g % 2) * 2 if NB == 2 else g % 4
            xo_ = (g % 2) * 16
            if g == 0:
                loads(0)
            if g + 1 < NG:
                loads(g + 1)
            for c in range(16):
                P.add("act", lambda e, c=c, xo_=xo_: e.activation(out=sq[c % 2][:], in_=xr[xo_ + c][:], func=AF.Square), r=[r_xr[xo_ + c]], w=[r_sq[c % 2]])
                for blk in range(NB):
                    P.add("pe", lambda e, c=c, blk=blk, pb=pb: e.matmul(ps[:, pb + blk, :], lhsT=C.onesb[:], rhs=sq[c % 2][:, blk * 512:(blk + 1) * 512],
                                                                      start=(c == 0), stop=(c == 15)), r=[r_sq[c % 2]], w=[r_ps[pb + blk]])
            rb = g % 2
            psv = ps[:, pb:pb + NB, :].rearrange("p b n -> p (b n)")
            P.add("act", lambda e, rb=rb, psv=psv: e.activation(out=rstd[rb][:], in_=psv, func=AF.Ln, scale=1.0 / D, bias=C.epsc[:, 0:1]),
                  r=[], w=[r_ps[pb + b_] for b_ in range(NB)] + [r_rstd[rb]])
            P.add("act", lambda e, rb=rb: e.activation(out=rstd[rb][:], in_=rstd[rb][:], func=AF.Exp, scale=-0.5), r=[], w=[r_rstd[rb]])
            if final_out is None:
                for c in range(16):
                    P.add("dve", lambda e, c=c, rb=rb, gs=gs, xo_=xo_: e.scalar_tensor_tensor(out=dstT[:, c, gs], in0=xr[xo_ + c][:], scalar=gT_col[:, c:c + 1],
                                                                                  in1=rstd[rb][:], op0=ALU.mult, op1=ALU.mult),
                          r=[r_xr[xo_ + c], r_rstd[rb]], w=[r_dst])
            else:
                ob = g % 2
                for c in range(16):
                    y = yn[c % 2]
                    P.add("dve", lambda e, c=c, y=y, rb=rb, xo_=xo_: e.scalar_tensor_tensor(out=y[:], in0=xr[xo_ + c][:], scalar=gT_col[:, c:c + 1],
                                                                                in1=rstd[rb][:], op0=ALU.mult, op1=ALU.mult),
                          r=[r_xr[xo_ + c], r_rstd[rb]], w=[r_yn[c % 2]])
                    s_ = nq % 4
                    nq += 1
                    for tt in range(4):
                        P.add("pe", lambda e, y=y, tt=tt, s_=s_: e.transpose(out=ptf[s_][:, tt, :], in_=y[:, tt * 128:(tt + 1) * 128], identity=C.identf[:]),
                              r=[r_yn[c % 2]], w=[r_ptf[s_]])
                    if c % 2 == 0:
                        P.add("act", lambda e, s_=s_, c=c, ob=ob: e.copy(out=ot[ob][:, c, :, :], in_=ptf[s_][:]), r=[], w=[r_ptf[s_], r_ot[ob]])
                    else:
                        P.add("dve", lambda e, s_=s_, c=c, ob=ob: e.tensor_copy(out=ot[ob][:, c, :, :], in_=ptf[s_][:]), r=[], w=[r_ptf[s_], r_ot[ob]])
                for tt in range(4):
                    t0 = g * 512 + tt * 128
                    dma(P, "sp", final_out[t0:t0 + 128, :].rearrange("p (c j) -> p c j", j=128), ot[ob][:, :, tt, :], r=[r_ot[ob]])
        P.end()


def phase_ffn(C, P, xnT, w_in, w_out, xT_dram):
    nc = C.nc
    xTc = chunked(xT_dram)
    PARTS = ((0, 22), (22, 22))
    JP = 22
    with contextlib.ExitStack() as st:
        gT = sb(st, nc, "m_gT", [128, JP, S], BF16)
        NWB = 2
        w1 = [sb(st, nc, "m_w1_%d" % i, [128, 16, 2, 128], BF16) for i in range(NWB)]
        w2 = [sb(st, nc, "m_w2_%d" % i, [128, JP, 128], BF16) for i in range(2)]
        sg = [sb(st, nc, "m_sg%d" % i, [128, 512], F32) for i in range(2)]
        xr = [sb(st, nc, "m_xr%d" % i, [128, S], F32) for i in range(2)]
        ps = pst(st, nc, "m_ps", [128, 8, 512], F32)
        r_ps = P.Rs(8)
        r_w1 = P.Rs(NWB)
        r_w2 = P.Rs(2)
        r_sg = P.Rs(2)
        r_xr = P.Rs(2)
        r_xn = P.R()
        r_xT = P.Rs(16)
        r_g = [[P.R() for _ in range(4)] for _ in range(JP)]
        nsg = 0
        nw1 = 0
        nw2 = 0
        nxr = 0
        unit = 0
        for (j0, njp) in PARTS:
            for jl in range(njp):
                j = j0 + jl
                wb = nw1 % NWB
                nw1 += 1
                dma(P, "pool", w1[wb][:, :, 0, :], chunked(w_in[:, j * 128:(j + 1) * 128]), w=[r_w1[wb]])
                dma(P, "pool", w1[wb][:, :, 1, :], chunked(w_in[:, DFF + j * 128:DFF + (j + 1) * 128]), w=[r_w1[wb]])
                for h in range(2):
                    bs = (unit % 2) * 4
                    unit += 1
                    for ab in range(2):
                        for blk in range(2):
                            bank = bs + ab * 2 + blk
                            tb = h * 2 + blk
                            for k in range(16):
                                P.add("pe", lambda e, bank=bank, wb=wb, ab=ab, k=k, tb=tb: e.matmul(
                                    ps[:, bank, :], lhsT=w1[wb][:, k, ab, :], rhs=xnT[:, k, tb * 512:(tb + 1) * 512],
                                    start=(k == 0), stop=(k == 15)), r=[r_w1[wb], r_xn], w=[r_ps[bank]])
                    for blk in range(2):
                        tb = h * 2 + blk
                        si = nsg % 2
                        nsg += 1
                        P.add("act", lambda e, si=si, bank=bs + blk: e.activation(out=sg[si][:], in_=ps[:, bank, :], func=AF.Silu),
                              r=[], w=[r_ps[bs + blk], r_sg[si]])
                        P.add("dve", lambda e, si=si, bank=bs + 2 + blk, jl=jl, tb=tb: e.tensor_tensor(
                            out=gT[:, jl, tb * 512:(tb + 1) * 512], in0=ps[:, bank, :], in1=sg[si][:], op=ALU.mult),
                            r=[r_sg[si]], w=[r_ps[bs + 2 + blk], r_g[jl][tb]])
            for c in range(16):
                wb = nw2 % 2
                nw2 += 1
                dma(P, "pool", w2[wb][:, 0:njp, :], chunked(w_out[j0 * 128:(j0 + njp) * 128, c * 128:(c + 1) * 128]), w=[r_w2[wb]])
                bs = (unit % 2) * 4
                unit += 1
                xb = nxr % 2
                nxr += 1
                dma(P, "sp", xr[xb][:], xTc[:, c, :], r=[r_xT[c]], w=[r_xr[xb]])
                for tb in range(4):
                    for jl in range(njp):
                        P.add("pe", lambda e, bank=bs + tb, wb=wb, jl=jl, tb=tb, njp=njp: e.matmul(
                            ps[:, bank, :], lhsT=w2[wb][:, jl, :], rhs=gT[:, jl, tb * 512:(tb + 1) * 512],
                            start=(jl == 0), stop=(jl == njp - 1)), r=[r_w2[wb], r_g[jl][tb]], w=[r_ps[bs + tb]])
                for tb in range(4):
                    P.add("dve", lambda e, bank=bs + tb, xb=xb, tb=tb: e.scalar_tensor_tensor(
                        out=xr[xb][:, tb * 512:(tb + 1) * 512], in0=ps[:, bank, :], scalar=0.5,
                        in1=xr[xb][:, tb * 512:(tb + 1) * 512], op0=ALU.mult, op1=ALU.add),
                        r=[], w=[r_ps[bs + tb], r_xr[xb]])
                dma(P, "sp", xTc[:, c, :], xr[xb][:], r=[r_xr[xb]], w=[r_xT[c]])
        P.end()


def phase_memkv(C, P, memnT, w_kv, gk_col, kmT, vm):
    nc = C.nc
    with contextlib.ExitStack() as st:
        wk = sb(st, nc, "k_w", [128, 16, 1024], BF16)
        sq = sb(st, nc, "k_sq", [128, 256], BF16)
        rstd = sb(st, nc, "k_rstd", [128, 256], F32)
        ps = pst(st, nc, "k_ps", [128, 4, 512], F32)
        r_w = P.Rs(2)
        r_ps = P.Rs(4)
        r_sq, r_rstd, r_km, r_vm, r_mn = P.Rs(5)
        for hf in range(2):
            dma(P, "pool", wk[:, :, hf * 512:(hf + 1) * 512], chunked(w_kv[:, hf * 512:(hf + 1) * 512]), w=[r_w[hf]])
        for h in range(4):
            for k in range(16):
                P.add("pe", lambda e, h=h, k=k: e.matmul(ps[:, 0, 0:256], lhsT=wk[:, k, h * 128:(h + 1) * 128], rhs=memnT[:, k, :],
                                                         start=(k == 0), stop=(k == 15)), r=[r_w[0], r_mn], w=[r_ps[0]])
            P.add("act", lambda e: e.activation(out=sq[:], in_=ps[:, 0, 0:256], func=AF.Square), r=[], w=[r_ps[0], r_sq])
            P.add("pe", lambda e: e.matmul(ps[:, 1, 0:256], lhsT=C.onesb[:], rhs=sq[:], start=True, stop=True), r=[r_sq], w=[r_ps[1]])
            P.add("act", lambda e: e.activation(out=rstd[:], in_=ps[:, 1, 0:256], func=AF.Ln, scale=1.0 / 128, bias=C.epsc[:, 0:1]),
                  r=[], w=[r_ps[1], r_rstd])
            P.add("act", lambda e: e.activation(out=rstd[:], in_=rstd[:], func=AF.Exp, scale=-0.5), r=[], w=[r_rstd])
            P.add("dve", lambda e, h=h: e.scalar_tensor_tensor(out=kmT[:, h, :], in0=ps[:, 0, 0:256], scalar=gk_col[:, 0:1], in1=rstd[:],
                                                              op0=ALU.mult, op1=ALU.mult), r=[r_rstd], w=[r_ps[0], r_km])
        for mt in range(2):
            for k in range(16):
                P.add("pe", lambda e, mt=mt, k=k: e.matmul(ps[:, 2 + mt, :], lhsT=memnT[:, k, mt * 128:(mt + 1) * 128], rhs=wk[:, k, 512:1024],
                                                           start=(k == 0), stop=(k == 15)), r=[r_w[1], r_mn], w=[r_ps[2 + mt]])
            P.add("act", lambda e, mt=mt: e.copy(out=vm[:, mt, :], in_=ps[:, 2 + mt, :]), r=[], w=[r_ps[2 + mt], r_vm])
        P.end()


def phase_win(C, P, hT, w_in, T):
    nc = C.nc
    GROUPS = ((1, 2048), (4, 512), (16, 128))
    with contextlib.ExitStack() as st:
        NWB = 3
        wb = [sb(st, nc, "w_wb%d" % i, [128, 16, 256], BF16) for i in range(NWB)]
        cosT = sb(st, nc, "w_cos", [128, S], F32)
        sinT = sb(st, nc, "w_sin", [128, S], F32)
        Rm = sb(st, nc, "w_R", [128, 128], BF16)
        sq = [sb(st, nc, "w_sq%d" % i, [128, 1024], BF16) for i in range(2)]
        qb = [sb(st, nc, "w_qb%d" % i, [128, 1024], BF16) for i in range(2)]
        rstd = [sb(st, nc, "w_rstd%d" % i, [128, 1024], F32) for i in range(2)]
        u1 = [sb(st, nc, "w_u1%d" % i, [128, 1024], F32) for i in range(2)]
        u2 = [sb(st, nc, "w_u2%d" % i, [128, 1024], F32) for i in range(2)]
        qo = [sb(st, nc, "w_qo%d" % i, [128, 1024], BF16) for i in range(2)]
        ub = [sb(st, nc, "w_ub%d" % i, [128, S + 2], F32) for i in range(2)]
        uc = [sb(st, nc, "w_uc%d" % i, [128, S], F32) for i in range(2)]
        ut = [sb(st, nc, "w_ut%d" % i, [128, 16, 128], BF16) for i in range(2)]
        gs = [sb(st, nc, "w_gs%d" % i, [128, 1024], F32) for i in range(3)]
        vs = [sb(st, nc, "w_vs%d" % i, [128, 256], BF16) for i in range(2)]
        psg = pst(st, nc, "w_psg", [128, 4, 512], F32)
        pss = pst(st, nc, "w_pss", [128, 2, 512], F32)
        psr = pst(st, nc, "w_psr", [128, 2, 512], F32)
        r_wb = P.Rs(NWB)
        r_psg = P.Rs(4)
        r_pss = P.Rs(2)
        r_psr = P.Rs(2)
        r_sq, r_qb, r_rstd, r_u1, r_u2, r_qo, r_ub, r_uc, r_ut, r_vs = (P.Rs(2) for _ in range(10))
        r_gs = P.Rs(3)
        r_c = P.Rs(3)
        r_h = P.R()
        dma(P, "sp", cosT[:], T.cosT, w=[r_c[0]])
        dma(P, "sp", sinT[:], T.sinT, w=[r_c[1]])
        dma(P, "sp", Rm[:], T.Rm, w=[r_c[2]])
        for i in range(2):
            P.add("pool", lambda e, i=i: e.memset(ub[i][:], 0.0), w=[r_ub[i]])
        cnt = {"u": 0, "qk": 0, "hy": 0, "g": 0, "v": 0, "tr": 0}

        def gemm_unit(wbi, jj, h):
            s_ = cnt["u"] % 2
            cnt["u"] += 1
            for blk in range(2):
                bank = s_ * 2 + blk
                tb = h * 2 + blk
                for k in range(16):
                    P.add("pe", lambda e, bank=bank, k=k, tb=tb: e.matmul(psg[:, bank, :], lhsT=wb[wbi][:, k, jj * 128:(jj + 1) * 128],
                                                                          rhs=hT[:, k, tb * 512:(tb + 1) * 512], start=(k == 0), stop=(k == 15)),
                          r=[r_wb[wbi], r_h], w=[r_psg[bank]])
            return s_

        def psv(s_):
            return psg[:, s_ * 2:s_ * 2 + 2, :].rearrange("p b n -> p (b n)")

        def norm_stats_b(b):
            for blk in range(2):
                P.add("pe", lambda e, blk=blk: e.matmul(pss[:, blk, :], lhsT=C.onesb[:], rhs=sq[b][:, blk * 512:(blk + 1) * 512], start=True, stop=True),
                      r=[r_sq[b]], w=[r_pss[blk]])
            P.add("act", lambda e: e.activation(out=rstd[b][:], in_=pss[:].rearrange("p b n -> p (b n)"), func=AF.Ln, scale=1.0 / 128,
                                                bias=C.epsc[:, 0:1]), r=[], w=[r_pss[0], r_pss[1], r_rstd[b]])
            P.add("act", lambda e: e.activation(out=rstd[b][:], in_=rstd[b][:], func=AF.Exp, scale=-0.5), r=[], w=[r_rstd[b]])

        pending = []

        def flush():
            for f in pending:
                f()
            del pending[:]

        dma(P, "sp", T.hT_d, hT[:], r=[r_h])
        for npi, pi in enumerate((0, 9, 1, 10, 11, 2, 12, 3, 13, 14, 4, 15, 5, 16, 17, 18, 19, 6, 7, 8)):
            wbi = npi % NWB
            dma(P, "pool", wb[wbi][:], chunked(w_in[:, pi * 256:(pi + 1) * 256]), w=[r_wb[wbi]])
            if 6 <= pi < 9:
                flush()
                g = pi - 6
                d, L = GROUPS[g]
                nt = L // 128
                for i in range(16):
                    r_, mt = i // nt, i % nt
                    start = r_ + d * mt * 128
                    bank = i % 2
                    for k in range(16):
                        P.add("pe", lambda e, k=k, start=start, d=d, bank=bank, wbi=wbi: e.matmul(
                            pss[:, bank, 0:256], lhsT=hT[:, k, start:start + d * 127 + 1:d], rhs=wb[wbi][:, k, :],
                            start=(k == 0), stop=(k == 15)), r=[r_wb[wbi], r_h], w=[r_pss[bank]])
                    vb = cnt["v"] % 2
                    cnt["v"] += 1
                    P.add("act", lambda e, vb=vb, bank=bank: e.copy(out=vs[vb][:], in_=pss[:, bank, 0:256]), r=[], w=[r_pss[bank], r_vs[vb]])
                    dma(P, "sp", T.va_d[g, i], vs[vb][:], r=[r_vs[vb]])
                continue
            for jj in range(2):
                j = pi * 2 + jj
                for h in range(2):
                    s_ = gemm_unit(wbi, jj, h)
                    flush()
                    banks = [r_psg[s_ * 2], r_psg[s_ * 2 + 1]]
                    hs = slice(h * 1024, (h + 1) * 1024)
                    if j < 12:
                        b = cnt["qk"] % 2
                        cnt["qk"] += 1
                        gcol = T.gq_col if j < 6 else T.gk_col
                        P.add("act", lambda e, s_=s_, b=b: e.activation(out=sq[b][:], in_=psv(s_), func=AF.Square), r=[], w=banks + [r_sq[b]])
                        P.add("act", lambda e, s_=s_, b=b, gcol=gcol: e.activation(out=qb[b][:], in_=psv(s_), func=AF.Copy, scale=gcol[:, 0:1]),
                              r=[], w=banks + [r_qb[b]])
                        P.add("dve", lambda e, s_=s_, b=b, gcol=gcol, hs=hs: e.scalar_tensor_tensor(out=u1[b][:], in0=psv(s_), scalar=gcol[:, 0:1], in1=cosT[:, hs],
                                                                                               op0=ALU.mult, op1=ALU.mult), r=[r_c[0]], w=banks + [r_u1[b]])

                        def stage_b(b=b, hs=hs, j=j):
                            norm_stats_b(b)
                            for blk in range(2):
                                P.add("pe", lambda e, b=b, blk=blk: e.matmul(psr[:, blk, :], lhsT=Rm[:], rhs=qb[b][:, blk * 512:(blk + 1) * 512], start=True, stop=True),
                                      r=[r_qb[b], r_c[2]], w=[r_psr[blk]])
                            P.add("dve", lambda e, b=b, hs=hs: e.tensor_tensor(out=u2[b][:], in0=psr[:].rearrange("p b n -> p (b n)"), in1=sinT[:, hs], op=ALU.mult),
                                  r=[r_c[1]], w=[r_psr[0], r_psr[1], r_u2[b]])
                            P.add("dve", lambda e, b=b: e.tensor_tensor(out=u1[b][:], in0=u1[b][:], in1=u2[b][:], op=ALU.add), r=[r_u2[b]], w=[r_u1[b]])
                            P.add("dve", lambda e, b=b: e.tensor_tensor(out=qo[b][:], in0=u1[b][:], in1=rstd[b][:], op=ALU.mult), r=[r_u1[b], r_rstd[b]], w=[r_qo[b]])
                            dma(P, "sp", T.qk_d[j][:, hs], qo[b][:], r=[r_qo[b]])
                        pending.append(stage_b)
                    elif 18 <= j < 36:
                        jh = j - 18
                        b = cnt["hy"] % 2
                        if h == 1:
                            cnt["hy"] += 1
                        P.add("act", lambda e, s_=s_, b=b, h=h: e.copy(out=ub[b][:, 1 + h * 1024:1 + (h + 1) * 1024], in_=psv(s_)), r=[], w=banks + [r_ub[b]])
                        if h == 1:
                            def stage_b(b=b, jh=jh):
                                P.add("dve", lambda e, b=b, jh=jh: e.tensor_scalar(out=uc[b][:], in0=ub[b][:, 1:S + 1], scalar1=T.cw[:, jh, 1:2], scalar2=T.cb[:, jh:jh + 1],
                                                                                  op0=ALU.mult, op1=ALU.add), r=[r_ub[b]], w=[r_uc[b]])
                                P.add("dve", lambda e, b=b, jh=jh: e.scalar_tensor_tensor(out=uc[b][:], in0=ub[b][:, 0:S], scalar=T.cw[:, jh, 0:1], in1=uc[b][:],
                                                                                         op0=ALU.mult, op1=ALU.add), r=[r_ub[b]], w=[r_uc[b]])
                                P.add("dve", lambda e, b=b, jh=jh: e.scalar_tensor_tensor(out=uc[b][:], in0=ub[b][:, 2:S + 2], scalar=T.cw[:, jh, 2:3], in1=uc[b][:],
                                                                                         op0=ALU.mult, op1=ALU.add), r=[r_ub[b]], w=[r_uc[b]])
                                dma(P, "sp", T.hyx_d[jh * 128:(jh + 1) * 128, :], uc[b][:], r=[r_uc[b]])
                                if jh < 6:
                                    for t4 in range(4):
                                        bank = cnt["tr"] % 2
                                        cnt["tr"] += 1
                                        for tt in range(4):
                                            tc_ = t4 * 4 + tt
                                            P.add("pe", lambda e, b=b, tc_=tc_, bank=bank, tt=tt: e.transpose(out=psr[:, bank, tt * 128:(tt + 1) * 128], in_=uc[b][:, tc_ * 128:(tc_ + 1) * 128],
                                                                                                             identity=C.identf[:]), r=[r_uc[b]], w=[r_psr[bank]])
                                        P.add("act", lambda e, b=b, bank=bank, t4=t4: e.copy(out=ut[b][:, t4 * 4:(t4 + 1) * 4, :], in_=psr[:, bank, :].rearrange("p (a c) -> p a c", c=128)),
                                              r=[], w=[r_psr[bank], r_ut[b]])
                                    dma(P, "sp", T.utm_d.rearrange("(a p) c -> p a c", p=128)[:, :, jh * 128:(jh + 1) * 128], ut[b][:], r=[r_ut[b]])
                            pending.append(stage_b)
                    elif 36 <= j < 40:
                        b = cnt["qk"] % 2
                        cnt["qk"] += 1
                        P.add("act", lambda e, s_=s_, b=b: e.activation(out=sq[b][:], in_=psv(s_), func=AF.Square), r=[], w=banks + [r_sq[b]])
                        P.add("act", lambda e, s_=s_, b=b: e.activation(out=u1[b][:], in_=psv(s_), func=AF.Copy, scale=T.mq_col[:, 0:1]), r=[], w=banks + [r_u1[b]])

                        def stage_b(b=b, hs=hs, j=j):
                            norm_stats_b(b)
                            P.add("dve", lambda e, b=b: e.tensor_tensor(out=qo[b][:], in0=u1[b][:], in1=rstd[b][:], op=ALU.mult), r=[r_u1[b], r_rstd[b]], w=[r_qo[b]])
                            dma(P, "sp", T.qc_d[j - 36][:, hs], qo[b][:], r=[r_qo[b]])
                        pending.append(stage_b)
        flush()
        P.end()


def phase_attn_a(C, P, T):
    nc = C.nc
    GROUPS = ((1, 2048), (4, 512), (16, 128))
    SCALE = 1.0 / math.sqrt(128.0)
    with contextlib.ExitStack() as st:
        mask = sb(st, nc, "a_mask", [128, 256], BF16)
        qT = [sb(st, nc, "a_q%d" % i, [128, S], BF16) for i in range(2)]
        kT = [sb(st, nc, "a_k%d" % i, [128, S], BF16) for i in range(2)]
        vv = [sb(st, nc, "a_v%d" % i, [128, 16, 256], BF16) for i in range(2)]
        accn = sb(st, nc, "a_accn", [128, S], F32)
        accd = sb(st, nc, "a_accd", [128, S], F32)
        E = [sb(st, nc, "a_E%d" % i, [128, 256], BF16) for i in range(3)]
        Em = [sb(st, nc, "a_Em%d" % i, [128, 256], BF16) for i in range(3)]
        yo = sb(st, nc, "a_yo", [128, S], BF16)
        ps = pst(st, nc, "a_ps", [128, 8, 512], F32)
        r_ps = P.Rs(8)
        r_q, r_k, r_v = P.Rs(2), P.Rs(2), P.Rs(2)
        r_E, r_Em = P.Rs(3), P.Rs(3)
        r_mask, r_accn, r_accd, r_yo = P.Rs(4)
        dma(P, "sp", mask[:], T.mask, w=[r_mask])
        nld = 0
        for hh in range(2):
            P.add("pool", lambda e: e.memset(accn[:], 0.0), w=[r_accn])
            P.add("pool", lambda e: e.memset(accd[:], 0.0), w=[r_accd])
            for g in range(3):
                d, L = GROUPS[g]
                nt = L // 128
                head = g * 2 + hh
                lb = nld % 2
                nld += 1
                dma(P, "sp", qT[lb][:], T.qk_d[head], w=[r_q[lb]])
                dma(P, "sp", kT[lb][:], T.qk_d[6 + head], w=[r_k[lb]])
                dma(P, "sp", vv[lb][:], T.va_d[g].rearrange("i p c -> p i c"), w=[r_v[lb]])
                its = [(r_, kt) for r_ in range(d) for kt in range(nt)]
                N = len(its)
                LAG = 2

                def geom(it):
                    r_, kt = its[it]
                    qlo = max(0, 128 * kt - 64)
                    qhi = min(L, 128 * kt + 192)
                    nq = qhi - qlo
                    mo = qlo - (128 * kt - 64)
                    ks = slice(r_ + d * 128 * kt, r_ + d * (128 * kt + 127) + 1, d)
                    qs = slice(r_ + d * qlo, r_ + d * (qhi - 1) + 1, d)
                    return r_ * nt + kt, nq, mo, ks, qs

                for it in range(N + LAG):
                    if it < N:
                        i, nq, mo, ks, qs = geom(it)
                        b3 = it % 3
                        P.add("pe", lambda e, b3=b3, nq=nq, ks=ks, qs=qs, lb=lb: e.matmul(ps[:, b3, 0:nq], lhsT=kT[lb][:, ks], rhs=qT[lb][:, qs], start=True, stop=True),
                              r=[r_q[lb], r_k[lb]], w=[r_ps[b3]])
                        P.add("act", lambda e, b3=b3, nq=nq: e.activation(out=E[b3][:, 0:nq], in_=ps[:, b3, 0:nq], func=AF.Exp, scale=SCALE), r=[], w=[r_ps[b3], r_E[b3]])
                        P.add("pool", lambda e, b3=b3, nq=nq, mo=mo: e.tensor_tensor(out=Em[b3][:, 0:nq], in0=E[b3][:, 0:nq], in1=mask[:, mo:mo + nq], op=ALU.mult),
                              r=[r_E[b3], r_mask], w=[r_Em[b3]])
                    jt = it - LAG
                    if jt >= 0:
                        i, nq, mo, ks, qs = geom(jt)
                        b3 = jt % 3
                        b2 = jt % 2
                        P.add("pe", lambda e, b3=b3, b2=b2, nq=nq, i=i, lb=lb, hh=hh: e.matmul(ps[:, 3 + b2, 0:nq], lhsT=vv[lb][:, i, hh * 128:(hh + 1) * 128], rhs=Em[b3][:, 0:nq],
                                                                                           start=True, stop=True), r=[r_v[lb], r_Em[b3]], w=[r_ps[3 + b2]])
                        P.add("pe", lambda e, b3=b3, b2=b2, nq=nq: e.matmul(ps[:, 5 + b2, 0:nq], lhsT=C.onesb[:], rhs=Em[b3][:, 0:nq], start=True, stop=True),
                              r=[r_Em[b3]], w=[r_ps[5 + b2]])
                        P.add("dve", lambda e, b2=b2, nq=nq, qs=qs: e.tensor_tensor(out=accn[:, qs], in0=accn[:, qs], in1=ps[:, 3 + b2, 0:nq], op=ALU.add), r=[], w=[r_ps[3 + b2], r_accn])
                        P.add("dve", lambda e, b2=b2, nq=nq, qs=qs: e.tensor_tensor(out=accd[:, qs], in0=accd[:, qs], in1=ps[:, 5 + b2, 0:nq], op=ALU.add), r=[], w=[r_ps[5 + b2], r_accd])
            P.add("dve", lambda e: e.reciprocal(out=accd[:], in_=accd[:]), r=[], w=[r_accd])
            P.add("dve", lambda e: e.tensor_tensor(out=yo[:], in0=accn[:], in1=accd[:], op=ALU.mult), r=[r_accn, r_accd], w=[r_yo])
            dma(P, "sp", T.ymix_d[hh], yo[:], r=[r_yo])
        P.end()


def phase_attn_c(C, P, T, kmT, vm):
    nc = C.nc
    SCALE = 1.0 / math.sqrt(128.0)
    with contextlib.ExitStack() as st:
        qT = [sb(st, nc, "c_q%d" % i, [128, S], BF16) for i in range(2)]
        E = [sb(st, nc, "c_E%d" % i, [128, 2, 512], BF16) for i in range(2)]
        rden = [sb(st, nc, "c_rd%d" % i, [128, 512], F32) for i in range(2)]
        yo = [sb(st, nc, "c_yo%d" % i, [128, S], BF16) for i in range(2)]
        ps = pst(st, nc, "c_ps", [128, 8, 512], F32)
        r_ps = P.Rs(8)
        r_q, r_E, r_rd, r_yo = P.Rs(2), P.Rs(2), P.Rs(2), P.Rs(2)
        r_km, r_vm = P.Rs(2)
        its = [(h, tb) for h in range(4) for tb in range(4)]

        def front(it):
            h, tb = its[it]
            qb = h % 2
            b = it % 2
            ts = slice(tb * 512, (tb + 1) * 512)
            if tb == 0:
                dma(P, "sp", qT[qb][:], T.qc_d[h], w=[r_q[qb]])
            for mt in range(2):
                P.add("pe", lambda e, b=b, mt=mt, h=h, qb=qb, ts=ts: e.matmul(ps[:, b * 2 + mt, :], lhsT=kmT[:, h, mt * 128:(mt + 1) * 128], rhs=qT[qb][:, ts], start=True, stop=True),
                      r=[r_km, r_q[qb]], w=[r_ps[b * 2 + mt]])
                P.add("act", lambda e, b=b, mt=mt: e.activation(out=E[b][:, mt, :], in_=ps[:, b * 2 + mt, :], func=AF.Exp, scale=SCALE), r=[], w=[r_ps[b * 2 + mt], r_E[b]])

        def back(it):
            h, tb = its[it]
            qb = h % 2
            b = it % 2
            ts = slice(tb * 512, (tb + 1) * 512)
            for mt in range(2):
                P.add("pe", lambda e, b=b, mt=mt, h=h: e.matmul(ps[:, 4 + b, :], lhsT=vm[:, mt, h * 128:(h + 1) * 128], rhs=E[b][:, mt, :], start=(mt == 0), stop=(mt == 1)),
                      r=[r_vm, r_E[b]], w=[r_ps[4 + b]])
            for mt in range(2):
                P.add("pe", lambda e, b=b, mt=mt: e.matmul(ps[:, 6 + b, :], lhsT=C.onesb[:], rhs=E[b][:, mt, :], start=(mt == 0), stop=(mt == 1)), r=[r_E[b]], w=[r_ps[6 + b]])
            P.add("act", lambda e, b=b: e.activation(out=rden[b][:], in_=ps[:, 6 + b, :], func=AF.Ln), r=[], w=[r_ps[6 + b], r_rd[b]])
            P.add("act", lambda e, b=b: e.activation(out=rden[b][:], in_=rden[b][:], func=AF.Exp, scale=-1.0), r=[], w=[r_rd[b]])
            P.add("dve", lambda e, b=b, qb=qb, ts=ts: e.tensor_tensor(out=yo[qb][:, ts], in0=ps[:, 4 + b, :], in1=rden[b][:], op=ALU.mult), r=[r_rd[b]], w=[r_ps[4 + b], r_yo[qb]])
            if tb == 3:
                dma(P, "sp", T.ymix_d[8 + h], yo[qb][:], r=[r_yo[qb]])

        front(0)
        for it in range(len(its)):
            if it + 1 < len(its):
                front(it + 1)
            back(it)
        P.end()


def phase_hy_filter_mlp(C, P, T, hh3, hh3b):
    nc = C.nc
    MAGIC = 12582912.0
    with contextlib.ExitStack() as st:
        zT = sb(st, nc, "h_zT", [64, S], F32)
        w1 = sb(st, nc, "h_w1", [64, 64], F32)
        w2 = sb(st, nc, "h_w2", [64, 64], F32)
        w3 = sb(st, nc, "h_w3", [64, 64], F32)
        hb = [sb(st, nc, "h_hb%d" % i, [64, S], F32) for i in range(2)]
        v = sb(st, nc, "h_v", [64, S], F32)
        t = sb(st, nc, "h_t", [64, S], F32)
        ps = pst(st, nc, "h_ps", [128, 4, 512], F32)
        r_ps = P.Rs(4)
        r_z, r_w1, r_w2, r_w3, r_v, r_t = P.Rs(6)
        r_hb = P.Rs(2)
        r_h3 = P.R()
        dma(P, "sp", zT[0:33, :], T.zT, w=[r_z])
        dma(P, "sp", w1[0:33, :], T.f_w1, w=[r_w1])
        dma(P, "sp", w2[:], T.f_w2, w=[r_w2])
        dma(P, "sp", w3[:], T.f_w3, w=[r_w3])
        srcs = [(zT, 33, w1, r_z, r_w1), (hb[0], 64, w2, r_hb[0], r_w2), (hb[1], 64, w3, r_hb[1], r_w3)]
        dsts = [(hb[0], r_hb[0]), (hb[1], r_hb[1]), (hh3, r_h3)]
        for li in range(3):
            src, kk, ww, r_s, r_w = srcs[li]
            dst, r_d = dsts[li]
            for tb in range(4):
                P.add("pe", lambda e, tb=tb, src=src, kk=kk, ww=ww: e.matmul(ps[0:64, tb, :], lhsT=ww[0:kk, :], rhs=src[0:kk, tb * 512:(tb + 1) * 512], start=True, stop=True),
                      r=[r_s, r_w], w=[r_ps[tb]])
            P.add("dve", lambda e, li=li: e.tensor_scalar(out=v[:], in0=ps[0:64, :, :].rearrange("p b n -> p (b n)"), scalar1=T.f_b[0:64, li:li + 1], scalar2=T.f_freq[0:64, 0:1],
                                                          op0=ALU.add, op1=ALU.mult), r=[], w=r_ps + [r_v])
            P.add("dve", lambda e: e.tensor_scalar(out=t[:], in0=v[:], scalar1=1.0 / (2 * math.pi), scalar2=MAGIC, op0=ALU.mult, op1=ALU.add), r=[r_v], w=[r_t])
            P.add("dve", lambda e: e.tensor_scalar(out=t[:], in0=t[:], scalar1=-MAGIC, scalar2=-2 * math.pi, op0=ALU.add, op1=ALU.mult), r=[], w=[r_t])
            P.add("dve", lambda e: e.tensor_tensor(out=v[:], in0=v[:], in1=t[:], op=ALU.add), r=[r_t], w=[r_v])
            P.add("act", lambda e, dst=dst: e.activation(out=dst[0:64, :], in_=v[:], func=AF.Sin), r=[r_v], w=[r_d])
        P.add("act", lambda e: e.copy(out=hh3b[0:64, :], in_=hh3[0:64, :]), r=[r_h3], w=[P.R()])
        P.end()


def phase_hy_filter(C, P, T, hh3, o, ksum, kdif):
    nc = C.nc
    with contextlib.ExitStack() as st:
        w4 = sb(st, nc, "g_w4", [64, 1536], BF16)
        dec = [sb(st, nc, "g_dec%d" % i, [128, HYW], F32) for i in range(2)]
        hbs = [sb(st, nc, "g_hbs%d" % i, [128, HYW], F32) for i in range(2)]
        sm = [sb(st, nc, "g_sm%d" % i, [128, HYW], F32) for i in range(2)]
        df = [sb(st, nc, "g_df%d" % i, [128, HYW], F32) for i in range(2)]
        ps = pst(st, nc, "g_ps", [128, 8, 512], F32)
        r_ps = P.Rs(8)
        r_w4, r_h3, r_ks, r_kd = P.Rs(4)
        r_dec, r_hbs, r_sm, r_df = P.Rs(2), P.Rs(2), P.Rs(2), P.Rs(2)
        dma(P, "pool", w4[:], T.f_w4[:, o * 1536:(o + 1) * 1536], w=[r_w4])
        for tt in range(16):
            b = tt % 2
            dma(P, "sp", dec[b][:], T.decay[tt * 128:(tt + 1) * 128, :], w=[r_dec[b]])
            for di in range(2):
                for (c0, cn, bk) in ((0, 512, 0), (512, 256, 1)):
                    bank = b * 4 + di * 2 + bk
                    P.add("pe", lambda e, tt=tt, di=di, c0=c0, cn=cn, bank=bank: e.matmul(ps[:, bank, 0:cn], lhsT=hh3[0:64, tt * 128:(tt + 1) * 128],
                                                                                        rhs=w4[:, di * HYW + c0:di * HYW + c0 + cn], start=True, stop=True),
                          r=[r_h3, r_w4], w=[r_ps[bank]])
            pf = ps[:, b * 4:b * 4 + 2, :].rearrange("p b n -> p (b n)")[:, 0:HYW]
            pb_ = ps[:, b * 4 + 2:b * 4 + 4, :].rearrange("p b n -> p (b n)")[:, 0:HYW]
            bf_ = [r_ps[b * 4], r_ps[b * 4 + 1]]
            bb_ = [r_ps[b * 4 + 2], r_ps[b * 4 + 3]]
            P.add("act", lambda e, b=b, pb_=pb_: e.copy(out=hbs[b][:], in_=pb_), r=[], w=bb_ + [r_hbs[b]])
            P.add("dve", lambda e, b=b, pf=pf: e.tensor_tensor(out=sm[b][:], in0=pf, in1=hbs[b][:], op=ALU.add), r=[r_hbs[b]], w=bf_ + [r_sm[b]])
            P.add("dve", lambda e, b=b, pf=pf: e.tensor_tensor(out=df[b][:], in0=hbs[b][:], in1=pf, op=ALU.subtract), r=[r_hbs[b]], w=bf_ + [r_df[b]])
            P.add("dve", lambda e, b=b, tt=tt: e.tensor_tensor(out=ksum[:, tt, :], in0=sm[b][:], in1=dec[b][:], op=ALU.mult), r=[r_sm[b], r_dec[b]], w=[r_ks])
            P.add("dve", lambda e, b=b, tt=tt: e.tensor_tensor(out=kdif[:, tt, :], in0=df[b][:], in1=dec[b][:], op=ALU.mult), r=[r_df[b], r_dec[b]], w=[r_kd])
        P.end()


def phase_hy_fwd(C, P, T, utm_src, ksum, kdif, Yre, Yim):
    nc = C.nc
    with contextlib.ExitStack() as st:
        utm = sb(st, nc, "d_utm", [128, 16, HYW], BF16)
        cs = [sb(st, nc, "d_cs%d" % i, [128, 2, 16, 128], BF16) for i in range(3)]
        kr = [sb(st, nc, "d_kr%d" % i, [128, HYW], F32) for i in range(2)]
        ki = [sb(st, nc, "d_ki%d" % i, [128, HYW], F32) for i in range(2)]
        t1 = [sb(st, nc, "d_t%d" % i, [128, HYW], F32) for i in range(4)]
        ps = pst(st, nc, "d_ps", [128, 8, 512], F32)
        r_ps = P.Rs(8)
        r_utm, r_ks, r_kd, r_yr, r_yi = P.Rs(5)
        r_cs = P.Rs(3)
        r_kr, r_ki = P.Rs(2), P.Rs(2)
        r_t = P.Rs(4)
        dma(P, "sp", utm[:], utm_src.rearrange("(a p) c -> p a c", p=128), w=[r_utm])

        def pv(b0):
            return ps[:, b0:b0 + 2, :].rearrange("p b n -> p (b n)")[:, 0:HYW]

        for fc in range(16):
            cb = fc % 3
            dma(P, "sp", cs[cb][:, 0], T.cts[fc].rearrange("p (a f) -> p a f", f=128), w=[r_cs[cb]])
            dma(P, "sp", cs[cb][:, 1], T.sts[fc].rearrange("p (a f) -> p a f", f=128), w=[r_cs[cb]])
            jobs = ((0, 0, ksum, r_ks), (2, 1, kdif, r_kd), (4, 0, utm, r_utm), (6, 1, utm, r_utm))
            for (b0, ci, src, r_s) in jobs:
                for (c0, cn, bk) in ((0, 512, 0), (512, 256, 1)):
                    for tc_ in range(16):
                        P.add("pe", lambda e, b0=b0, bk=bk, cn=cn, c0=c0, ci=ci, tc_=tc_, src=src, cb=cb: e.matmul(
                            ps[:, b0 + bk, 0:cn], lhsT=cs[cb][:, ci, tc_, :], rhs=src[:, tc_, c0:c0 + cn], start=(tc_ == 0), stop=(tc_ == 15)),
                            r=[r_cs[cb], r_s], w=[r_ps[b0 + bk]])
            kb = fc % 2
            P.add("act", lambda e, kb=kb: e.copy(out=kr[kb][:], in_=pv(0)), r=[], w=[r_ps[0], r_ps[1], r_kr[kb]])
            P.add("act", lambda e, kb=kb: e.copy(out=ki[kb][:], in_=pv(2)), r=[], w=[r_ps[2], r_ps[3], r_ki[kb]])
            bu = [r_ps[4], r_ps[5]]
            bs_ = [r_ps[6], r_ps[7]]
            P.add("dve", lambda e, kb=kb: e.tensor_tensor(out=t1[0][:], in0=pv(4), in1=kr[kb][:], op=ALU.mult), r=[r_kr[kb]], w=bu + [r_t[0]])
            P.add("dve", lambda e, kb=kb: e.tensor_tensor(out=t1[1][:], in0=pv(6), in1=ki[kb][:], op=ALU.mult), r=[r_ki[kb]], w=bs_ + [r_t[1]])
            P.add("dve", lambda e, kb=kb: e.tensor_tensor(out=t1[2][:], in0=pv(6), in1=kr[kb][:], op=ALU.mult), r=[r_kr[kb]], w=bs_ + [r_t[2]])
            P.add("dve", lambda e, kb=kb: e.tensor_tensor(out=t1[3][:], in0=pv(4), in1=ki[kb][:], op=ALU.mult), r=[r_ki[kb]], w=bu + [r_t[3]])
            P.add("dve", lambda e, fc=fc: e.tensor_tensor(out=Yre[:, fc, :], in0=t1[0][:], in1=t1[1][:], op=ALU.add), r=[r_t[0], r_t[1]], w=[r_yr])
            P.add("dve", lambda e, fc=fc: e.tensor_tensor(out=Yim[:, fc, :], in0=t1[2][:], in1=t1[3][:], op=ALU.subtract), r=[r_t[2], r_t[3]], w=[r_yi])
        P.end()


def phase_hy_inv(C, P, T, o, Yre, Yim):
    nc = C.nc
    with contextlib.ExitStack() as st:
        cs = [sb(st, nc, "i_cs%d" % i, [128, 2, 16, 512], BF16) for i in range(2)]
        zp = [sb(st, nc, "i_zp%d" % i, [128, 512], F32) for i in range(2)]
        xo = [sb(st, nc, "i_xo%d" % i, [128, 512], F32) for i in range(2)]
        zz = [sb(st, nc, "i_zz%d" % i, [128, 512], F32) for i in range(2)]
        zb = [sb(st, nc, "i_zb%d" % i, [128, 512], BF16) for i in range(2)]
        ut = [sb(st, nc, "i_ut%d" % i, [128, 4, 128], BF16) for i in range(2)]
        ps = pst(st, nc, "i_ps", [128, 4, 512], F32)
        r_ps = P.Rs(4)
        r_cs, r_zp, r_xo, r_zz, r_zb, r_ut = (P.Rs(2) for _ in range(6))
        r_yr, r_yi = P.Rs(2)
        zprev = T.hyx_d if o == 0 else T.z1_d
        units = [(tb, cc) for tb in range(4) for cc in range(6)]

        def loads(n):
            tb, cc = units[n]
            b = n % 2
            ts = slice(tb * 512, (tb + 1) * 512)
            dma(P, "sp", zp[b][:], zprev[cc * 128:(cc + 1) * 128, ts], w=[r_zp[b]])
            dma(P, "sp", xo[b][:], T.hyx_d[(o + 1) * HYW + cc * 128:(o + 1) * HYW + (cc + 1) * 128, ts], w=[r_xo[b]])

        pending = []
        for tb in range(2):
            cb = tb % 2
            dma(P, "sp", cs[cb][:, 0], T.cinv[tb].rearrange("p (a t) -> p a t", t=512), w=[r_cs[cb]])
            dma(P, "sp", cs[cb][:, 1], T.sinv[tb].rearrange("p (a t) -> p a t", t=512), w=[r_cs[cb]])
        loads(0)
        for n, (tb, cc) in enumerate(units):
            b = n % 2
            cb = tb % 2
            ts = slice(tb * 512, (tb + 1) * 512)
            if cc == 0 and 1 <= tb <= 2:
                dma(P, "sp", cs[1 - cb][:, 0], T.cinv[tb + 1].rearrange("p (a t) -> p a t", t=512), w=[r_cs[1 - cb]])
                dma(P, "sp", cs[1 - cb][:, 1], T.sinv[tb + 1].rearrange("p (a t) -> p a t", t=512), w=[r_cs[1 - cb]])
            for fc in range(16):
                P.add("pe", lambda e, b=b, fc=fc, cc=cc, cb=cb: e.matmul(ps[:, b, :], lhsT=Yre[:, fc, cc * 128:(cc + 1) * 128], rhs=cs[cb][:, 0, fc, :], start=(fc == 0), stop=False),
                      r=[r_yr, r_cs[cb]], w=[r_ps[b]])
                P.add("pe", lambda e, b=b, fc=fc, cc=cc, cb=cb: e.matmul(ps[:, b, :], lhsT=Yim[:, fc, cc * 128:(cc + 1) * 128], rhs=cs[cb][:, 1, fc, :], start=False, stop=(fc == 15)),
                      r=[r_yi, r_cs[cb]], w=[r_ps[b]])
            for f in pending:
                f()
            del pending[:]
            P.add("dve", lambda e, b=b, cc=cc: e.tensor_scalar(out=zp[b][:], in0=zp[b][:], scalar1=T.hbias[:, o * 6 + cc:o * 6 + cc + 1], scalar2=None, op0=ALU.mult), r=[], w=[r_zp[b]])
            P.add("dve", lambda e, b=b: e.scalar_tensor_tensor(out=zz[b][:], in0=ps[:, b, :], scalar=2.0 / NFFT, in1=zp[b][:], op0=ALU.mult, op1=ALU.add),
                  r=[r_zp[b]], w=[r_ps[b], r_zz[b]])
            if o == 0:
                P.add("dve", lambda e, b=b: e.tensor_tensor(out=zz[b][:], in0=zz[b][:], in1=xo[b][:], op=ALU.mult), r=[r_xo[b]], w=[r_zz[b]])
                if n + 1 < len(units):
                    loads(n + 1)
                dma(P, "sp", T.z1_d[cc * 128:(cc + 1) * 128, ts], zz[b][:], r=[r_zz[b]])

                def stage_b(b=b, tb=tb, cc=cc):
                    for tt in range(4):
                        P.add("pe", lambda e, b=b, tt=tt: e.transpose(out=ps[:, 2 + b, tt * 128:(tt + 1) * 128], in_=zz[b][:, tt * 128:(tt + 1) * 128], identity=C.identf[:]),
                              r=[r_zz[b]], w=[r_ps[2 + b]])
                    P.add("act", lambda e, b=b: e.copy(out=ut[b][:], in_=ps[:, 2 + b, :].rearrange("p (a c) -> p a c", c=128)), r=[], w=[r_ps[2 + b], r_ut[b]])
                    dma(P, "sp", T.utm2_d.rearrange("(a p) c -> p a c", p=128)[:, tb * 4:(tb + 1) * 4, cc * 128:(cc + 1) * 128], ut[b][:], r=[r_ut[b]])
                pending.append(stage_b)
            else:
                P.add("dve", lambda e, b=b: e.tensor_tensor(out=zb[b][:], in0=zz[b][:], in1=xo[b][:], op=ALU.mult), r=[r_xo[b], r_zz[b]], w=[r_zb[b]])
                if n + 1 < len(units):
                    loads(n + 1)
                dma(P, "sp", T.ymix_d[2 + cc][:, ts], zb[b][:], r=[r_zb[b]])
        for f in pending:
            f()
        P.end()


def phase_merge(C, P, T, w_in, w_brs, mg):
    nc = C.nc
    with contextlib.ExitStack() as st:
        hTh = sb(st, nc, "o_hT", [128, 16, 1024], BF16)
        ym = sb(st, nc, "o_ym", [128, 12, 1024], BF16)
        wg = [sb(st, nc, "o_wg%d" % i, [128, 16, 3, 128], BF16) for i in range(2)]
        wbr = [sb(st, nc, "o_wbr%d" % i, [128, 12, 128], BF16) for i in range(2)]
        sg = [sb(st, nc, "o_sg%d" % i, [128, 1024], F32) for i in range(3)]
        tq = [sb(st, nc, "o_tq%d" % i, [128, 1024], F32) for i in range(2)]
        mm = [sb(st, nc, "o_mm%d" % i, [128, 1024], F32) for i in range(2)]
        ps = pst(st, nc, "o_ps", [128, 8, 512], F32)
        r_ps = P.Rs(8)
        r_h = P.Rs(16)
        r_ym = P.Rs(12)
        r_wg, r_wbr, r_tq, r_mm = P.Rs(2), P.Rs(2), P.Rs(2), P.Rs(2)
        r_sg = P.Rs(3)
        r_mg = P.R()
        BR = ((0, 2, 0), (2, 6, 1), (8, 4, 2))
        GATE0 = 5120
        n = 0
        nu = 0
        nt = 0
        for h in range(2):
            hs = slice(h * 1024, (h + 1) * 1024)
            for k in range(16):
                dma(P, "sp", hTh[:, k, :], T.hT_d[:, k, hs], w=[r_h[k]])
            for i in range(12):
                dma(P, "sp", ym[:, i, :], T.ymix_d[i][:, hs], w=[r_ym[i]])
            for c in range(16):
                wb_ = c % 2
                for (k0, nk, bi) in BR:
                    dma(P, "pool", wg[wb_][:, :, bi, :], chunked(w_in[:, GATE0 + bi * D + c * 128:GATE0 + bi * D + (c + 1) * 128]), w=[r_wg[wb_]])
                for (k0, nk, bi) in BR:
                    dma(P, "pool", wbr[wb_][:, k0:k0 + nk, :], chunked(w_brs[bi][:, c * 128:(c + 1) * 128]), w=[r_wbr[wb_]])
                mb = nu % 2
                nu += 1
                for (k0, nk, bi) in BR:
                    gset = (n % 2) * 2
                    bset = 4 + (n % 2) * 2
                    sb_ = n % 3
                    n += 1
                    for blk in range(2):
                        for k in range(16):
                            P.add("pe", lambda e, gset=gset, blk=blk, k=k, wb_=wb_, bi=bi: e.matmul(
                                ps[:, gset + blk, :], lhsT=wg[wb_][:, k, bi, :], rhs=hTh[:, k, blk * 512:(blk + 1) * 512], start=(k == 0), stop=(k == 15)),
                                r=[r_wg[wb_], r_h[k]], w=[r_ps[gset + blk]])
                    for blk in range(2):
                        for kk in range(nk):
                            P.add("pe", lambda e, bset=bset, blk=blk, k=k0 + kk, wb_=wb_, kk=kk, nk=nk: e.matmul(
                                ps[:, bset + blk, :], lhsT=wbr[wb_][:, k, :], rhs=ym[:, k, blk * 512:(blk + 1) * 512], start=(kk == 0), stop=(kk == nk - 1)),
                                r=[r_wbr[wb_], r_ym[k0 + kk]], w=[r_ps[bset + blk]])
                    gv = ps[:, gset:gset + 2, :].rearrange("p b n -> p (b n)")
                    bv = ps[:, bset:bset + 2, :].rearrange("p b n -> p (b n)")
                    P.add("act", lambda e, gv=gv, sb_=sb_: e.activation(out=sg[sb_][:], in_=gv, func=AF.Sigmoid), r=[], w=[r_ps[gset], r_ps[gset + 1], r_sg[sb_]])
                    bbanks = [r_ps[bset], r_ps[bset + 1]]
                    if bi == 0:
                        P.add("dve", lambda e, bv=bv, sb_=sb_, mb=mb: e.tensor_tensor(out=mm[mb][:], in0=bv, in1=sg[sb_][:], op=ALU.mult), r=[r_sg[sb_]], w=bbanks + [r_mm[mb]])
                    else:
                        tb_ = nt % 2
                        nt += 1
                        P.add("dve", lambda e, bv=bv, sb_=sb_, tb_=tb_: e.tensor_tensor(out=tq[tb_][:], in0=bv, in1=sg[sb_][:], op=ALU.mult), r=[r_sg[sb_]], w=bbanks + [r_tq[tb_]])
                        if bi == 1:
                            P.add("dve", lambda e, mb=mb, tb_=tb_: e.tensor_tensor(out=mm[mb][:], in0=mm[mb][:], in1=tq[tb_][:], op=ALU.add), r=[r_tq[tb_]], w=[r_mm[mb]])
                        else:
                            P.add("dve", lambda e, mb=mb, tb_=tb_, c=c, hs=hs: e.tensor_tensor(out=mg[:, c, hs], in0=mm[mb][:], in1=tq[tb_][:], op=ALU.add),
                                  r=[r_tq[tb_], r_mm[mb]], w=[r_mg])
        P.end()


def phase_out_proj(C, P, mg, w_o, xT_dram):
    nc = C.nc
    xTc = chunked(xT_dram)
    with contextlib.ExitStack() as st:
        wo = [sb(st, nc, "p_wo%d" % i, [128, 16, 256], BF16) for i in range(2)]
        xr = [sb(st, nc, "p_xr%d" % i, [128, S], F32) for i in range(3)]
        ps = pst(st, nc, "p_ps", [128, 8, 512], F32)
        r_ps = P.Rs(8)
        r_wo = P.Rs(2)
        r_xr = P.Rs(3)
        r_mg = P.R()
        for c2 in range(8):
            wb_ = c2 % 2
            dma(P, "pool", wo[wb_][:], chunked(w_o[:, c2 * 256:(c2 + 1) * 256]), w=[r_wo[wb_]])
            for cc in range(2):
                c = c2 * 2 + cc
                bs = (c % 2) * 4
                xb = c % 3
                dma(P, "sp", xr[xb][:], xTc[:, c, :], w=[r_xr[xb]])
                for tb in range(4):
                    for k in range(16):
                        P.add("pe", lambda e, bank=bs + tb, wb_=wb_, k=k, cc=cc, tb=tb: e.matmul(
                            ps[:, bank, :], lhsT=wo[wb_][:, k, cc * 128:(cc + 1) * 128], rhs=mg[:, k, tb * 512:(tb + 1) * 512],
                            start=(k == 0), stop=(k == 15)), r=[r_wo[wb_], r_mg], w=[r_ps[bs + tb]])
                for tb in range(4):
                    P.add("dve", lambda e, bank=bs + tb, xb=xb, tb=tb: e.tensor_tensor(
                        out=xr[xb][:, tb * 512:(tb + 1) * 512], in0=ps[:, bank, :], in1=xr[xb][:, tb * 512:(tb + 1) * 512], op=ALU.add),
                        r=[], w=[r_ps[bs + tb], r_xr[xb]])
                dma(P, "sp", xTc[:, c, :], xr[xb][:], r=[r_xr[xb]])
        P.end()
NCOL = 136


def build_program(upto=99, debug=False):
    nc = bass.Bass("TRN2", target_bir_lowering=False)
    C = Ctx()
    C.nc = nc
    T = Ctx()

    def inp(name, shape, dt=F32):
        return nc.dram_tensor(name, list(shape), dt, kind="ExternalInput").ap()

    def scr(name, shape, dt=F32):
        return nc.dram_tensor(name, list(shape), dt).ap()

    x = inp("x", [S, D])
    mem = inp("mem", [NMEM, D])
    g_ff1 = inp("g_ff1", [1, D])
    g_mem = inp("g_mem", [1, D])
    w_ff1_in = inp("w_ff1_in", [D, 2 * DFF])
    w_ff1_out = inp("w_ff1_out", [DFF, D])
    w_ff2_in = inp("w_ff2_in", [D, 2 * DFF])
    w_ff2_out = inp("w_ff2_out", [DFF, D])
    w_in = inp("w_in", [D, INW])
    w_mem_kv = inp("w_mem_kv", [D, 1024])
    w_br_a = inp("w_br_a", [256, D])
    w_br_b = inp("w_br_b", [768, D])
    w_br_c = inp("w_br_c", [512, D])
    w_out = inp("w_out", [D, D])
    T.f_w1 = inp("hy_f_w1", [33, 64])
    T.f_w2 = inp("hy_f_w2", [64, 64])
    T.f_w3 = inp("hy_f_w3", [64, 64])
    T.f_w4 = inp("hy_f_w4", [64, 3072])
    cols_d = inp("c_cols", [128, NCOL])
    fcols_d = inp("c_fcols", [64, 4])
    identb_d = inp("c_identb", [128, 128], BF16)
    identf_d = inp("c_identf", [128, 128])
    T.cosT = inp("c_cosT", [128, S])
    T.sinT = inp("c_sinT", [128, S])
    T.Rm = inp("c_Rm", [128, 128], BF16)
    T.mask = inp("c_mask", [128, 256], BF16)
    T.zT = inp("c_zT", [33, S])
    T.decay = inp("c_decay", [S, HYW])
    T.cts = inp("c_cts", [16, 128, 2048], BF16)
    T.sts = inp("c_sts", [16, 128, 2048], BF16)
    T.cinv = inp("c_cinv", [4, 128, 8192], BF16)
    T.sinv = inp("c_sinv", [4, 128, 8192], BF16)
    out = nc.dram_tensor("out", [S, D], F32, kind="ExternalOutput").ap()
    xT = scr("xT_scr", [D, S])
    T.qk_d = scr("qk_scr", [12, 128, S], BF16)
    T.va_d = scr("va_scr", [3, 16, 128, 256], BF16)
    T.hyx_d = scr("hyx_scr", [3 * HYW, S])
    T.utm_d = scr("utm_scr", [S, HYW], BF16)
    T.utm2_d = scr("utm2_scr", [S, HYW], BF16)
    T.qc_d = scr("qc_scr", [4, 128, S], BF16)
    T.hT_d = scr("hT_scr", [128, 16, S], BF16)
    T.z1_d = scr("z1_scr", [HYW, S])
    T.ymix_d = scr("ymix_scr", [12, 128, S], BF16)
    dbg = {}
    if debug:
        dbg["xT"] = nc.dram_tensor("dbg_xT", [D, S], F32, kind="ExternalOutput").ap()
        dbg["ymix"] = nc.dram_tensor("dbg_ymix", [12, 128, S], BF16, kind="ExternalOutput").ap()
        dbg["qk"] = nc.dram_tensor("dbg_qk", [12, 128, S], BF16, kind="ExternalOutput").ap()
        dbg["hyx"] = nc.dram_tensor("dbg_hyx", [3 * HYW, S], F32, kind="ExternalOutput").ap()
        dbg["z1"] = nc.dram_tensor("dbg_z1", [HYW, S], F32, kind="ExternalOutput").ap()

    with contextlib.ExitStack() as st:
        P = Prog(nc, st)
        C.identb = sb(st, nc, "identb", [128, 128], BF16)
        C.identf = sb(st, nc, "identf", [128, 128], F32)
        C.onesb = sb(st, nc, "onesb", [128, 128], BF16)
        C.epsc = sb(st, nc, "epsc", [128, 1], F32)
        cols = sb(st, nc, "cols", [128, NCOL], F32)
        fcols = sb(st, nc, "fcols", [64, 4], F32)
        kmT = sb(st, nc, "kmT", [128, 4, NMEM], BF16)
        vm = sb(st, nc, "vm", [128, 2, 512], BF16)
        r0 = P.Rs(6)
        dma(P, "sp", C.identb[:], identb_d, w=[r0[0]])
        dma(P, "sp", C.identf[:], identf_d, w=[r0[1]])
        dma(P, "sp", cols[:], cols_d, w=[r0[2]])
        dma(P, "sp", fcols[:], fcols_d, w=[r0[3]])
        P.add("dve", lambda e: e.memset(C.onesb[:], 1.0), w=[r0[4]])
        P.add("dve", lambda e: e.memset(C.epsc[:], EPS), w=[r0[5]])
        P.end()
        T.gq_col = cols[:, 0:1]
        T.gk_col = cols[:, 1:2]
        T.mq_col = cols[:, 2:3]
        mk_col = cols[:, 3:4]
        gmixT = cols[:, 4:20]
        gff2T = cols[:, 20:36]
        gpostT = cols[:, 36:52]
        T.hbias = cols[:, 52:64]
        T.cw = cols[:, 64:118].rearrange("p (j t) -> p j t", t=3)
        T.cb = cols[:, 118:136]
        T.f_b = fcols[:, 0:3]
        T.f_freq = fcols[:, 3:4]

        def dump():
            if debug:
                r = P.Rs(5)
                dma(P, "sp", dbg["xT"], xT, w=[r[0]])
                dma(P, "sp", dbg["ymix"], T.ymix_d, w=[r[1]])
                dma(P, "sp", dbg["qk"], T.qk_d, w=[r[2]])
                dma(P, "sp", dbg["hyx"], T.hyx_d, w=[r[3]])
                dma(P, "sp", dbg["z1"], T.z1_d, w=[r[4]])
                P.end()

        def run():
            with contextlib.ExitStack() as st2:
                xnT = sb(st2, nc, "xnT", [128, 16, S], BF16)
                phase_tm_norm(C, P, x, 16, g_ff1[0:1, :].broadcast_to([128, D]), xnT, P.R(), xT_dram=xT)
                if upto < 1:
                    return
                phase_ffn(C, P, xnT, w_ff1_in, w_ff1_out, xT)
            if upto < 2:
                return
            with contextlib.ExitStack() as st2:
                memnT = sb(st2, nc, "memnT", [128, 16, NMEM], BF16)
                phase_tm_norm(C, P, mem, 2, g_mem[0:1, :].broadcast_to([128, D]), memnT, P.R())
                phase_memkv(C, P, memnT, w_mem_kv, mk_col, kmT, vm)
            with contextlib.ExitStack() as st2:
                hT = sb(st2, nc, "hT", [128, 16, S], BF16)
                phase_fm_norm(C, P, xT, gmixT, hT)
                phase_win(C, P, hT, w_in, T)
            if upto < 3:
                return
            phase_attn_a(C, P, T)
            phase_attn_c(C, P, T, kmT, vm)
            if upto < 4:
                return
            with contextlib.ExitStack() as st2:
                hh3 = sb(st2, nc, "hh3", [64, S], F32)
                hh3b = sb(st2, nc, "hh3b", [64, S], BF16)
                phase_hy_filter_mlp(C, P, T, hh3, hh3b)
                for o in range(2):
                    with contextlib.ExitStack() as st3:
                        Yre = sb(st3, nc, "Yre", [128, 16, HYW], BF16)
                        Yim = sb(st3, nc, "Yim", [128, 16, HYW], BF16)
                        with contextlib.ExitStack() as st4:
                            ksum = sb(st4, nc, "ksum", [128, 16, HYW], BF16)
                            kdif = sb(st4, nc, "kdif", [128, 16, HYW], BF16)
                            phase_hy_filter(C, P, T, hh3b, o, ksum, kdif)
                            phase_hy_fwd(C, P, T, T.utm_d if o == 0 else T.utm2_d, ksum, kdif, Yre, Yim)
                        phase_hy_inv(C, P, T, o, Yre, Yim)
            if upto < 5:
                return
            with contextlib.ExitStack() as st2:
                mg = sb(st2, nc, "mg", [128, 16, S], BF16)
                phase_merge(C, P, T, w_in, (w_br_a, w_br_b, w_br_c), mg)
                phase_out_proj(C, P, mg, w_out, xT)
            if upto < 6:
                return
            with contextlib.ExitStack() as st2:
                xnT = sb(st2, nc, "xnT2", [128, 16, S], BF16)
                phase_fm_norm(C, P, xT, gff2T, xnT)
                phase_ffn(C, P, xnT, w_ff2_in, w_ff2_out, xT)
            if upto < 7:
                return
            phase_fm_norm(C, P, xT, gpostT, None, final_out=out)

        run()
        dump()
    C.P = P
    return nc


_CONSTS = {}


def host_consts():
    if _CONSTS:
        return _CONSTS
    bf = ml_dtypes.bfloat16
    c = {}
    c["c_identb"] = np.eye(128, dtype=np.float32).astype(bf)
    c["c_identf"] = np.eye(128, dtype=np.float32)
    inv = np.power(np.float32(500000.0), -np.arange(0, 32, 2, dtype=np.float32) / np.float32(32)).astype(np.float32)
    ang = (np.arange(S, dtype=np.float32)[:, None] * inv[None, :]).astype(np.float32)
    cosT = np.ones((128, S), np.float32)
    sinT = np.zeros((128, S), np.float32)
    cosT[0:16] = np.cos(ang).T
    cosT[16:32] = np.cos(ang).T
    sinT[0:16] = np.sin(ang).T
    sinT[16:32] = np.sin(ang).T
    c["c_cosT"] = cosT
    c["c_sinT"] = sinT
    Rm = np.zeros((128, 128), np.float32)
    for d in range(16):
        Rm[d + 16, d] = -1.0
        Rm[d, d + 16] = 1.0
    c["c_Rm"] = Rm.astype(bf)
    a = np.arange(128)[:, None]
    b = np.arange(256)[None, :]
    c["c_mask"] = ((b >= a) & (b <= a + 128)).astype(np.float32).astype(bf)
    bands = 16
    t = np.linspace(0.0, 1.0, S, dtype=np.float32)[:, None]
    w = (2.0 * math.pi * np.arange(S, dtype=np.float32)[:, None] / S).astype(np.float32)
    f = np.linspace(1e-4, bands - 1, bands, dtype=np.float32)[None, :]
    z = np.concatenate([t, np.cos(f * w), -np.sin(f * w)], axis=-1).astype(np.float32)
    c["c_zT"] = np.ascontiguousarray(z.T)
    max_decay = math.log(1e-2) / 0.3
    min_decay = math.log(1e-2) / 1.5
    deltas = np.linspace(min_decay, max_decay, HYW, dtype=np.float32)
    c["c_decay"] = np.exp(-t * np.abs(deltas)[None, :]).astype(np.float32)
    fi = np.arange(2048, dtype=np.int64)
    ti = np.arange(2048, dtype=np.int64)
    ph = ((2 * fi[:, None] + 1) * ti[None, :]) % (2 * NFFT)
    th = ph.astype(np.float64) * (math.pi / NFFT)
    Cft = np.cos(th)
    Sft = np.sin(th)
    def fwd(M):
        A = M.T.reshape(16, 128, 16, 128)
        return np.ascontiguousarray(A.transpose(2, 1, 0, 3).reshape(16, 128, 2048).astype(np.float32).astype(bf))
    def invm(M):
        A = M.reshape(16, 128, 4, 512)
        return np.ascontiguousarray(A.transpose(2, 1, 0, 3).reshape(4, 128, 8192).astype(np.float32).astype(bf))
    c["c_cts"] = fwd(Cft)
    c["c_sts"] = fwd(Sft)
    c["c_cinv"] = invm(Cft)
    c["c_sinv"] = invm(Sft)
    _CONSTS.update(c)
    return _CONSTS


def layout_small(inputs):
    cols = np.zeros((128, NCOL), np.float32)
    cols[:, 0] = inputs["a_gq"][0]
    cols[:, 1] = inputs["a_gk"][0]
    cols[:, 2] = inputs["m_gq"][0]
    cols[:, 3] = inputs["m_gk"][0]
    cols[:, 4:20] = inputs["g_mix"][0].reshape(16, 128).T
    cols[:, 20:36] = inputs["g_ff2"][0].reshape(16, 128).T
    cols[:, 36:52] = inputs["g_post"][0].reshape(16, 128).T
    cols[:, 52:64] = inputs["hy_bias"][0].reshape(12, 128).T
    cw = inputs["hy_conv_w"][0]
    cols[:, 64:118] = cw.reshape(3, 18, 128).transpose(2, 1, 0).reshape(128, 54)
    cols[:, 118:136] = inputs["hy_conv_b"][0].reshape(18, 128).T
    fcols = np.zeros((64, 4), np.float32)
    fcols[:, 0] = inputs["hy_f_b1"][0]
    fcols[:, 1] = inputs["hy_f_b2"][0]
    fcols[:, 2] = inputs["hy_f_b3"][0]
    fcols[:, 3] = inputs["hy_f_freq"][0]
    return {"c_cols": cols, "c_fcols": fcols}


BIG = ("g_ff1", "g_mem", "w_ff1_in", "w_ff1_out", "w_ff2_in", "w_ff2_out", "w_in", "w_mem_kv", "w_br_a", "w_br_b",
       "w_br_c", "w_out", "hy_f_w1", "hy_f_w2", "hy_f_w3", "hy_f_w4")


def make_in_map(inputs, b):
    m = dict(host_consts())
    m.update(layout_small(inputs))
    m["x"] = np.ascontiguousarray(inputs["x"][b])
    m["mem"] = np.ascontiguousarray(inputs["mem"][b])
    for k in BIG:
        v = np.asarray(inputs[k])
        m[k] = np.ascontiguousarray(v[0]) if v.ndim == 3 else np.ascontiguousarray(v)
    return m


def kernel(**inputs):
    inputs = {k: np.asarray(v) for k, v in inputs.items()}
    nc = build_program()
    in_maps = [make_in_map(inputs, b) for b in range(8)]
    res = run_bass_kernel_spmd(nc, in_maps, core_ids=list(range(8)))
    return np.stack([np.asarray(r["out"], dtype=np.float32) for r in res.results], axis=0)
```

```python
import contextlib
import math
import numpy as np
import ml_dtypes
import concourse.bass as bass
import concourse.mybir as mybir
from concourse.bass_utils import run_bass_kernel_spmd

F32 = mybir.dt.float32
BF16 = mybir.dt.bfloat16
AF = mybir.ActivationFunctionType
ALU = mybir.AluOpType

D = 2048
S = 2048
DFF = 5632
NMEM = 256
EPS = 1e-6
INW = 11264
HYW = 768
NFFT = 4096
ENGS = ("pe", "act", "dve", "pool", "sp")
SAME_ENGINE_SYNC = True


class Res:
    __slots__ = ("name", "lw", "rd", "rdd")

    def __init__(self, name=""):
        self.name = name
        self.lw = None
        self.rd = {}
        self.rdd = []


class Op:
    __slots__ = ("eng", "fn", "deps", "dma", "need_inc", "sem", "val", "pre", "n")

    def __init__(self, eng, fn, dma, n):
        self.eng = eng
        self.fn = fn
        self.dma = dma
        self.deps = []
        self.need_inc = False
        self.sem = None
        self.val = 0
        self.pre = None
        self.n = n


class Prog:
    def __init__(self, nc, stack, ndma=8):
        self.nc = nc
        self.S = ndma
        self.sem = {e: stack.enter_context(nc.semaphore("c_" + e)) for e in ENGS}
        self.cnt = {e: 0 for e in ENGS}
        self.dsem = {
            q: [stack.enter_context(nc.semaphore("d_%s%d" % (q, i))) for i in range(ndma)]
            for q in ("sp", "act", "pool")
        }
        self.dcnt = {q: 0 for q in ("sp", "act", "pool")}
        self.waited = {e: {} for e in ENGS}
        self.ops = []
        self.res = []
        self.nphase = 0
        self.total_ops = 0

    def R(self, name=""):
        r = Res(name)
        self.res.append(r)
        return r

    def Rs(self, n, name=""):
        return [self.R("%s%d" % (name, i)) for i in range(n)]

    def add(self, eng, fn, r=(), w=(), dma=False):
        op = Op(eng, fn, dma, len(self.ops))
        cd = {}
        dd = {}

        def dep(p):
            if p is None or p is op:
                return
            if p.dma:
                dd[id(p)] = p
            else:
                q = cd.get(p.eng)
                if q is None or q.n < p.n:
                    cd[p.eng] = p

        for x in r:
            dep(x.lw)
        for x in w:
            dep(x.lw)
            for y in x.rd.values():
                dep(y)
            for y in x.rdd:
                dep(y)
        for x in r:
            if dma:
                x.rdd.append(op)
            else:
                x.rd[eng] = op
        for x in w:
            x.lw = op
            x.rd = {}
            x.rdd = []
        op.deps = list(cd.values()) + list(dd.values())
        self.ops.append(op)
        return op

    def _needs_signal(self, p, x):
        if p.dma:
            return True
        if p.eng == x.eng and not x.dma:
            if p.eng == "pe":
                return False
            return SAME_ENGINE_SYNC
        return True

    def end(self):
        nc = self.nc
        ops = self.ops
        for x in ops:
            x.deps = [p for p in x.deps if self._needs_signal(p, x)]
            for p in x.deps:
                p.need_inc = True
        for x in ops:
            if x.dma:
                i = self.dcnt[x.eng]
                self.dcnt[x.eng] += 1
                x.sem = self.dsem[x.eng][i % self.S]
                x.val = 16 * (i // self.S + 1)
                x.pre = (x.sem, 16 * (i // self.S)) if i >= self.S else None
            elif x.need_inc:
                self.cnt[x.eng] += 1
                x.sem = self.sem[x.eng]
                x.val = self.cnt[x.eng]
        final_dma = []
        for q in ("sp", "act", "pool"):
            n = self.dcnt[q]
            for j in range(min(n, self.S)):
                i = n - 1 - j
                final_dma.append((self.dsem[q][i % self.S], 16 * (i // self.S + 1)))

        def emit(eng_name, e):
            wd = self.waited[eng_name]

            def wait(sem, val):
                if wd.get(id(sem), 0) >= val:
                    return
                wd[id(sem)] = val
                e.wait_ge(sem, val)

            for x in ops:
                if x.eng != eng_name:
                    continue
                for p in x.deps:
                    wait(p.sem, p.val)
                if x.pre is not None:
                    wait(*x.pre)
                ins = x.fn(e)
                if x.dma:
                    ins.then_inc(x.sem, 16)
                elif x.need_inc:
                    ins.then_inc(x.sem, 1)
            if eng_name == "sp":
                for sem, val in final_dma:
                    wait(sem, val)

        with nc.Block("ph%d" % self.nphase, no_gpsimd_drain=True) as block:
            @block.tensor
            def _(e):
                emit("pe", e)

            @block.scalar
            def _(e):
                emit("act", e)

            @block.vector
            def _(e):
                emit("dve", e)

            @block.gpsimd
            def _(e):
                emit("pool", e)

            @block.sync
            def _(e):
                emit("sp", e)

        self.nphase += 1
        self.total_ops += len(ops)
        self.ops = []
        for r in self.res:
            r.lw = None
            r.rd = {}
            r.rdd = []
        self.res = []


class Ctx:
    pass


def dma(P, q, out, in_, r=(), w=()):
    return P.add(q, lambda e: e.dma_start(out=out, in_=in_), r=r, w=w, dma=True)


_UID = [0]


def uid(name):
    _UID[0] += 1
    return "%s_u%d" % (name, _UID[0])


def sb(st, nc, name, shape, dt):
    return st.enter_context(nc.sbuf_tensor(uid(name), shape, dt))


def pst(st, nc, name, shape, dt):
    return st.enter_context(nc.psum_tensor(uid(name), shape, dt))


def chunked(ap2d):
    return ap2d.rearrange("(k p) n -> p k n", p=128)


def phase_tm_norm(C, P, src, ntiles, g_row, dstT, r_dst, xT_dram=None):
    nc = C.nc
    with contextlib.ExitStack() as st:
        gb = sb(st, nc, "n_gb", [128, D], F32)
        xs = [sb(st, nc, "n_xs%d" % i, [128, D], F32) for i in range(3)]
        junk = sb(st, nc, "n_junk", [128, D], BF16)
        ss = [sb(st, nc, "n_ss%d" % i, [128, 1], F32) for i in range(2)]
        rs = [sb(st, nc, "n_rs%d" % i, [128, 1], F32) for i in range(2)]
        xb = [sb(st, nc, "n_xb%d" % i, [128, D], BF16) for i in range(2)]
        xTs = [sb(st, nc, "n_xTs%d" % i, [128, 16, 128], F32) for i in range(2)]
        pt = [pst(st, nc, "n_pt%d" % i, [128, 8, 128], BF16) for i in range(2)]
        ptf = [pst(st, nc, "n_ptf%d" % i, [128, 4, 128], F32) for i in range(2)]
        r_gb = P.R()
        r_xs, r_ss, r_rs, r_xb, r_xTs, r_pt, r_ptf = (P.Rs(2) for _ in range(7))
        r_junk = P.R()
        r_xs = P.Rs(3)
        dma(P, "sp", gb[:], g_row, w=[r_gb])
        npt = 0
        nptf = 0
        dma(P, "sp", xs[0][:], src[0:128, :], w=[r_xs[0]])
        for i in range(ntiles):
            b = i % 2
            bx = i % 3
            if i + 1 < ntiles:
                dma(P, "sp", xs[(i + 1) % 3][:], src[(i + 1) * 128:(i + 2) * 128, :], w=[r_xs[(i + 1) % 3]])
            P.add("act", lambda e, b=b, bx=bx: e.activation(out=junk[:], in_=xs[bx][:], func=AF.Square, accum_out=ss[b][:]),
                  r=[r_xs[bx]], w=[r_junk, r_ss[b]])
            P.add("act", lambda e, b=b: e.activation(out=rs[b][:], in_=ss[b][:], func=AF.Sqrt, scale=1.0 / D, bias=C.epsc[:, 0:1]),
                  r=[r_ss[b]], w=[r_rs[b]])
            P.add("dve", lambda e, b=b: e.reciprocal(out=rs[b][:], in_=rs[b][:]), r=[], w=[r_rs[b]])
            P.add("dve", lambda e, b=b, bx=bx: e.scalar_tensor_tensor(out=xb[b][:], in0=xs[bx][:], scalar=rs[b][:, 0:1],
                                                               in1=gb[:], op0=ALU.mult, op1=ALU.mult),
                  r=[r_xs[bx], r_rs[b], r_gb], w=[r_xb[b]])
            for k0 in range(0, 16, 8):
                s_ = npt % 2
                npt += 1
                for k in range(k0, k0 + 8):
                    P.add("pe", lambda e, b=b, k=k, s_=s_: e.transpose(out=pt[s_][:, k % 8, :],
                                                                      in_=xb[b][:, k * 128:(k + 1) * 128],
                                                                      identity=C.identb[:]),
                          r=[r_xb[b]], w=[r_pt[s_]])
                P.add("act", lambda e, s_=s_, k0=k0, i=i: e.copy(out=dstT[:, k0:k0 + 8, i * 128:(i + 1) * 128],
                                                              in_=pt[s_][:]),
                      r=[], w=[r_pt[s_], r_dst])
            if xT_dram is not None:
                for k0 in range(0, 16, 4):
                    s_ = nptf % 2
                    nptf += 1
                    for k in range(k0, k0 + 4):
                        P.add("pe", lambda e, bx=bx, k=k, s_=s_: e.transpose(out=ptf[s_][:, k % 4, :],
                                                                          in_=xs[bx][:, k * 128:(k + 1) * 128],
                                                                          identity=C.identf[:]),
                              r=[r_xs[bx]], w=[r_ptf[s_]])
                    P.add("dve", lambda e, s_=s_, k0=k0, b=b: e.tensor_copy(out=xTs[b][:, k0:k0 + 4, :], in_=ptf[s_][:]),
                          r=[], w=[r_ptf[s_], r_xTs[b]])
                dma(P, "pool", chunked(xT_dram)[:, :, i * 128:(i + 1) * 128], xTs[b][:], r=[r_xTs[b]])
        P.end()


def phase_fm_norm(C, P, xT_dram, gT_col, dstT, final_out=None):
    nc = C.nc
    xTc = chunked(xT_dram)
    G = 512
    NG = S // G
    NB = G // 512
    with contextlib.ExitStack() as st:
        xr = [sb(st, nc, "f_xr%d" % i, [128, G], F32) for i in range(32)]
        sq = [sb(st, nc, "f_sq%d" % i, [128, G], BF16) for i in range(2)]
        rstd = [sb(st, nc, "f_rstd%d" % i, [128, G], F32) for i in range(2)]
        ps = pst(st, nc, "f_ps", [128, 4, 512], F32)
        r_xr = P.Rs(32)
        r_sq = P.Rs(2)
        r_ps = P.Rs(4)
        r_rstd = P.Rs(2)
        r_dst = P.R()
        if final_out is not None:
            yn = [sb(st, nc, "f_yn%d" % i, [128, 512], F32) for i in range(2)]
            ot = [sb(st, nc, "f_ot%d" % i, [128, 16, 4, 128], F32) for i in range(2)]
            ptf = [pst(st, nc, "f_ptf%d" % i, [128, 4, 128], F32) for i in range(4)]
            r_yn = P.Rs(2)
            r_ptf = P.Rs(4)
            r_ot = P.Rs(2)
        nq = 0

        def loads(g):
            for c in range(16):
                dma(P, "sp", xr[(g % 2) * 16 + c][:], xTc[:, c, g * G:(g + 1) * G], w=[r_xr[(g % 2) * 16 + c]])

        for g in range(NG):
            gs = slice(g * G, (g + 1) * G)
            pb = (g % 2) * 2 if NB == 2 else g % 4
            xo_ = (g % 2) * 16
            if g == 0:
                loads(0)
            if g + 1 < NG:
                loads(g + 1)
            for c in range(16):
                P.add("act", lambda e, c=c, xo_=xo_: e.activation(out=sq[c % 2][:], in_=xr[xo_ + c][:], func=AF.Square), r=[r_xr[xo_ + c]], w=[r_sq[c % 2]])
                for blk in range(NB):
                    P.add("pe", lambda e, c=c, blk=blk, pb=pb: e.matmul(ps[:, pb + blk, :], lhsT=C.onesb[:], rhs=sq[c % 2][:, blk * 512:(blk + 1) * 512],
                                                                      start=(c == 0), stop=(c == 15)), r=[r_sq[c % 2]], w=[r_ps[pb + blk]])
            rb = g % 2
            psv = ps[:, pb:pb + NB, :].rearrange("p b n -> p (b n)")
            P.add("act", lambda e, rb=rb, psv=psv: e.activation(out=rstd[rb][:], in_=psv, func=AF.Ln, scale=1.0 / D, bias=C.epsc[:, 0:1]),
                  r=[], w=[r_ps[pb + b_] for b_ in range(NB)] + [r_rstd[rb]])
            P.add("act", lambda e, rb=rb: e.activation(out=rstd[rb][:], in_=rstd[rb][:], func=AF.Exp, scale=-0.5), r=[], w=[r_rstd[rb]])
            if final_out is None:
                for c in range(16):
                    P.add("dve", lambda e, c=c, rb=rb, gs=gs, xo_=xo_: e.scalar_tensor_tensor(out=dstT[:, c, gs], in0=xr[xo_ + c][:], scalar=gT_col[:, c:c + 1],
                                                                                  in1=rstd[rb][:], op0=ALU.mult, op1=ALU.mult),
                          r=[r_xr[xo_ + c], r_rstd[rb]], w=[r_dst])
            else:
                ob = g % 2
                for c in range(16):
                    y = yn[c % 2]
                    P.add("dve", lambda e, c=c, y=y, rb=rb, xo_=xo_: e.scalar_tensor_tensor(out=y[:], in0=xr[xo_ + c][:], scalar=gT_col[:, c:c + 1],
                                                                                in1=rstd[rb][:], op0=ALU.mult, op1=ALU.mult),
                          r=[r_xr[xo_ + c], r_rstd[rb]], w=[r_yn[c % 2]])
                    s_ = nq % 4
                    nq += 1
                    for tt in range(4):
                        P.add("pe", lambda e, y=y, tt=tt, s_=s_: e.transpose(out=ptf[s_][:, tt, :], in_=y[:, tt * 128:(tt + 1) * 128], identity=C.identf[:]),
                              r=[r_yn[c % 2]], w=[r_ptf[s_]])
                    if c % 2 == 0:
                        P.add("act", lambda e, s_=s_, c=c, ob=ob: e.copy(out=ot[ob][:, c, :, :], in_=ptf[s_][:]), r=[], w=[r_ptf[s_], r_ot[ob]])
                    else:
                        P.add("dve", lambda e, s_=s_, c=c, ob=ob: e.tensor_copy(out=ot[ob][:, c, :, :], in_=ptf[s_][:]), r=[], w=[r_ptf[s_], r_ot[ob]])
                for tt in range(4):
                    t0 = g * 512 + tt * 128
                    dma(P, "pool", final_out[t0:t0 + 128, :].rearrange("p (c j) -> p c j", j=128), ot[ob][:, :, tt, :], r=[r_ot[ob]])
        P.end()


def phase_ffn(C, P, xnT, w_in, w_out, xT_dram):
    nc = C.nc
    xTc = chunked(xT_dram)
    PARTS = ((0, 22), (22, 22))
    JP = 22
    with contextlib.ExitStack() as st:
        gT = sb(st, nc, "m_gT", [128, JP, S], BF16)
        NWB = 2
        w1 = [sb(st, nc, "m_w1_%d" % i, [128, 16, 2, 128], BF16) for i in range(NWB)]
        w2 = [sb(st, nc, "m_w2_%d" % i, [128, JP, 128], BF16) for i in range(2)]
        sg = [sb(st, nc, "m_sg%d" % i, [128, 512], F32) for i in range(2)]
        xr = [sb(st, nc, "m_xr%d" % i, [128, S], F32) for i in range(2)]
        ps = pst(st, nc, "m_ps", [128, 8, 512], F32)
        r_ps = P.Rs(8)
        r_w1 = P.Rs(NWB)
        r_w2 = P.Rs(2)
        r_sg = P.Rs(2)
        r_xr = P.Rs(2)
        r_xn = P.R()
        r_xT = P.Rs(16)
        r_g = [[P.R() for _ in range(4)] for _ in range(JP)]
        nsg = 0
        nw1 = 0
        nw2 = 0
        nxr = 0
        unit = 0
        for (j0, njp) in PARTS:
            for jl in range(njp):
                j = j0 + jl
                wb = nw1 % NWB
                nw1 += 1
                dma(P, "pool", w1[wb][:, :, 0, :], chunked(w_in[:, j * 128:(j + 1) * 128]), w=[r_w1[wb]])
                dma(P, "pool", w1[wb][:, :, 1, :], chunked(w_in[:, DFF + j * 128:DFF + (j + 1) * 128]), w=[r_w1[wb]])
                for h in range(2):
                    bs = (unit % 2) * 4
                    unit += 1
                    for ab in range(2):
                        for blk in range(2):
                            bank = bs + ab * 2 + blk
                            tb = h * 2 + blk
                            for k in range(16):
                                P.add("pe", lambda e, bank=bank, wb=wb, ab=ab, k=k, tb=tb: e.matmul(
                                    ps[:, bank, :], lhsT=w1[wb][:, k, ab, :], rhs=xnT[:, k, tb * 512:(tb + 1) * 512],
                                    start=(k == 0), stop=(k == 15)), r=[r_w1[wb], r_xn], w=[r_ps[bank]])
                    for blk in range(2):
                        tb = h * 2 + blk
                        si = nsg % 2
                        nsg += 1
                        P.add("act", lambda e, si=si, bank=bs + blk: e.activation(out=sg[si][:], in_=ps[:, bank, :], func=AF.Silu),
                              r=[], w=[r_ps[bs + blk], r_sg[si]])
                        P.add("dve", lambda e, si=si, bank=bs + 2 + blk, jl=jl, tb=tb: e.tensor_tensor(
                            out=gT[:, jl, tb * 512:(tb + 1) * 512], in0=ps[:, bank, :], in1=sg[si][:], op=ALU.mult),
                            r=[r_sg[si]], w=[r_ps[bs + 2 + blk], r_g[jl][tb]])
            for c in range(16):
                wb = nw2 % 2
                nw2 += 1
                dma(P, "pool", w2[wb][:, 0:njp, :], chunked(w_out[j0 * 128:(j0 + njp) * 128, c * 128:(c + 1) * 128]), w=[r_w2[wb]])
                bs = (unit % 2) * 4
                unit += 1
                xb = nxr % 2
                nxr += 1
                dma(P, "sp", xr[xb][:], xTc[:, c, :], r=[r_xT[c]], w=[r_xr[xb]])
                for tb in range(4):
                    for jl in range(njp):
                        P.add("pe", lambda e, bank=bs + tb, wb=wb, jl=jl, tb=tb, njp=njp: e.matmul(
                            ps[:, bank, :], lhsT=w2[wb][:, jl, :], rhs=gT[:, jl, tb * 512:(tb + 1) * 512],
                            start=(jl == 0), stop=(jl == njp - 1)), r=[r_w2[wb], r_g[jl][tb]], w=[r_ps[bs + tb]])
                for tb in range(4):
                    P.add("dve", lambda e, bank=bs + tb, xb=xb, tb=tb: e.scalar_tensor_tensor(
                        out=xr[xb][:, tb * 512:(tb + 1) * 512], in0=ps[:, bank, :], scalar=0.5,
                        in1=xr[xb][:, tb * 512:(tb + 1) * 512], op0=ALU.mult, op1=ALU.add),
                        r=[], w=[r_ps[bs + tb], r_xr[xb]])
                dma(P, "sp", xTc[:, c, :], xr[xb][:], r=[r_xr[xb]], w=[r_xT[c]])
        P.end()


def phase_memkv(C, P, memnT, w_kv, gk_col, kmT, vm):
    nc = C.nc
    with contextlib.ExitStack() as st:
        wk = sb(st, nc, "k_w", [128, 16, 1024], BF16)
        sq = sb(st, nc, "k_sq", [128, 256], BF16)
        rstd = sb(st, nc, "k_rstd", [128, 256], F32)
        ps = pst(st, nc, "k_ps", [128, 4, 512], F32)
        r_w = P.Rs(2)
        r_ps = P.Rs(4)
        r_sq, r_rstd, r_km, r_vm, r_mn = P.Rs(5)
        for hf in range(2):
            dma(P, "pool", wk[:, :, hf * 512:(hf + 1) * 512], chunked(w_kv[:, hf * 512:(hf + 1) * 512]), w=[r_w[hf]])
        for h in range(4):
            for k in range(16):
                P.add("pe", lambda e, h=h, k=k: e.matmul(ps[:, 0, 0:256], lhsT=wk[:, k, h * 128:(h + 1) * 128], rhs=memnT[:, k, :],
                                                         start=(k == 0), stop=(k == 15)), r=[r_w[0], r_mn], w=[r_ps[0]])
            P.add("act", lambda e: e.activation(out=sq[:], in_=ps[:, 0, 0:256], func=AF.Square), r=[], w=[r_ps[0], r_sq])
            P.add("pe", lambda e: e.matmul(ps[:, 1, 0:256], lhsT=C.onesb[:], rhs=sq[:], start=True, stop=True), r=[r_sq], w=[r_ps[1]])
            P.add("act", lambda e: e.activation(out=rstd[:], in_=ps[:, 1, 0:256], func=AF.Ln, scale=1.0 / 128, bias=C.epsc[:, 0:1]),
                  r=[], w=[r_ps[1], r_rstd])
            P.add("act", lambda e: e.activation(out=rstd[:], in_=rstd[:], func=AF.Exp, scale=-0.5), r=[], w=[r_rstd])
            P.add("dve", lambda e, h=h: e.scalar_tensor_tensor(out=kmT[:, h, :], in0=ps[:, 0, 0:256], scalar=gk_col[:, 0:1], in1=rstd[:],
                                                              op0=ALU.mult, op1=ALU.mult), r=[r_rstd], w=[r_ps[0], r_km])
        for mt in range(2):
            for k in range(16):
                P.add("pe", lambda e, mt=mt, k=k: e.matmul(ps[:, 2 + mt, :], lhsT=memnT[:, k, mt * 128:(mt + 1) * 128], rhs=wk[:, k, 512:1024],
                                                           start=(k == 0), stop=(k == 15)), r=[r_w[1], r_mn], w=[r_ps[2 + mt]])
            P.add("act", lambda e, mt=mt: e.copy(out=vm[:, mt, :], in_=ps[:, 2 + mt, :]), r=[], w=[r_ps[2 + mt], r_vm])
        P.end()


def phase_win(C, P, hT, w_in, T):
    nc = C.nc
    GROUPS = ((1, 2048), (4, 512), (16, 128))
    with contextlib.ExitStack() as st:
        NWB = 3
        wb = [sb(st, nc, "w_wb%d" % i, [128, 16, 256], BF16) for i in range(NWB)]
        cosT = sb(st, nc, "w_cos", [128, S], F32)
        sinT = sb(st, nc, "w_sin", [128, S], F32)
        Rm = sb(st, nc, "w_R", [128, 128], BF16)
        sq = [sb(st, nc, "w_sq%d" % i, [128, 1024], BF16) for i in range(2)]
        qb = [sb(st, nc, "w_qb%d" % i, [128, 1024], BF16) for i in range(2)]
        rstd = [sb(st, nc, "w_rstd%d" % i, [128, 1024], F32) for i in range(2)]
        u1 = [sb(st, nc, "w_u1%d" % i, [128, 1024], F32) for i in range(2)]
        u2 = [sb(st, nc, "w_u2%d" % i, [128, 1024], F32) for i in range(2)]
        qo = [sb(st, nc, "w_qo%d" % i, [128, 1024], BF16) for i in range(2)]
        ub = [sb(st, nc, "w_ub%d" % i, [128, S + 2], F32) for i in range(2)]
        uc = [sb(st, nc, "w_uc%d" % i, [128, S], F32) for i in range(2)]
        ut = [sb(st, nc, "w_ut%d" % i, [128, 16, 128], BF16) for i in range(2)]
        gs = [sb(st, nc, "w_gs%d" % i, [128, 1024], F32) for i in range(3)]
        vs = [sb(st, nc, "w_vs%d" % i, [128, 256], BF16) for i in range(2)]
        psg = pst(st, nc, "w_psg", [128, 4, 512], F32)
        pss = pst(st, nc, "w_pss", [128, 2, 512], F32)
        psr = pst(st, nc, "w_psr", [128, 2, 512], F32)
        r_wb = P.Rs(NWB)
        r_psg = P.Rs(4)
        r_pss = P.Rs(2)
        r_psr = P.Rs(2)
        r_sq, r_qb, r_rstd, r_u1, r_u2, r_qo, r_ub, r_uc, r_ut, r_vs = (P.Rs(2) for _ in range(10))
        r_gs = P.Rs(3)
        r_c = P.Rs(3)
        r_h = P.R()
        dma(P, "sp", cosT[:], T.cosT, w=[r_c[0]])
        dma(P, "sp", sinT[:], T.sinT, w=[r_c[1]])
        dma(P, "sp", Rm[:], T.Rm, w=[r_c[2]])
        for i in range(2):
            P.add("pool", lambda e, i=i: e.memset(ub[i][:], 0.0), w=[r_ub[i]])
        cnt = {"u": 0, "qk": 0, "hy": 0, "g": 0, "v": 0, "tr": 0}

        def gemm_unit(wbi, jj, h):
            s_ = cnt["u"] % 2
            cnt["u"] += 1
            for blk in range(2):
                bank = s_ * 2 + blk
                tb = h * 2 + blk
                for k in range(16):
                    P.add("pe", lambda e, bank=bank, k=k, tb=tb: e.matmul(psg[:, bank, :], lhsT=wb[wbi][:, k, jj * 128:(jj + 1) * 128],
                                                                          rhs=hT[:, k, tb * 512:(tb + 1) * 512], start=(k == 0), stop=(k == 15)),
                          r=[r_wb[wbi], r_h], w=[r_psg[bank]])
            return s_

        def psv(s_):
            return psg[:, s_ * 2:s_ * 2 + 2, :].rearrange("p b n -> p (b n)")

        def norm_stats_b(b):
            for blk in range(2):
                P.add("pe", lambda e, blk=blk: e.matmul(pss[:, blk, :], lhsT=C.onesb[:], rhs=sq[b][:, blk * 512:(blk + 1) * 512], start=True, stop=True),
                      r=[r_sq[b]], w=[r_pss[blk]])
            P.add("act", lambda e: e.activation(out=rstd[b][:], in_=pss[:].rearrange("p b n -> p (b n)"), func=AF.Ln, scale=1.0 / 128,
                                                bias=C.epsc[:, 0:1]), r=[], w=[r_pss[0], r_pss[1], r_rstd[b]])
            P.add("act", lambda e: e.activation(out=rstd[b][:], in_=rstd[b][:], func=AF.Exp, scale=-0.5), r=[], w=[r_rstd[b]])

        pending = []

        def flush():
            for f in pending:
                f()
            del pending[:]

        dma(P, "sp", T.hT_d, hT[:], r=[r_h])
        for npi, pi in enumerate((0, 9, 1, 10, 11, 2, 12, 3, 13, 14, 4, 15, 5, 16, 17, 18, 19, 6, 7, 8)):
            wbi = npi % NWB
            dma(P, "pool", wb[wbi][:], chunked(w_in[:, pi * 256:(pi + 1) * 256]), w=[r_wb[wbi]])
            if 6 <= pi < 9:
                flush()
                g = pi - 6
                d, L = GROUPS[g]
                nt = L // 128
                for i in range(16):
                    r_, mt = i // nt, i % nt
                    start = r_ + d * mt * 128
                    bank = i % 2
                    for k in range(16):
                        P.add("pe", lambda e, k=k, start=start, d=d, bank=bank, wbi=wbi: e.matmul(
                            pss[:, bank, 0:256], lhsT=hT[:, k, start:start + d * 127 + 1:d], rhs=wb[wbi][:, k, :],
                            start=(k == 0), stop=(k == 15)), r=[r_wb[wbi], r_h], w=[r_pss[bank]])
                    vb = cnt["v"] % 2
                    cnt["v"] += 1
                    P.add("act", lambda e, vb=vb, bank=bank: e.copy(out=vs[vb][:], in_=pss[:, bank, 0:256]), r=[], w=[r_pss[bank], r_vs[vb]])
                    dma(P, "sp", T.va_d[g, i], vs[vb][:], r=[r_vs[vb]])
                continue
            for jj in range(2):
                j = pi * 2 + jj
                for h in range(2):
                    s_ = gemm_unit(wbi, jj, h)
                    flush()
                    banks = [r_psg[s_ * 2], r_psg[s_ * 2 + 1]]
                    hs = slice(h * 1024, (h + 1) * 1024)
                    if j < 12:
                        b = cnt["qk"] % 2
                        cnt["qk"] += 1
                        gcol = T.gq_col if j < 6 else T.gk_col
                        P.add("act", lambda e, s_=s_, b=b: e.activation(out=sq[b][:], in_=psv(s_), func=AF.Square), r=[], w=banks + [r_sq[b]])
                        P.add("act", lambda e, s_=s_, b=b, gcol=gcol: e.activation(out=qb[b][:], in_=psv(s_), func=AF.Copy, scale=gcol[:, 0:1]),
                              r=[], w=banks + [r_qb[b]])
                        P.add("dve", lambda e, s_=s_, b=b, gcol=gcol, hs=hs: e.scalar_tensor_tensor(out=u1[b][:], in0=psv(s_), scalar=gcol[:, 0:1], in1=cosT[:, hs],
                                                                                               op0=ALU.mult, op1=ALU.mult), r=[r_c[0]], w=banks + [r_u1[b]])

                        def stage_b(b=b, hs=hs, j=j):
                            norm_stats_b(b)
                            for blk in range(2):
                                P.add("pe", lambda e, b=b, blk=blk: e.matmul(psr[:, blk, :], lhsT=Rm[:], rhs=qb[b][:, blk * 512:(blk + 1) * 512], start=True, stop=True),
                                      r=[r_qb[b], r_c[2]], w=[r_psr[blk]])
                            P.add("dve", lambda e, b=b, hs=hs: e.tensor_tensor(out=u2[b][:], in0=psr[:].rearrange("p b n -> p (b n)"), in1=sinT[:, hs], op=ALU.mult),
                                  r=[r_c[1]], w=[r_psr[0], r_psr[1], r_u2[b]])
                            P.add("dve", lambda e, b=b: e.tensor_tensor(out=u1[b][:], in0=u1[b][:], in1=u2[b][:], op=ALU.add), r=[r_u2[b]], w=[r_u1[b]])
                            P.add("dve", lambda e, b=b: e.tensor_tensor(out=qo[b][:], in0=u1[b][:], in1=rstd[b][:], op=ALU.mult), r=[r_u1[b], r_rstd[b]], w=[r_qo[b]])
                            dma(P, "sp", T.qk_d[j][:, hs], qo[b][:], r=[r_qo[b]])
                        pending.append(stage_b)
                    elif 18 <= j < 36:
                        jh = j - 18
                        b = cnt["hy"] % 2
                        if h == 1:
                            cnt["hy"] += 1
                        P.add("act", lambda e, s_=s_, b=b, h=h: e.copy(out=ub[b][:, 1 + h * 1024:1 + (h + 1) * 1024], in_=psv(s_)), r=[], w=banks + [r_ub[b]])
                        if h == 1:
                            def stage_b(b=b, jh=jh):
                                P.add("dve", lambda e, b=b, jh=jh: e.tensor_scalar(out=uc[b][:], in0=ub[b][:, 1:S + 1], scalar1=T.cw[:, jh, 1:2], scalar2=T.cb[:, jh:jh + 1],
                                                                                  op0=ALU.mult, op1=ALU.add), r=[r_ub[b]], w=[r_uc[b]])
                                P.add("dve", lambda e, b=b, jh=jh: e.scalar_tensor_tensor(out=uc[b][:], in0=ub[b][:, 0:S], scalar=T.cw[:, jh, 0:1], in1=uc[b][:],
                                                                                         op0=ALU.mult, op1=ALU.add), r=[r_ub[b]], w=[r_uc[b]])
                                P.add("dve", lambda e, b=b, jh=jh: e.scalar_tensor_tensor(out=uc[b][:], in0=ub[b][:, 2:S + 2], scalar=T.cw[:, jh, 2:3], in1=uc[b][:],
                                                                                         op0=ALU.mult, op1=ALU.add), r=[r_ub[b]], w=[r_uc[b]])
                                dma(P, "sp", T.hyx_d[jh * 128:(jh + 1) * 128, :], uc[b][:], r=[r_uc[b]])
                                if jh < 6:
                                    for t4 in range(4):
                                        bank = cnt["tr"] % 2
                                        cnt["tr"] += 1
                                        for tt in range(4):
                                            tc_ = t4 * 4 + tt
                                            P.add("pe", lambda e, b=b, tc_=tc_, bank=bank, tt=tt: e.transpose(out=psr[:, bank, tt * 128:(tt + 1) * 128], in_=uc[b][:, tc_ * 128:(tc_ + 1) * 128],
                                                                                                             identity=C.identf[:]), r=[r_uc[b]], w=[r_psr[bank]])
                                        P.add("act", lambda e, b=b, bank=bank, t4=t4: e.copy(out=ut[b][:, t4 * 4:(t4 + 1) * 4, :], in_=psr[:, bank, :].rearrange("p (a c) -> p a c", c=128)),
                                              r=[], w=[r_psr[bank], r_ut[b]])
                                    dma(P, "sp", T.utm_d.rearrange("(a p) c -> p a c", p=128)[:, :, jh * 128:(jh + 1) * 128], ut[b][:], r=[r_ut[b]])
                            pending.append(stage_b)
                    elif 36 <= j < 40:
                        b = cnt["qk"] % 2
                        cnt["qk"] += 1
                        P.add("act", lambda e, s_=s_, b=b: e.activation(out=sq[b][:], in_=psv(s_), func=AF.Square), r=[], w=banks + [r_sq[b]])
                        P.add("act", lambda e, s_=s_, b=b: e.activation(out=u1[b][:], in_=psv(s_), func=AF.Copy, scale=T.mq_col[:, 0:1]), r=[], w=banks + [r_u1[b]])

                        def stage_b(b=b, hs=hs, j=j):
                            norm_stats_b(b)
                            P.add("dve", lambda e, b=b: e.tensor_tensor(out=qo[b][:], in0=u1[b][:], in1=rstd[b][:], op=ALU.mult), r=[r_u1[b], r_rstd[b]], w=[r_qo[b]])
                            dma(P, "sp", T.qc_d[j - 36][:, hs], qo[b][:], r=[r_qo[b]])
                        pending.append(stage_b)
        flush()
        P.end()


def phase_attn_a(C, P, T):
    nc = C.nc
    GROUPS = ((1, 2048), (4, 512), (16, 128))
    SCALE = 1.0 / math.sqrt(128.0)
    with contextlib.ExitStack() as st:
        mask = sb(st, nc, "a_mask", [128, 256], BF16)
        qT = [sb(st, nc, "a_q%d" % i, [128, S], BF16) for i in range(2)]
        kT = [sb(st, nc, "a_k%d" % i, [128, S], BF16) for i in range(2)]
        vv = [sb(st, nc, "a_v%d" % i, [128, 16, 256], BF16) for i in range(2)]
        accn = sb(st, nc, "a_accn", [128, S], F32)
        accd = sb(st, nc, "a_accd", [128, S], F32)
        E = [sb(st, nc, "a_E%d" % i, [128, 256], BF16) for i in range(3)]
        Em = [sb(st, nc, "a_Em%d" % i, [128, 256], BF16) for i in range(3)]
        yo = sb(st, nc, "a_yo", [128, S], BF16)
        ps = pst(st, nc, "a_ps", [128, 8, 512], F32)
        r_ps = P.Rs(8)
        r_q, r_k, r_v = P.Rs(2), P.Rs(2), P.Rs(2)
        r_E, r_Em = P.Rs(3), P.Rs(3)
        r_mask, r_accn, r_accd, r_yo = P.Rs(4)
        dma(P, "sp", mask[:], T.mask, w=[r_mask])
        nld = 0
        for hh in range(2):
            P.add("pool", lambda e: e.memset(accn[:], 0.0), w=[r_accn])
            P.add("pool", lambda e: e.memset(accd[:], 0.0), w=[r_accd])
            for g in range(3):
                d, L = GROUPS[g]
                nt = L // 128
                head = g * 2 + hh
                lb = nld % 2
                nld += 1
                dma(P, "sp", qT[lb][:], T.qk_d[head], w=[r_q[lb]])
                dma(P, "sp", kT[lb][:], T.qk_d[6 + head], w=[r_k[lb]])
                dma(P, "sp", vv[lb][:], T.va_d[g].rearrange("i p c -> p i c"), w=[r_v[lb]])
                its = [(r_, kt) for r_ in range(d) for kt in range(nt)]
                N = len(its)
                LAG = 2

                def geom(it):
                    r_, kt = its[it]
                    qlo = max(0, 128 * kt - 64)
                    qhi = min(L, 128 * kt + 192)
                    nq = qhi - qlo
                    mo = qlo - (128 * kt - 64)
                    ks = slice(r_ + d * 128 * kt, r_ + d * (128 * kt + 127) + 1, d)
                    qs = slice(r_ + d * qlo, r_ + d * (qhi - 1) + 1, d)
                    return r_ * nt + kt, nq, mo, ks, qs

                for it in range(N + LAG):
                    if it < N:
                        i, nq, mo, ks, qs = geom(it)
                        b3 = it % 3
                        P.add("pe", lambda e, b3=b3, nq=nq, ks=ks, qs=qs, lb=lb: e.matmul(ps[:, b3, 0:nq], lhsT=kT[lb][:, ks], rhs=qT[lb][:, qs], start=True, stop=True),
                              r=[r_q[lb], r_k[lb]], w=[r_ps[b3]])
                        P.add("act", lambda e, b3=b3, nq=nq: e.activation(out=E[b3][:, 0:nq], in_=ps[:, b3, 0:nq], func=AF.Exp, scale=SCALE), r=[], w=[r_ps[b3], r_E[b3]])
                        P.add("pool", lambda e, b3=b3, nq=nq, mo=mo: e.tensor_tensor(out=Em[b3][:, 0:nq], in0=E[b3][:, 0:nq], in1=mask[:, mo:mo + nq], op=ALU.mult),
                              r=[r_E[b3], r_mask], w=[r_Em[b3]])
                    jt = it - LAG
                    if jt >= 0:
                        i, nq, mo, ks, qs = geom(jt)
                        b3 = jt % 3
                        b2 = jt % 2
                        P.add("pe", lambda e, b3=b3, b2=b2, nq=nq, i=i, lb=lb, hh=hh: e.matmul(ps[:, 3 + b2, 0:nq], lhsT=vv[lb][:, i, hh * 128:(hh + 1) * 128], rhs=Em[b3][:, 0:nq],
                                                                                           start=True, stop=True), r=[r_v[lb], r_Em[b3]], w=[r_ps[3 + b2]])
                        P.add("pe", lambda e, b3=b3, b2=b2, nq=nq: e.matmul(ps[:, 5 + b2, 0:nq], lhsT=C.onesb[:], rhs=Em[b3][:, 0:nq], start=True, stop=True),
                              r=[r_Em[b3]], w=[r_ps[5 + b2]])
                        P.add("dve", lambda e, b2=b2, nq=nq, qs=qs: e.tensor_tensor(out=accn[:, qs], in0=accn[:, qs], in1=ps[:, 3 + b2, 0:nq], op=ALU.add), r=[], w=[r_ps[3 + b2], r_accn])
                        P.add("dve", lambda e, b2=b2, nq=nq, qs=qs: e.tensor_tensor(out=accd[:, qs], in0=accd[:, qs], in1=ps[:, 5 + b2, 0:nq], op=ALU.add), r=[], w=[r_ps[5 + b2], r_accd])
            P.add("dve", lambda e: e.reciprocal(out=accd[:], in_=accd[:]), r=[], w=[r_accd])
            P.add("dve", lambda e: e.tensor_tensor(out=yo[:], in0=accn[:], in1=accd[:], op=ALU.mult), r=[r_accn, r_accd], w=[r_yo])
            dma(P, "sp", T.ymix_d[hh], yo[:], r=[r_yo])
        P.end()


def phase_attn_c(C, P, T, kmT, vm):
    nc = C.nc
    SCALE = 1.0 / math.sqrt(128.0)
    with contextlib.ExitStack() as st:
        qT = [sb(st, nc, "c_q%d" % i, [128, S], BF16) for i in range(2)]
        E = [sb(st, nc, "c_E%d" % i, [128, 2, 512], BF16) for i in range(2)]
        rden = [sb(st, nc, "c_rd%d" % i, [128, 512], F32) for i in range(2)]
        yo = [sb(st, nc, "c_yo%d" % i, [128, S], BF16) for i in range(2)]
        ps = pst(st, nc, "c_ps", [128, 8, 512], F32)
        r_ps = P.Rs(8)
        r_q, r_E, r_rd, r_yo = P.Rs(2), P.Rs(2), P.Rs(2), P.Rs(2)
        r_km, r_vm = P.Rs(2)
        its = [(h, tb) for h in range(4) for tb in range(4)]

        def front(it):
            h, tb = its[it]
            qb = h % 2
            b = it % 2
            ts = slice(tb * 512, (tb + 1) * 512)
            if tb == 0:
                dma(P, "sp", qT[qb][:], T.qc_d[h], w=[r_q[qb]])
            for mt in range(2):
                P.add("pe", lambda e, b=b, mt=mt, h=h, qb=qb, ts=ts: e.matmul(ps[:, b * 2 + mt, :], lhsT=kmT[:, h, mt * 128:(mt + 1) * 128], rhs=qT[qb][:, ts], start=True, stop=True),
                      r=[r_km, r_q[qb]], w=[r_ps[b * 2 + mt]])
                P.add("act", lambda e, b=b, mt=mt: e.activation(out=E[b][:, mt, :], in_=ps[:, b * 2 + mt, :], func=AF.Exp, scale=SCALE), r=[], w=[r_ps[b * 2 + mt], r_E[b]])

        def back(it):
            h, tb = its[it]
            qb = h % 2
            b = it % 2
            ts = slice(tb * 512, (tb + 1) * 512)
            for mt in range(2):
                P.add("pe", lambda e, b=b, mt=mt, h=h: e.matmul(ps[:, 4 + b, :], lhsT=vm[:, mt, h * 128:(h + 1) * 128], rhs=E[b][:, mt, :], start=(mt == 0), stop=(mt == 1)),
                      r=[r_vm, r_E[b]], w=[r_ps[4 + b]])
            for mt in range(2):
                P.add("pe", lambda e, b=b, mt=mt: e.matmul(ps[:, 6 + b, :], lhsT=C.onesb[:], rhs=E[b][:, mt, :], start=(mt == 0), stop=(mt == 1)), r=[r_E[b]], w=[r_ps[6 + b]])
            P.add("act", lambda e, b=b: e.activation(out=rden[b][:], in_=ps[:, 6 + b, :], func=AF.Ln), r=[], w=[r_ps[6 + b], r_rd[b]])
            P.add("act", lambda e, b=b: e.activation(out=rden[b][:], in_=rden[b][:], func=AF.Exp, scale=-1.0), r=[], w=[r_rd[b]])
            P.add("dve", lambda e, b=b, qb=qb, ts=ts: e.tensor_tensor(out=yo[qb][:, ts], in0=ps[:, 4 + b, :], in1=rden[b][:], op=ALU.mult), r=[r_rd[b]], w=[r_ps[4 + b], r_yo[qb]])
            if tb == 3:
                dma(P, "sp", T.ymix_d[8 + h], yo[qb][:], r=[r_yo[qb]])

        front(0)
        for it in range(len(its)):
            if it + 1 < len(its):
                front(it + 1)
            back(it)
        P.end()


def phase_hy_filter_mlp(C, P, T, hh3, hh3b):
    nc = C.nc
    MAGIC = 12582912.0
    with contextlib.ExitStack() as st:
        zT = sb(st, nc, "h_zT", [64, S], F32)
        w1 = sb(st, nc, "h_w1", [64, 64], F32)
        w2 = sb(st, nc, "h_w2", [64, 64], F32)
        w3 = sb(st, nc, "h_w3", [64, 64], F32)
        hb = [sb(st, nc, "h_hb%d" % i, [64, S], F32) for i in range(2)]
        v = sb(st, nc, "h_v", [64, S], F32)
        t = sb(st, nc, "h_t", [64, S], F32)
        ps = pst(st, nc, "h_ps", [128, 4, 512], F32)
        r_ps = P.Rs(4)
        r_z, r_w1, r_w2, r_w3, r_v, r_t = P.Rs(6)
        r_hb = P.Rs(2)
        r_h3 = P.R()
        dma(P, "sp", zT[0:33, :], T.zT, w=[r_z])
        dma(P, "sp", w1[0:33, :], T.f_w1, w=[r_w1])
        dma(P, "sp", w2[:], T.f_w2, w=[r_w2])
        dma(P, "sp", w3[:], T.f_w3, w=[r_w3])
        srcs = [(zT, 33, w1, r_z, r_w1), (hb[0], 64, w2, r_hb[0], r_w2), (hb[1], 64, w3, r_hb[1], r_w3)]
        dsts = [(hb[0], r_hb[0]), (hb[1], r_hb[1]), (hh3, r_h3)]
        for li in range(3):
            src, kk, ww, r_s, r_w = srcs[li]
            dst, r_d = dsts[li]
            for tb in range(4):
                P.add("pe", lambda e, tb=tb, src=src, kk=kk, ww=ww: e.matmul(ps[0:64, tb, :], lhsT=ww[0:kk, :], rhs=src[0:kk, tb * 512:(tb + 1) * 512], start=True, stop=True),
                      r=[r_s, r_w], w=[r_ps[tb]])
            P.add("dve", lambda e, li=li: e.tensor_scalar(out=v[:], in0=ps[0:64, :, :].rearrange("p b n -> p (b n)"), scalar1=T.f_b[0:64, li:li + 1], scalar2=T.f_freq[0:64, 0:1],
                                                          op0=ALU.add, op1=ALU.mult), r=[], w=r_ps + [r_v])
            P.add("dve", lambda e: e.tensor_scalar(out=t[:], in0=v[:], scalar1=1.0 / (2 * math.pi), scalar2=MAGIC, op0=ALU.mult, op1=ALU.add), r=[r_v], w=[r_t])
            P.add("dve", lambda e: e.tensor_scalar(out=t[:], in0=t[:], scalar1=-MAGIC, scalar2=-2 * math.pi, op0=ALU.add, op1=ALU.mult), r=[], w=[r_t])
            P.add("dve", lambda e: e.tensor_tensor(out=v[:], in0=v[:], in1=t[:], op=ALU.add), r=[r_t], w=[r_v])
            P.add("act", lambda e, dst=dst: e.activation(out=dst[0:64, :], in_=v[:], func=AF.Sin), r=[r_v], w=[r_d])
        P.add("act", lambda e: e.copy(out=hh3b[0:64, :], in_=hh3[0:64, :]), r=[r_h3], w=[P.R()])
        P.end()


def phase_hy_filter(C, P, T, hh3, o, ksum, kdif):
    nc = C.nc
    with contextlib.ExitStack() as st:
        w4 = sb(st, nc, "g_w4", [64, 1536], BF16)
        dec = [sb(st, nc, "g_dec%d" % i, [128, HYW], F32) for i in range(2)]
        hbs = [sb(st, nc, "g_hbs%d" % i, [128, HYW], F32) for i in range(2)]
        sm = [sb(st, nc, "g_sm%d" % i, [128, HYW], F32) for i in range(2)]
        df = [sb(st, nc, "g_df%d" % i, [128, HYW], F32) for i in range(2)]
        ps = pst(st, nc, "g_ps", [128, 8, 512], F32)
        r_ps = P.Rs(8)
        r_w4, r_h3, r_ks, r_kd = P.Rs(4)
        r_dec, r_hbs, r_sm, r_df = P.Rs(2), P.Rs(2), P.Rs(2), P.Rs(2)
        dma(P, "pool", w4[:], T.f_w4[:, o * 1536:(o + 1) * 1536], w=[r_w4])
        for tt in range(16):
            b = tt % 2
            dma(P, "sp", dec[b][:], T.decay[tt * 128:(tt + 1) * 128, :], w=[r_dec[b]])
            for di in range(2):
                for (c0, cn, bk) in ((0, 512, 0), (512, 256, 1)):
                    bank = b * 4 + di * 2 + bk
                    P.add("pe", lambda e, tt=tt, di=di, c0=c0, cn=cn, bank=bank: e.matmul(ps[:, bank, 0:cn], lhsT=hh3[0:64, tt * 128:(tt + 1) * 128],
                                                                                        rhs=w4[:, di * HYW + c0:di * HYW + c0 + cn], start=True, stop=True),
                          r=[r_h3, r_w4], w=[r_ps[bank]])
            pf = ps[:, b * 4:b * 4 + 2, :].rearrange("p b n -> p (b n)")[:, 0:HYW]
            pb_ = ps[:, b * 4 + 2:b * 4 + 4, :].rearrange("p b n -> p (b n)")[:, 0:HYW]
            bf_ = [r_ps[b * 4], r_ps[b * 4 + 1]]
            bb_ = [r_ps[b * 4 + 2], r_ps[b * 4 + 3]]
            P.add("act", lambda e, b=b, pb_=pb_: e.copy(out=hbs[b][:], in_=pb_), r=[], w=bb_ + [r_hbs[b]])
            P.add("dve", lambda e, b=b, pf=pf: e.tensor_tensor(out=sm[b][:], in0=pf, in1=hbs[b][:], op=ALU.add), r=[r_hbs[b]], w=bf_ + [r_sm[b]])
            P.add("dve", lambda e, b=b, pf=pf: e.tensor_tensor(out=df[b][:], in0=hbs[b][:], in1=pf, op=ALU.subtract), r=[r_hbs[b]], w=bf_ + [r_df[b]])
            P.add("dve", lambda e, b=b, tt=tt: e.tensor_tensor(out=ksum[:, tt, :], in0=sm[b][:], in1=dec[b][:], op=ALU.mult), r=[r_sm[b], r_dec[b]], w=[r_ks])
            P.add("dve", lambda e, b=b, tt=tt: e.tensor_tensor(out=kdif[:, tt, :], in0=df[b][:], in1=dec[b][:], op=ALU.mult), r=[r_df[b], r_dec[b]], w=[r_kd])
        P.end()


def phase_hy_fwd(C, P, T, utm_src, ksum, kdif, Yre, Yim):
    nc = C.nc
    with contextlib.ExitStack() as st:
        utm = sb(st, nc, "d_utm", [128, 16, HYW], BF16)
        cs = [sb(st, nc, "d_cs%d" % i, [128, 2, 16, 128], BF16) for i in range(3)]
        kr = [sb(st, nc, "d_kr%d" % i, [128, HYW], F32) for i in range(2)]
        ki = [sb(st, nc, "d_ki%d" % i, [128, HYW], F32) for i in range(2)]
        t1 = [sb(st, nc, "d_t%d" % i, [128, HYW], F32) for i in range(4)]
        ps = pst(st, nc, "d_ps", [128, 8, 512], F32)
        r_ps = P.Rs(8)
        r_utm, r_ks, r_kd, r_yr, r_yi = P.Rs(5)
        r_cs = P.Rs(3)
        r_kr, r_ki = P.Rs(2), P.Rs(2)
        r_t = P.Rs(4)
        dma(P, "sp", utm[:], utm_src.rearrange("(a p) c -> p a c", p=128), w=[r_utm])

        def pv(b0):
            return ps[:, b0:b0 + 2, :].rearrange("p b n -> p (b n)")[:, 0:HYW]

        for fc in range(16):
            cb = fc % 3
            dma(P, "sp", cs[cb][:, 0], T.cts[fc].rearrange("p (a f) -> p a f", f=128), w=[r_cs[cb]])
            dma(P, "sp", cs[cb][:, 1], T.sts[fc].rearrange("p (a f) -> p a f", f=128), w=[r_cs[cb]])
            jobs = ((0, 0, ksum, r_ks), (2, 1, kdif, r_kd), (4, 0, utm, r_utm), (6, 1, utm, r_utm))
            for (b0, ci, src, r_s) in jobs:
                for (c0, cn, bk) in ((0, 512, 0), (512, 256, 1)):
                    for tc_ in range(16):
                        P.add("pe", lambda e, b0=b0, bk=bk, cn=cn, c0=c0, ci=ci, tc_=tc_, src=src, cb=cb: e.matmul(
                            ps[:, b0 + bk, 0:cn], lhsT=cs[cb][:, ci, tc_, :], rhs=src[:, tc_, c0:c0 + cn], start=(tc_ == 0), stop=(tc_ == 15)),
                            r=[r_cs[cb], r_s], w=[r_ps[b0 + bk]])
            kb = fc % 2
            P.add("act", lambda e, kb=kb: e.copy(out=kr[kb][:], in_=pv(0)), r=[], w=[r_ps[0], r_ps[1], r_kr[kb]])
            P.add("act", lambda e, kb=kb: e.copy(out=ki[kb][:], in_=pv(2)), r=[], w=[r_ps[2], r_ps[3], r_ki[kb]])
            bu = [r_ps[4], r_ps[5]]
            bs_ = [r_ps[6], r_ps[7]]
            P.add("dve", lambda e, kb=kb: e.tensor_tensor(out=t1[0][:], in0=pv(4), in1=kr[kb][:], op=ALU.mult), r=[r_kr[kb]], w=bu + [r_t[0]])
            P.add("dve", lambda e, kb=kb: e.tensor_tensor(out=t1[1][:], in0=pv(6), in1=ki[kb][:], op=ALU.mult), r=[r_ki[kb]], w=bs_ + [r_t[1]])
            P.add("dve", lambda e, kb=kb: e.tensor_tensor(out=t1[2][:], in0=pv(6), in1=kr[kb][:], op=ALU.mult), r=[r_kr[kb]], w=bs_ + [r_t[2]])
            P.add("dve", lambda e, kb=kb: e.tensor_tensor(out=t1[3][:], in0=pv(4), in1=ki[kb][:], op=ALU.mult), r=[r_ki[kb]], w=bu + [r_t[3]])
            P.add("dve", lambda e, fc=fc: e.tensor_tensor(out=Yre[:, fc, :], in0=t1[0][:], in1=t1[1][:], op=ALU.add), r=[r_t[0], r_t[1]], w=[r_yr])
            P.add("dve", lambda e, fc=fc: e.tensor_tensor(out=Yim[:, fc, :], in0=t1[2][:], in1=t1[3][:], op=ALU.subtract), r=[r_t[2], r_t[3]], w=[r_yi])
        P.end()


def phase_hy_inv(C, P, T, o, Yre, Yim):
    nc = C.nc
    with contextlib.ExitStack() as st:
        cs = [sb(st, nc, "i_cs%d" % i, [128, 2, 16, 512], BF16) for i in range(2)]
        zp = [sb(st, nc, "i_zp%d" % i, [128, 512], F32) for i in range(2)]
        xo = [sb(st, nc, "i_xo%d" % i, [128, 512], F32) for i in range(2)]
        zz = [sb(st, nc, "i_zz%d" % i, [128, 512], F32) for i in range(2)]
        zb = [sb(st, nc, "i_zb%d" % i, [128, 512], BF16) for i in range(2)]
        ut = [sb(st, nc, "i_ut%d" % i, [128, 4, 128], BF16) for i in range(2)]
        ps = pst(st, nc, "i_ps", [128, 4, 512], F32)
        r_ps = P.Rs(4)
        r_cs, r_zp, r_xo, r_zz, r_zb, r_ut = (P.Rs(2) for _ in range(6))
        r_yr, r_yi = P.Rs(2)
        zprev = T.hyx_d if o == 0 else T.z1_d
        units = [(tb, cc) for tb in range(4) for cc in range(6)]

        def loads(n):
            tb, cc = units[n]
            b = n % 2
            ts = slice(tb * 512, (tb + 1) * 512)
            dma(P, "sp", zp[b][:], zprev[cc * 128:(cc + 1) * 128, ts], w=[r_zp[b]])
            dma(P, "sp", xo[b][:], T.hyx_d[(o + 1) * HYW + cc * 128:(o + 1) * HYW + (cc + 1) * 128, ts], w=[r_xo[b]])

        pending = []
        for tb in range(2):
            cb = tb % 2
            dma(P, "sp", cs[cb][:, 0], T.cinv[tb].rearrange("p (a t) -> p a t", t=512), w=[r_cs[cb]])
            dma(P, "sp", cs[cb][:, 1], T.sinv[tb].rearrange("p (a t) -> p a t", t=512), w=[r_cs[cb]])
        loads(0)
        for n, (tb, cc) in enumerate(units):
            b = n % 2
            cb = tb % 2
            ts = slice(tb * 512, (tb + 1) * 512)
            if cc == 0 and 1 <= tb <= 2:
                dma(P, "sp", cs[1 - cb][:, 0], T.cinv[tb + 1].rearrange("p (a t) -> p a t", t=512), w=[r_cs[1 - cb]])
                dma(P, "sp", cs[1 - cb][:, 1], T.sinv[tb + 1].rearrange("p (a t) -> p a t", t=512), w=[r_cs[1 - cb]])
            for fc in range(16):
                P.add("pe", lambda e, b=b, fc=fc, cc=cc, cb=cb: e.matmul(ps[:, b, :], lhsT=Yre[:, fc, cc * 128:(cc + 1) * 128], rhs=cs[cb][:, 0, fc, :], start=(fc == 0), stop=False),
                      r=[r_yr, r_cs[cb]], w=[r_ps[b]])
                P.add("pe", lambda e, b=b, fc=fc, cc=cc, cb=cb: e.matmul(ps[:, b, :], lhsT=Yim[:, fc, cc * 128:(cc + 1) * 128], rhs=cs[cb][:, 1, fc, :], start=False, stop=(fc == 15)),
                      r=[r_yi, r_cs[cb]], w=[r_ps[b]])
            for f in pending:
                f()
            del pending[:]
            P.add("dve", lambda e, b=b, cc=cc: e.tensor_scalar(out=zp[b][:], in0=zp[b][:], scalar1=T.hbias[:, o * 6 + cc:o * 6 + cc + 1], scalar2=None, op0=ALU.mult), r=[], w=[r_zp[b]])
            P.add("dve", lambda e, b=b: e.scalar_tensor_tensor(out=zz[b][:], in0=ps[:, b, :], scalar=2.0 / NFFT, in1=zp[b][:], op0=ALU.mult, op1=ALU.add),
                  r=[r_zp[b]], w=[r_ps[b], r_zz[b]])
            if o == 0:
                P.add("dve", lambda e, b=b: e.tensor_tensor(out=zz[b][:], in0=zz[b][:], in1=xo[b][:], op=ALU.mult), r=[r_xo[b]], w=[r_zz[b]])
                if n + 1 < len(units):
                    loads(n + 1)
                dma(P, "pool", T.z1_d[cc * 128:(cc + 1) * 128, ts], zz[b][:], r=[r_zz[b]])

                def stage_b(b=b, tb=tb, cc=cc):
                    for tt in range(4):
                        P.add("pe", lambda e, b=b, tt=tt: e.transpose(out=ps[:, 2 + b, tt * 128:(tt + 1) * 128], in_=zz[b][:, tt * 128:(tt + 1) * 128], identity=C.identf[:]),
                              r=[r_zz[b]], w=[r_ps[2 + b]])
                    P.add("act", lambda e, b=b: e.copy(out=ut[b][:], in_=ps[:, 2 + b, :].rearrange("p (a c) -> p a c", c=128)), r=[], w=[r_ps[2 + b], r_ut[b]])
                    dma(P, "pool", T.utm2_d.rearrange("(a p) c -> p a c", p=128)[:, tb * 4:(tb + 1) * 4, cc * 128:(cc + 1) * 128], ut[b][:], r=[r_ut[b]])
                pending.append(stage_b)
            else:
                P.add("dve", lambda e, b=b: e.tensor_tensor(out=zb[b][:], in0=zz[b][:], in1=xo[b][:], op=ALU.mult), r=[r_xo[b], r_zz[b]], w=[r_zb[b]])
                if n + 1 < len(units):
                    loads(n + 1)
                dma(P, "pool", T.ymix_d[2 + cc][:, ts], zb[b][:], r=[r_zb[b]])
        for f in pending:
            f()
        P.end()


def phase_merge(C, P, T, w_in, w_brs, mg):
    nc = C.nc
    with contextlib.ExitStack() as st:
        hTh = sb(st, nc, "o_hT", [128, 16, 1024], BF16)
        ym = sb(st, nc, "o_ym", [128, 12, 1024], BF16)
        wg = [sb(st, nc, "o_wg%d" % i, [128, 16, 3, 128], BF16) for i in range(2)]
        wbr = [sb(st, nc, "o_wbr%d" % i, [128, 12, 128], BF16) for i in range(2)]
        sg = [sb(st, nc, "o_sg%d" % i, [128, 1024], F32) for i in range(3)]
        tq = [sb(st, nc, "o_tq%d" % i, [128, 1024], F32) for i in range(2)]
        mm = [sb(st, nc, "o_mm%d" % i, [128, 1024], F32) for i in range(2)]
        ps = pst(st, nc, "o_ps", [128, 8, 512], F32)
        r_ps = P.Rs(8)
        r_h = P.Rs(16)
        r_ym = P.Rs(12)
        r_wg, r_wbr, r_tq, r_mm = P.Rs(2), P.Rs(2), P.Rs(2), P.Rs(2)
        r_sg = P.Rs(3)
        r_mg = P.R()
        BR = ((0, 2, 0), (2, 6, 1), (8, 4, 2))
        GATE0 = 5120
        n = 0
        nu = 0
        nt = 0
        for h in range(2):
            hs = slice(h * 1024, (h + 1) * 1024)
            for k in range(16):
                dma(P, "sp", hTh[:, k, :], T.hT_d[:, k, hs], w=[r_h[k]])
            for i in range(12):
                dma(P, "sp", ym[:, i, :], T.ymix_d[i][:, hs], w=[r_ym[i]])
            for c in range(16):
                wb_ = c % 2
                for (k0, nk, bi) in BR:
                    dma(P, "pool", wg[wb_][:, :, bi, :], chunked(w_in[:, GATE0 + bi * D + c * 128:GATE0 + bi * D + (c + 1) * 128]), w=[r_wg[wb_]])
                for (k0, nk, bi) in BR:
                    dma(P, "pool", wbr[wb_][:, k0:k0 + nk, :], chunked(w_brs[bi][:, c * 128:(c + 1) * 128]), w=[r_wbr[wb_]])
                mb = nu % 2
                nu += 1
                for (k0, nk, bi) in BR:
                    gset = (n % 2) * 2
                    bset = 4 + (n % 2) * 2
                    sb_ = n % 3
                    n += 1
                    for blk in range(2):
                        for k in range(16):
                            P.add("pe", lambda e, gset=gset, blk=blk, k=k, wb_=wb_, bi=bi: e.matmul(
                                ps[:, gset + blk, :], lhsT=wg[wb_][:, k, bi, :], rhs=hTh[:, k, blk * 512:(blk + 1) * 512], start=(k == 0), stop=(k == 15)),
                                r=[r_wg[wb_], r_h[k]], w=[r_ps[gset + blk]])
                    for blk in range(2):
                        for kk in range(nk):
                            P.add("pe", lambda e, bset=bset, blk=blk, k=k0 + kk, wb_=wb_, kk=kk, nk=nk: e.matmul(
                                ps[:, bset + blk, :], lhsT=wbr[wb_][:, k, :], rhs=ym[:, k, blk * 512:(blk + 1) * 512], start=(kk == 0), stop=(kk == nk - 1)),
                                r=[r_wbr[wb_], r_ym[k0 + kk]], w=[r_ps[bset + blk]])
                    gv = ps[:, gset:gset + 2, :].rearrange("p b n -> p (b n)")
                    bv = ps[:, bset:bset + 2, :].rearrange("p b n -> p (b n)")
                    P.add("act", lambda e, gv=gv, sb_=sb_: e.activation(out=sg[sb_][:], in_=gv, func=AF.Sigmoid), r=[], w=[r_ps[gset], r_ps[gset + 1], r_sg[sb_]])
                    bbanks = [r_ps[bset], r_ps[bset + 1]]
                    if bi == 0:
                        P.add("dve", lambda e, bv=bv, sb_=sb_, mb=mb: e.tensor_tensor(out=mm[mb][:], in0=bv, in1=sg[sb_][:], op=ALU.mult), r=[r_sg[sb_]], w=bbanks + [r_mm[mb]])
                    else:
                        tb_ = nt % 2
                        nt += 1
                        P.add("dve", lambda e, bv=bv, sb_=sb_, tb_=tb_: e.tensor_tensor(out=tq[tb_][:], in0=bv, in1=sg[sb_][:], op=ALU.mult), r=[r_sg[sb_]], w=bbanks + [r_tq[tb_]])
                        if bi == 1:
                            P.add("dve", lambda e, mb=mb, tb_=tb_: e.tensor_tensor(out=mm[mb][:], in0=mm[mb][:], in1=tq[tb_][:], op=ALU.add), r=[r_tq[tb_]], w=[r_mm[mb]])
                        else:
                            P.add("dve", lambda e, mb=mb, tb_=tb_, c=c, hs=hs: e.tensor_tensor(out=mg[:, c, hs], in0=mm[mb][:], in1=tq[tb_][:], op=ALU.add),
                                  r=[r_tq[tb_], r_mm[mb]], w=[r_mg])
        P.end()


def phase_out_proj(C, P, mg, w_o, xT_dram):
    nc = C.nc
    xTc = chunked(xT_dram)
    with contextlib.ExitStack() as st:
        wo = [sb(st, nc, "p_wo%d" % i, [128, 16, 256], BF16) for i in range(2)]
        xr = [sb(st, nc, "p_xr%d" % i, [128, S], F32) for i in range(3)]
        ps = pst(st, nc, "p_ps", [128, 8, 512], F32)
        r_ps = P.Rs(8)
        r_wo = P.Rs(2)
        r_xr = P.Rs(3)
        r_mg = P.R()
        for c2 in range(8):
            wb_ = c2 % 2
            dma(P, "pool", wo[wb_][:], chunked(w_o[:, c2 * 256:(c2 + 1) * 256]), w=[r_wo[wb_]])
            for cc in range(2):
                c = c2 * 2 + cc
                bs = (c % 2) * 4
                xb = c % 3
                dma(P, "sp", xr[xb][:], xTc[:, c, :], w=[r_xr[xb]])
                for tb in range(4):
                    for k in range(16):
                        P.add("pe", lambda e, bank=bs + tb, wb_=wb_, k=k, cc=cc, tb=tb: e.matmul(
                            ps[:, bank, :], lhsT=wo[wb_][:, k, cc * 128:(cc + 1) * 128], rhs=mg[:, k, tb * 512:(tb + 1) * 512],
                            start=(k == 0), stop=(k == 15)), r=[r_wo[wb_], r_mg], w=[r_ps[bs + tb]])
                for tb in range(4):
                    P.add("dve", lambda e, bank=bs + tb, xb=xb, tb=tb: e.tensor_tensor(
                        out=xr[xb][:, tb * 512:(tb + 1) * 512], in0=ps[:, bank, :], in1=xr[xb][:, tb * 512:(tb + 1) * 512], op=ALU.add),
                        r=[], w=[r_ps[bs + tb], r_xr[xb]])
                dma(P, "sp", xTc[:, c, :], xr[xb][:], r=[r_xr[xb]])
        P.end()
NCOL = 136


def build_program(upto=99, debug=False):
    nc = bass.Bass("TRN2", target_bir_lowering=False)
    C = Ctx()
    C.nc = nc
    T = Ctx()

    def inp(name, shape, dt=F32):
        return nc.dram_tensor(name, list(shape), dt, kind="ExternalInput").ap()

    def scr(name, shape, dt=F32):
        return nc.dram_tensor(name, list(shape), dt).ap()

    x = inp("x", [S, D])
    mem = inp("mem", [NMEM, D])
    g_ff1 = inp("g_ff1", [1, D])
    g_mem = inp("g_mem", [1, D])
    w_ff1_in = inp("w_ff1_in", [D, 2 * DFF])
    w_ff1_out = inp("w_ff1_out", [DFF, D])
    w_ff2_in = inp("w_ff2_in", [D, 2 * DFF])
    w_ff2_out = inp("w_ff2_out", [DFF, D])
    w_in = inp("w_in", [D, INW])
    w_mem_kv = inp("w_mem_kv", [D, 1024])
    w_br_a = inp("w_br_a", [256, D])
    w_br_b = inp("w_br_b", [768, D])
    w_br_c = inp("w_br_c", [512, D])
    w_out = inp("w_out", [D, D])
    T.f_w1 = inp("hy_f_w1", [33, 64])
    T.f_w2 = inp("hy_f_w2", [64, 64])
    T.f_w3 = inp("hy_f_w3", [64, 64])
    T.f_w4 = inp("hy_f_w4", [64, 3072])
    cols_d = inp("c_cols", [128, NCOL])
    fcols_d = inp("c_fcols", [64, 4])
    identb_d = inp("c_identb", [128, 128], BF16)
    identf_d = inp("c_identf", [128, 128])
    T.cosT = inp("c_cosT", [128, S])
    T.sinT = inp("c_sinT", [128, S])
    T.Rm = inp("c_Rm", [128, 128], BF16)
    T.mask = inp("c_mask", [128, 256], BF16)
    T.zT = inp("c_zT", [33, S])
    T.decay = inp("c_decay", [S, HYW])
    T.cts = inp("c_cts", [16, 128, 2048], BF16)
    T.sts = inp("c_sts", [16, 128, 2048], BF16)
    T.cinv = inp("c_cinv", [4, 128, 8192], BF16)
    T.sinv = inp("c_sinv", [4, 128, 8192], BF16)
    out = nc.dram_tensor("out", [S, D], F32, kind="ExternalOutput").ap()
    xT = scr("xT_scr", [D, S])
    T.qk_d = scr("qk_scr", [12, 128, S], BF16)
    T.va_d = scr("va_scr", [3, 16, 128, 256], BF16)
    T.hyx_d = scr("hyx_scr", [3 * HYW, S])
    T.utm_d = scr("utm_scr", [S, HYW], BF16)
    T.utm2_d = scr("utm2_scr", [S, HYW], BF16)
    T.qc_d = scr("qc_scr", [4, 128, S], BF16)
    T.hT_d = scr("hT_scr", [128, 16, S], BF16)
    T.z1_d = scr("z1_scr", [HYW, S])
    T.ymix_d = scr("ymix_scr", [12, 128, S], BF16)
    dbg = {}
    if debug:
        dbg["xT"] = nc.dram_tensor("dbg_xT", [D, S], F32, kind="ExternalOutput").ap()
        dbg["ymix"] = nc.dram_tensor("dbg_ymix", [12, 128, S], BF16, kind="ExternalOutput").ap()
        dbg["qk"] = nc.dram_tensor("dbg_qk", [12, 128, S], BF16, kind="ExternalOutput").ap()
        dbg["hyx"] = nc.dram_tensor("dbg_hyx", [3 * HYW, S], F32, kind="ExternalOutput").ap()
        dbg["z1"] = nc.dram_tensor("dbg_z1", [HYW, S], F32, kind="ExternalOutput").ap()

    with contextlib.ExitStack() as st:
        P = Prog(nc, st)
        C.identb = sb(st, nc, "identb", [128, 128], BF16)
        C.identf = sb(st, nc, "identf", [128, 128], F32)
        C.onesb = sb(st, nc, "onesb", [128, 128], BF16)
        C.epsc = sb(st, nc, "epsc", [128, 1], F32)
        cols = sb(st, nc, "cols", [128, NCOL], F32)
        fcols = sb(st, nc, "fcols", [64, 4], F32)
        kmT = sb(st, nc, "kmT", [128, 4, NMEM], BF16)
        vm = sb(st, nc, "vm", [128, 2, 512], BF16)
        r0 = P.Rs(6)
        dma(P, "sp", C.identb[:], identb_d, w=[r0[0]])
        dma(P, "sp", C.identf[:], identf_d, w=[r0[1]])
        dma(P, "sp", cols[:], cols_d, w=[r0[2]])
        dma(P, "sp", fcols[:], fcols_d, w=[r0[3]])
        P.add("dve", lambda e: e.memset(C.onesb[:], 1.0), w=[r0[4]])
        P.add("dve", lambda e: e.memset(C.epsc[:], EPS), w=[r0[5]])
        P.end()
        T.gq_col = cols[:, 0:1]
        T.gk_col = cols[:, 1:2]
        T.mq_col = cols[:, 2:3]
        mk_col = cols[:, 3:4]
        gmixT = cols[:, 4:20]
        gff2T = cols[:, 20:36]
        gpostT = cols[:, 36:52]
        T.hbias = cols[:, 52:64]
        T.cw = cols[:, 64:118].rearrange("p (j t) -> p j t", t=3)
        T.cb = cols[:, 118:136]
        T.f_b = fcols[:, 0:3]
        T.f_freq = fcols[:, 3:4]

        def dump():
            if debug:
                r = P.Rs(5)
                dma(P, "sp", dbg["xT"], xT, w=[r[0]])
                dma(P, "sp", dbg["ymix"], T.ymix_d, w=[r[1]])
                dma(P, "sp", dbg["qk"], T.qk_d, w=[r[2]])
                dma(P, "sp", dbg["hyx"], T.hyx_d, w=[r[3]])
                dma(P, "sp", dbg["z1"], T.z1_d, w=[r[4]])
                P.end()

        def run():
            with contextlib.ExitStack() as st2:
                xnT = sb(st2, nc, "xnT", [128, 16, S], BF16)
                phase_tm_norm(C, P, x, 16, g_ff1[0:1, :].broadcast_to([128, D]), xnT, P.R(), xT_dram=xT)
                if upto < 1:
                    return
                phase_ffn(C, P, xnT, w_ff1_in, w_ff1_out, xT)
            if upto < 2:
                return
            with contextlib.ExitStack() as st2:
                memnT = sb(st2, nc, "memnT", [128, 16, NMEM], BF16)
                phase_tm_norm(C, P, mem, 2, g_mem[0:1, :].broadcast_to([128, D]), memnT, P.R())
                phase_memkv(C, P, memnT, w_mem_kv, mk_col, kmT, vm)
            with contextlib.ExitStack() as st2:
                hT = sb(st2, nc, "hT", [128, 16, S], BF16)
                phase_fm_norm(C, P, xT, gmixT, hT)
                phase_win(C, P, hT, w_in, T)
            if upto < 3:
                return
            phase_attn_a(C, P, T)
            phase_attn_c(C, P, T, kmT, vm)
            if upto < 4:
                return
            with contextlib.ExitStack() as st2:
                hh3 = sb(st2, nc, "hh3", [64, S], F32)
                hh3b = sb(st2, nc, "hh3b", [64, S], BF16)
                phase_hy_filter_mlp(C, P, T, hh3, hh3b)
                for o in range(2):
                    with contextlib.ExitStack() as st3:
                        Yre = sb(st3, nc, "Yre", [128, 16, HYW], BF16)
                        Yim = sb(st3, nc, "Yim", [128, 16, HYW], BF16)
                        with contextlib.ExitStack() as st4:
                            ksum = sb(st4, nc, "ksum", [128, 16, HYW], BF16)
                            kdif = sb(st4, nc, "kdif", [128, 16, HYW], BF16)
                            phase_hy_filter(C, P, T, hh3b, o, ksum, kdif)
                            phase_hy_fwd(C, P, T, T.utm_d if o == 0 else T.utm2_d, ksum, kdif, Yre, Yim)
                        phase_hy_inv(C, P, T, o, Yre, Yim)
            if upto < 5:
                return
            with contextlib.ExitStack() as st2:
                mg = sb(st2, nc, "mg", [128, 16, S], BF16)
                phase_merge(C, P, T, w_in, (w_br_a, w_br_b, w_br_c), mg)
                phase_out_proj(C, P, mg, w_out, xT)
            if upto < 6:
                return
            with contextlib.ExitStack() as st2:
                xnT = sb(st2, nc, "xnT2", [128, 16, S], BF16)
                phase_fm_norm(C, P, xT, gff2T, xnT)
                phase_ffn(C, P, xnT, w_ff2_in, w_ff2_out, xT)
            if upto < 7:
                return
            phase_fm_norm(C, P, xT, gpostT, None, final_out=out)

        run()
        dump()
    C.P = P
    return nc


_CONSTS = {}


def host_consts():
    if _CONSTS:
        return _CONSTS
    bf = ml_dtypes.bfloat16
    c = {}
    c["c_identb"] = np.eye(128, dtype=np.float32).astype(bf)
    c["c_identf"] = np.eye(128, dtype=np.float32)
    inv = np.power(np.float32(500000.0), -np.arange(0, 32, 2, dtype=np.float32) / np.float32(32)).astype(np.float32)
    ang = (np.arange(S, dtype=np.float32)[:, None] * inv[None, :]).astype(np.float32)
    cosT = np.ones((128, S), np.float32)
    sinT = np.zeros((128, S), np.float32)
    cosT[0:16] = np.cos(ang).T
    cosT[16:32] = np.cos(ang).T
    sinT[0:16] = np.sin(ang).T
    sinT[16:32] = np.sin(ang).T
    c["c_cosT"] = cosT
    c["c_sinT"] = sinT
    Rm = np.zeros((128, 128), np.float32)
    for d in range(16):
        Rm[d + 16, d] = -1.0
        Rm[d, d + 16] = 1.0
    c["c_Rm"] = Rm.astype(bf)
    a = np.arange(128)[:, None]
    b = np.arange(256)[None, :]
    c["c_mask"] = ((b >= a) & (b <= a + 128)).astype(np.float32).astype(bf)
    bands = 16
    t = np.linspace(0.0, 1.0, S, dtype=np.float32)[:, None]
    w = (2.0 * math.pi * np.arange(S, dtype=np.float32)[:, None] / S).astype(np.float32)
    f = np.linspace(1e-4, bands - 1, bands, dtype=np.float32)[None, :]
    z = np.concatenate([t, np.cos(f * w), -np.sin(f * w)], axis=-1).astype(np.float32)
    c["c_zT"] = np.ascontiguousarray(z.T)
    max_decay = math.log(1e-2) / 0.3
    min_decay = math.log(1e-2) / 1.5
    deltas = np.linspace(min_decay, max_decay, HYW, dtype=np.float32)
    c["c_decay"] = np.exp(-t * np.abs(deltas)[None, :]).astype(np.float32)
    fi = np.arange(2048, dtype=np.int64)
    ti = np.arange(2048, dtype=np.int64)
    ph = ((2 * fi[:, None] + 1) * ti[None, :]) % (2 * NFFT)
    th = ph.astype(np.float64) * (math.pi / NFFT)
    Cft = np.cos(th)
    Sft = np.sin(th)
    def fwd(M):
        A = M.T.reshape(16, 128, 16, 128)
        return np.ascontiguousarray(A.transpose(2, 1, 0, 3).reshape(16, 128, 2048).astype(np.float32).astype(bf))
    def invm(M):
        A = M.reshape(16, 128, 4, 512)
        return np.ascontiguousarray(A.transpose(2, 1, 0, 3).reshape(4, 128, 8192).astype(np.float32).astype(bf))
    c["c_cts"] = fwd(Cft)
    c["c_sts"] = fwd(Sft)
    c["c_cinv"] = invm(Cft)
    c["c_sinv"] = invm(Sft)
    _CONSTS.update(c)
    return _CONSTS


def layout_small(inputs):
    cols = np.zeros((128, NCOL), np.float32)
    cols[:, 0] = inputs["a_gq"][0]
    cols[:, 1] = inputs["a_gk"][0]
    cols[:, 2] = inputs["m_gq"][0]
    cols[:, 3] = inputs["m_gk"][0]
    cols[:, 4:20] = inputs["g_mix"][0].reshape(16, 128).T
    cols[:, 20:36] = inputs["g_ff2"][0].reshape(16, 128).T
    cols[:, 36:52] = inputs["g_post"][0].reshape(16, 128).T
    cols[:, 52:64] = inputs["hy_bias"][0].reshape(12, 128).T
    cw = inputs["hy_conv_w"][0]
    cols[:, 64:118] = cw.reshape(3, 18, 128).transpose(2, 1, 0).reshape(128, 54)
    cols[:, 118:136] = inputs["hy_conv_b"][0].reshape(18, 128).T
    fcols = np.zeros((64, 4), np.float32)
    fcols[:, 0] = inputs["hy_f_b1"][0]
    fcols[:, 1] = inputs["hy_f_b2"][0]
    fcols[:, 2] = inputs["hy_f_b3"][0]
    fcols[:, 3] = inputs["hy_f_freq"][0]
    return {"c_cols": cols, "c_fcols": fcols}


BIG = ("g_ff1", "g_mem", "w_ff1_in", "w_ff1_out", "w_ff2_in", "w_ff2_out", "w_in", "w_mem_kv", "w_br_a", "w_br_b",
       "w_br_c", "w_out", "hy_f_w1", "hy_f_w2", "hy_f_w3", "hy_f_w4")


def make_in_map(inputs, b):
    m = dict(host_consts())
    m.update(layout_small(inputs))
    m["x"] = np.ascontiguousarray(inputs["x"][b])
    m["mem"] = np.ascontiguousarray(inputs["mem"][b])
    for k in BIG:
        v = np.asarray(inputs[k])
        m[k] = np.ascontiguousarray(v[0]) if v.ndim == 3 else np.ascontiguousarray(v)
    return m


def kernel(**inputs):
    inputs = {k: np.asarray(v) for k, v in inputs.items()}
    nc = build_program()
    in_maps = [make_in_map(inputs, b) for b in range(8)]
    res = run_bass_kernel_spmd(nc, in_maps, core_ids=list(range(8)))
    return np.stack([np.asarray(r["out"], dtype=np.float32) for r in res.results], axis=0)
```
